# Optimizing a Trainium2 kernel written in Bass

```python
import math
import jax
import jax.numpy as jnp
from jax import lax
import numpy as np

D_MODEL = 2048
BATCH = 16
SEQ = 2048
DEPTH = 4

CTX_LEN = 256
GRID_W = 64
N_HEADS = 16
N_KV_HEADS = 4
HEAD_DIM = 128
ATT_WIDTH = N_HEADS * HEAD_DIM
KV_WIDTH = N_KV_HEADS * HEAD_DIM
WINDOW = 128
ATT_BLOCK = 128
ROPE_THETA = 10000.0
SSD_EXPAND = 2
D_INNER = SSD_EXPAND * D_MODEL
SSD_HEAD_DIM = 64
SSD_HEADS = D_INNER // SSD_HEAD_DIM
SSD_GROUPS = 8
HEADS_PER_GROUP = SSD_HEADS // SSD_GROUPS
D_STATE = 128
CONV_K = 5
SSD_CHUNK = 128
CONV_CH = D_INNER + 2 * SSD_GROUPS * D_STATE
D_FF = 4 * D_MODEL
N_BRANCHES = 2
N_MOD = 6
EPS = 1e-6

COL_K = 0
COL_V = COL_K + KV_WIDTH
COL_XBC = COL_V + KV_WIDTH
COL_DT = COL_XBC + CONV_CH
COL_Q = COL_DT + 2 * SSD_HEADS
CTX_SIDE_COLS = COL_Q
COL_Z = COL_Q + ATT_WIDTH
COL_GATE = COL_Z + D_INNER
IN_COLS = COL_GATE + N_BRANCHES * D_MODEL

F32 = jnp.float32

kernel_name = 'hybrid_swa_ssd_dit_block'


def rms_norm(x, g):
    xf = x.astype(F32)
    y = xf * lax.rsqrt(jnp.mean(xf * xf, axis=-1, keepdims=True) + EPS)
    return (y * g.astype(F32)).astype(x.dtype)


def modulate(h, shift, scale):
    return h * (1 + scale) + shift


def squared_relu_mlp(h, w1, w2):
    return jnp.square(jax.nn.relu(h @ w1)) @ w2


def axial_rope_tables(rows):
    t = jnp.arange(rows * GRID_W)
    row = (t // GRID_W).astype(F32)
    col = (t % GRID_W).astype(F32)
    axis_dim = HEAD_DIM // 2
    inv_freq = ROPE_THETA ** (-jnp.arange(0, axis_dim, 2, dtype=F32) / axis_dim)
    ang_r = row[:, None] * inv_freq[None]
    ang_c = col[:, None] * inv_freq[None]
    return (jnp.cos(ang_r), jnp.sin(ang_r), jnp.cos(ang_c), jnp.sin(ang_c))


def _rotate(u, cos, sin):
    u1, u2 = jnp.split(u, 2, axis=-1)
    cos = cos[None, :, None, :]
    sin = sin[None, :, None, :]
    return jnp.concatenate([u1 * cos - u2 * sin, u2 * cos + u1 * sin], axis=-1)


def apply_axial_rope(u, tables):
    cos_r, sin_r, cos_c, sin_c = tables
    uf = u.astype(F32)
    half = HEAD_DIM // 2
    out = jnp.concatenate([_rotate(uf[..., :half], cos_r, sin_r),
                           _rotate(uf[..., half:], cos_c, sin_c)], axis=-1)
    return out.astype(u.dtype)


def window_attention_latent(q, k, v, k_ctx, v_ctx, sink):
    b, s = q.shape[:2]
    l = k_ctx.shape[1]
    nb = s // ATT_BLOCK
    nbr = -(-WINDOW // ATT_BLOCK)
    span = (2 * nbr + 1) * ATT_BLOCK
    rep = N_HEADS // N_KV_HEADS
    scale = HEAD_DIM ** -0.5
    qb = q.reshape(b, nb, ATT_BLOCK, N_KV_HEADS, rep, HEAD_DIM)

    def band(t):
        tp = jnp.pad(t, ((0, 0), (nbr * ATT_BLOCK, nbr * ATT_BLOCK), (0, 0), (0, 0)))
        tb = tp.reshape(b, nb + 2 * nbr, ATT_BLOCK, N_KV_HEADS, HEAD_DIM)
        return jnp.concatenate([tb[:, o:o + nb] for o in range(2 * nbr + 1)], axis=2)

    kw, vw = band(k), band(v)
    qpos = jnp.arange(s).reshape(nb, ATT_BLOCK)
    kpos = (jnp.arange(nb)[:, None] - nbr) * ATT_BLOCK + jnp.arange(span)[None]
    in_range = ((kpos >= 0) & (kpos < s))[:, None, :]
    valid = (jnp.abs(kpos[:, None, :] - qpos[:, :, None]) <= WINDOW) & in_range

    s_win = jnp.einsum('bnqhrd,bnkhd->bnhrqk', qb, kw).astype(F32) * scale
    s_win = jnp.where(valid[None, :, None, None], s_win, -jnp.inf)
    s_ctx = jnp.einsum('bnqhrd,blhd->bnhrql', qb, k_ctx).astype(F32) * scale
    s_sink = jnp.broadcast_to(sink.astype(F32).reshape(N_KV_HEADS, rep)[None, None, :, :, None, None],
                              s_win.shape[:-1] + (1,))
    p = jax.nn.softmax(jnp.concatenate([s_win, s_ctx, s_sink], axis=-1), axis=-1)
    p_win = p[..., :span].astype(v.dtype)
    p_ctx = p[..., span:span + l].astype(v.dtype)
    o = (jnp.einsum('bnhrqk,bnkhd->bnqhrd', p_win, vw)
         + jnp.einsum('bnhrql,blhd->bnqhrd', p_ctx, v_ctx))
    return o.reshape(b, s, ATT_WIDTH)


def context_attention(q, k, v, sink):
    b, l = q.shape[:2]
    rep = N_HEADS // N_KV_HEADS
    qg = q.reshape(b, l, N_KV_HEADS, rep, HEAD_DIM)
    sc = jnp.einsum('bqhrd,bkhd->bhrqk', qg, k).astype(F32) * HEAD_DIM ** -0.5
    s_sink = jnp.broadcast_to(sink.astype(F32).reshape(N_KV_HEADS, rep)[None, :, :, None, None],
                              sc.shape[:-1] + (1,))
    p = jax.nn.softmax(jnp.concatenate([sc, s_sink], axis=-1), axis=-1)
    o = jnp.einsum('bhrqk,bkhd->bqhrd', p[..., :l].astype(v.dtype), v)
    return o.reshape(b, l, ATT_WIDTH)


def centred_depthwise_conv(u, w, bias):
    pad = CONV_K // 2
    out = lax.conv_general_dilated(u, w[:, None, :].astype(u.dtype), window_strides=(1,),
                                   padding=[(pad, pad)], dimension_numbers=('NWC', 'WIO', 'NWC'),
                                   feature_group_count=u.shape[-1])
    return out + bias.astype(u.dtype)


def ssd_chunked(xh, dt, a_neg, bm, cm, init_state, with_output):
    b, t = xh.shape[:2]
    nc = t // SSD_CHUNK
    x = xh.astype(F32).reshape(b, nc, SSD_CHUNK, SSD_GROUPS, HEADS_PER_GROUP, SSD_HEAD_DIM)
    dt = dt.reshape(b, nc, SSD_CHUNK, SSD_GROUPS, HEADS_PER_GROUP)
    bm = bm.astype(F32).reshape(b, nc, SSD_CHUNK, SSD_GROUPS, D_STATE)
    cm = cm.astype(F32).reshape(b, nc, SSD_CHUNK, SSD_GROUPS, D_STATE)
    a_cum = jnp.cumsum(dt * a_neg.reshape(SSD_GROUPS, HEADS_PER_GROUP), axis=2)
    a_last = a_cum[:, :, -1]
    xdt = x * dt[..., None]
    w_end = jnp.exp(a_last[:, :, None] - a_cum)[..., None] * xdt
    chunk_states = jnp.einsum('bcqgn,bcqgep->bcgepn', bm, w_end)

    def step(state, inp):
        decay, new = inp
        return decay[..., None, None] * state + new, state

    final, s_in = lax.scan(step, init_state,
                           (jnp.moveaxis(jnp.exp(a_last), 1, 0), jnp.moveaxis(chunk_states, 1, 0)))
    if not with_output:
        return final
    s_in = jnp.moveaxis(s_in, 0, 1)
    seg = a_cum[:, :, :, None] - a_cum[:, :, None, :]
    lower = jnp.tril(jnp.ones((SSD_CHUNK, SSD_CHUNK), dtype=bool))[None, None, :, :, None, None]
    decay = jnp.exp(jnp.where(lower, seg, -jnp.inf))
    cb = jnp.einsum('bcign,bcjgn->bcijg', cm, bm)
    y_diag = jnp.einsum('bcijge,bcjgep->bcigep', cb[..., None] * decay, xdt)
    y_off = jnp.einsum('bcign,bcgepn->bcigep', cm, s_in) * jnp.exp(a_cum)[..., None]
    y = (y_diag + y_off).reshape(b, t, SSD_HEADS, SSD_HEAD_DIM)
    return y, final


def _time_order(u, reverse):
    return jnp.flip(u, axis=1) if reverse else u


def ssd_bidirectional(xs, dt_raw, bm, cm, dt_bias, a_log, d_skip, inits, with_output):
    y_sum = None
    finals = []
    for d in range(2):
        rev = d == 1
        dt = jax.nn.softplus(dt_raw[:, :, d].astype(F32) + dt_bias[d].astype(F32))
        a_neg = -jnp.exp(a_log[d].astype(F32))
        out = ssd_chunked(_time_order(xs, rev), _time_order(dt, rev), a_neg,
                          _time_order(bm, rev), _time_order(cm, rev), inits[d], with_output)
        if with_output:
            y, fin = out
            y = _time_order(y, rev) + xs.astype(F32) * d_skip[d].astype(F32)[:, None]
            y_sum = y if y_sum is None else y_sum + y
        else:
            fin = out
        finals.append(fin)
    return y_sum, finals


def _heads(t, lo, width, n_heads):
    return t[..., lo:lo + width].reshape(t.shape[:2] + (n_heads, width // n_heads))


def _ssd_inputs(p, conv_w, conv_b):
    xbc = jax.nn.silu(centred_depthwise_conv(p[..., COL_XBC:COL_DT], conv_w, conv_b))
    lead = p.shape[:2]
    gn = SSD_GROUPS * D_STATE
    xs = xbc[..., :D_INNER].reshape(lead + (SSD_HEADS, SSD_HEAD_DIM))
    bm = xbc[..., D_INNER:D_INNER + gn].reshape(lead + (SSD_GROUPS, D_STATE))
    cm = xbc[..., D_INNER + gn:].reshape(lead + (SSD_GROUPS, D_STATE))
    dt_raw = p[..., COL_DT:COL_Q].reshape(lead + (2, SSD_HEADS))
    return xs, dt_raw, bm, cm


def hybrid_mixer(h_x, h_c, rope, w_in, sink, conv_w, conv_b, dt_bias, a_log, d_skip, g_ssd,
                 w_o_attn, w_o_ssd, w_out, ctx_out):
    b = h_x.shape[0]
    p_x = h_x @ w_in
    p_c = h_c @ (w_in if ctx_out else w_in[:, :CTX_SIDE_COLS])

    q_x = apply_axial_rope(_heads(p_x, COL_Q, ATT_WIDTH, N_HEADS), rope)
    k_x = apply_axial_rope(_heads(p_x, COL_K, KV_WIDTH, N_KV_HEADS), rope)
    v_x = _heads(p_x, COL_V, KV_WIDTH, N_KV_HEADS)
    k_c = _heads(p_c, COL_K, KV_WIDTH, N_KV_HEADS)
    v_c = _heads(p_c, COL_V, KV_WIDTH, N_KV_HEADS)
    att_x = window_attention_latent(q_x, k_x, v_x, k_c, v_c, sink)

    zero_state = jnp.zeros((b, SSD_GROUPS, HEADS_PER_GROUP, SSD_HEAD_DIM, D_STATE), F32)
    xs_c, dt_c, bm_c, cm_c = _ssd_inputs(p_c, conv_w, conv_b)
    y_c, fin_c = ssd_bidirectional(xs_c, dt_c, bm_c, cm_c, dt_bias, a_log, d_skip,
                                   (zero_state, zero_state), ctx_out)
    xs_x, dt_x, bm_x, cm_x = _ssd_inputs(p_x, conv_w, conv_b)
    y_x, _ = ssd_bidirectional(xs_x, dt_x, bm_x, cm_x, dt_bias, a_log, d_skip, fin_c, True)

    def ssd_out(y, p):
        z = p[..., COL_Z:COL_GATE].astype(F32)
        y = y.reshape(y.shape[:2] + (D_INNER,))
        return rms_norm(y * jax.nn.silu(z), g_ssd).astype(p.dtype)

    def merge(att, ssd, p):
        gates = jax.nn.sigmoid(p[..., COL_GATE:IN_COLS].astype(F32)).astype(p.dtype)
        g_att, g_ssd_branch = gates[..., :D_MODEL], gates[..., D_MODEL:]
        return (g_att * (att @ w_o_attn) + g_ssd_branch * (ssd @ w_o_ssd)) @ w_out

    out_x = merge(att_x, ssd_out(y_x, p_x), p_x)
    out_c = None
    if ctx_out:
        q_c = _heads(p_c, COL_Q, ATT_WIDTH, N_HEADS)
        att_c = context_attention(q_c, k_c, v_c, sink)
        out_c = merge(att_c, ssd_out(y_c, p_c), p_c)
    return out_x, out_c


def setup_inputs(seed: int = 0) -> dict:
    key = jax.random.key(seed)
    ks = jax.random.split(key, 24)

    def nrm(k, shape, scale):
        return jax.random.normal(k, shape, F32) * scale

    x = nrm(ks[0], (BATCH, SEQ, D_MODEL), 1.0)
    c = nrm(ks[1], (BATCH, D_MODEL), 1.0)
    ctx = nrm(ks[2], (BATCH, CTX_LEN, D_MODEL), 1.0)
    c_ctx = nrm(ks[3], (D_MODEL,), 1.0)
    w_ada = nrm(ks[4], (DEPTH, D_MODEL, N_MOD * D_MODEL), 0.5 * D_MODEL ** -0.5)
    b_ada = nrm(ks[5], (DEPTH, N_MOD * D_MODEL), 0.02)
    g_norm1 = 1.0 + nrm(ks[6], (DEPTH, D_MODEL), 0.02)
    g_norm2 = 1.0 + nrm(ks[7], (DEPTH, D_MODEL), 0.02)
    w_in = nrm(ks[8], (DEPTH, D_MODEL, IN_COLS), D_MODEL ** -0.5)
    attn_sink = nrm(ks[9], (DEPTH, N_HEADS), 0.5)
    conv_w = nrm(ks[10], (DEPTH, CONV_K, CONV_CH), CONV_K ** -0.5)
    conv_b = nrm(ks[11], (DEPTH, CONV_CH), 0.02)
    dt0 = jnp.exp(jax.random.uniform(ks[12], (DEPTH, 2, SSD_HEADS), F32,
                                     minval=math.log(1e-3), maxval=math.log(1e-1)))
    dt_bias = dt0 + jnp.log(-jnp.expm1(-dt0))
    a_log = jnp.log(jax.random.uniform(ks[13], (DEPTH, 2, SSD_HEADS), F32, minval=1.0, maxval=16.0))
    d_skip = 1.0 + nrm(ks[14], (DEPTH, 2, SSD_HEADS), 0.1)
    g_ssd = 1.0 + nrm(ks[15], (DEPTH, D_INNER), 0.02)
    w_o_attn = nrm(ks[16], (DEPTH, ATT_WIDTH, D_MODEL), ATT_WIDTH ** -0.5)
    w_o_ssd = nrm(ks[17], (DEPTH, D_INNER, D_MODEL), D_INNER ** -0.5)
    w_out = nrm(ks[18], (DEPTH, D_MODEL, D_MODEL), D_MODEL ** -0.5)
    w_ff1 = nrm(ks[19], (DEPTH, D_MODEL, D_FF), D_MODEL ** -0.5)
    w_ff2 = nrm(ks[20], (DEPTH, D_FF, D_MODEL), D_FF ** -0.5)
    g_final = 1.0 + nrm(ks[21], (D_MODEL,), 0.02)
    return {'x': x, 'c': c, 'ctx': ctx, 'c_ctx': c_ctx, 'w_ada': w_ada, 'b_ada': b_ada,
            'g_norm1': g_norm1, 'g_norm2': g_norm2, 'w_in': w_in, 'attn_sink': attn_sink,
            'conv_w': conv_w, 'conv_b': conv_b, 'dt_bias': dt_bias, 'a_log': a_log,
            'd_skip': d_skip, 'g_ssd': g_ssd, 'w_o_attn': w_o_attn, 'w_o_ssd': w_o_ssd,
            'w_out': w_out, 'w_ff1': w_ff1, 'w_ff2': w_ff2, 'g_final': g_final}


def reference(x, c, ctx, c_ctx, w_ada, b_ada, g_norm1, g_norm2, w_in, attn_sink, conv_w, conv_b,
              dt_bias, a_log, d_skip, g_ssd, w_o_attn, w_o_ssd, w_out, w_ff1, w_ff2, g_final):
    rows = x.shape[1] // GRID_W
    rope = axial_rope_tables(rows)
    silu_c = jax.nn.silu(c)[:, None, :]
    silu_cc = jax.nn.silu(c_ctx)
    h_ctx = ctx
    for i in range(DEPTH):
        ctx_out = i < DEPTH - 1
        mod_x = jnp.split(silu_c @ w_ada[i] + b_ada[i], N_MOD, axis=-1)
        n_ctx_mod = N_MOD if ctx_out else 2
        mod_c = jnp.split(silu_cc @ w_ada[i][:, :n_ctx_mod * D_MODEL] + b_ada[i][:n_ctx_mod * D_MODEL],
                          n_ctx_mod, axis=-1)
        hx = modulate(rms_norm(x, g_norm1[i]), mod_x[0], mod_x[1])
        hc = modulate(rms_norm(h_ctx, g_norm1[i]), mod_c[0], mod_c[1])
        mix_x, mix_c = hybrid_mixer(hx, hc, rope, w_in[i], attn_sink[i], conv_w[i], conv_b[i],
                                    dt_bias[i], a_log[i], d_skip[i], g_ssd[i], w_o_attn[i],
                                    w_o_ssd[i], w_out[i], ctx_out)
        x = x + mod_x[2] * mix_x
        x = x + mod_x[5] * squared_relu_mlp(modulate(rms_norm(x, g_norm2[i]), mod_x[3], mod_x[4]),
                                            w_ff1[i], w_ff2[i])
        if ctx_out:
            h_ctx = h_ctx + mod_c[2] * mix_c
            h_ctx = h_ctx + mod_c[5] * squared_relu_mlp(
                modulate(rms_norm(h_ctx, g_norm2[i]), mod_c[3], mod_c[4]), w_ff1[i], w_ff2[i])
    return rms_norm(x, g_final)
```

```python
import math
import numpy as np
import concourse.bass as bass
import concourse.mybir as mybir
from concourse.bass_utils import run_bass_kernel_spmd

F32 = mybir.dt.float32
BF16 = mybir.dt.bfloat16
AF = mybir.ActivationFunctionType
ALU = mybir.AluOpType
AX = mybir.AxisListType

SAME_ENGINE_SYNC = True
DEBUG_STOP = None
DEBUG_SUB = None
DEBUG_NORM = None
DEBUG_TILES = None
DEBUG_ROPE = None
POOL_ENG = "dve"
EPS = 1e-6


class Cfg:
    def __init__(s, D=2048, B=16, S=2048, DEPTH=4, L=256, NH=16, NKV=4, G=8, NCORES=8):
        s.D, s.B, s.S, s.DEPTH, s.L, s.NH, s.NKV, s.G, s.NCORES = D, B, S, DEPTH, L, NH, NKV, G, NCORES
        s.HD = 128
        s.AW = NH * 128
        s.KW = NKV * 128
        s.REP = NH // NKV
        s.DI = 2 * D
        s.P = 64
        s.H = s.DI // 64
        s.E = s.H // G
        s.N = 128
        s.K = 5
        s.GN = G * 128
        s.CC = s.DI + 2 * s.GN
        s.DFF = 4 * D
        s.GRID_W = 64
        s.COL_K = 0
        s.COL_V = s.KW
        s.COL_XBC = 2 * s.KW
        s.COL_DT = s.COL_XBC + s.CC
        s.COL_Q = s.COL_DT + 2 * s.H
        s.COL_Z = s.COL_Q + s.AW
        s.COL_GATE = s.COL_Z + s.DI
        s.IN = s.COL_GATE + 2 * D
        s.BPC = B // NCORES
        s.T = s.BPC * (L + S)
        s.DC = D // 128
        s.NM = s.BPC + 1
        s.CB = s.CC // 128
        assert (s.BPC * L) % 512 == 0 and S % 512 == 0
        assert s.E * 64 == 512

    def ctx_off(s, b):
        return b * s.L

    def lat_off(s, b):
        return s.BPC * s.L + b * s.S


class TT:
    def __init__(self, t, name):
        self.t = t
        self.name = name
        self.w = {}
        self.r = {}
        self.excl = False

    def __getitem__(self, k):
        return self.t[k]


ENGS = ["pe", "act", "dve", "pool", "sp"]
NDMA = 16


class Prog:
    def __init__(self, nc, stack):
        self.nc = nc
        self.stack = stack
        self.ops = {e: [] for e in ENGS}
        self.sems = []
        self.prog_sem = {}
        self.seq = {}
        for e in ["pe", "act", "dve", "pool"]:
            self.prog_sem[e] = self._new_sem("prog_" + e)
            self.seq[e] = 0
        self.waited = {e: {} for e in ENGS}
        self.dma_pool = {}
        self.dma_rr = {}
        self.sem_val = {}
        for q in ["sp", "pool", "act"]:
            self.dma_pool[q] = [self._new_sem("dma_%s_%d" % (q, i)) for i in range(NDMA)]
            self.dma_rr[q] = 0
            for sidx in self.dma_pool[q]:
                self.sem_val[sidx] = 0
        self.tiles = []
        self.nops = 0

    def _new_sem(self, name):
        h = self.stack.enter_context(self.nc.semaphore(name))
        self.sems.append(h)
        return len(self.sems) - 1

    def track(self, t, name):
        tt = TT(t, name)
        self.tiles.append(tt)
        return tt

    def _deps(self, r, w):
        evs = {}
        for t in r:
            for k, v in t.w.items():
                if evs.get(k, 0) < v:
                    evs[k] = v
        for t in w:
            for k, v in t.w.items():
                if evs.get(k, 0) < v:
                    evs[k] = v
            for k, v in t.r.items():
                if evs.get(k, 0) < v:
                    evs[k] = v
        return evs

    def _emit_waits(self, eng, evs):
        wd = self.waited[eng]
        for k, v in evs.items():
            if wd.get(k, 0) < v:
                wd[k] = v
                self.ops[eng].append(("wait", k, v))

    def op(self, eng, fn, r=(), w=(), inc=True):
        w = list(w) + [t for t in r if t.excl and t not in w]
        r = [t for t in r if not t.excl]
        evs = self._deps(r, w)
        own = self.prog_sem[eng]
        if not SAME_ENGINE_SYNC or eng == "pe":
            evs.pop(own, None)
        self._emit_waits(eng, evs)
        if inc:
            self.seq[eng] += 1
            ev = self.seq[eng]
        else:
            ev = self.seq[eng] + 1
        self.ops[eng].append(("op", fn, own if inc else None))
        self.nops += 1
        for t in r:
            if t.r.get(own, 0) < ev:
                t.r[own] = ev
        for t in w:
            t.w = {own: ev}
            t.r = {}

    def dma(self, q, out_ap, in_ap, r=(), w=(), slow=False):
        i = self.dma_rr[q]
        self.dma_rr[q] = (i + 1) % NDMA
        sem = self.dma_pool[q][i]
        prev = self.sem_val[sem]
        evs = self._deps(r, w)
        if prev > 0 and evs.get(sem, 0) < prev:
            evs[sem] = prev
        self._emit_waits(q, evs)
        val = prev + 16
        self.sem_val[sem] = val
        self.ops[q].append(("dma", out_ap, in_ap, sem, slow))
        self.nops += 1
        for t in r:
            if t.r.get(sem, 0) < val:
                t.r[sem] = val
        for t in w:
            t.w = {sem: val}
            t.r = {}

    def barrier(self):
        evs = {}
        for e, sidx in self.prog_sem.items():
            if self.seq[e] > 0:
                evs[sidx] = self.seq[e]
        for sidx, v in self.sem_val.items():
            if v > 0:
                evs[sidx] = v
        for e in ENGS:
            ev2 = dict(evs)
            if e in self.prog_sem:
                ev2.pop(self.prog_sem[e], None)
            self._emit_waits(e, ev2)
        for t in self.tiles:
            t.w = {}
            t.r = {}

    def emit(self):
        nc = self.nc
        sems = self.sems

        def replay(eng_name):
            def body(e):
                for o in self.ops[eng_name]:
                    if o[0] == "wait":
                        e.wait_ge(sems[o[1]], o[2])
                    elif o[0] == "op":
                        ins = o[1](e)
                        if o[2] is not None:
                            ins.then_inc(sems[o[2]], 1)
                    else:
                        if o[4]:
                            e.dma_start(out=o[1], in_=o[2], allow_slow_non_contiguous=True).then_inc(sems[o[3]], 16)
                        else:
                            e.dma_start(out=o[1], in_=o[2]).then_inc(sems[o[3]], 16)
            return body

        with nc.Block() as block:
            block.tensor(replay("pe"))
            block.scalar(replay("act"))
            block.vector(replay("dve"))
            block.gpsimd(replay("pool"))
            block.sync(replay("sp"))


def bc_ap(ap, dims):
    a = ap.ap
    return bass.AP(ap.tensor, ap.offset, [list(a[0])] + [list(d) for d in dims])


class Builder:
    def __init__(self, cfg, nc, stack):
        self.c = cfg
        self.nc = nc
        self.stack = stack
        self.P = Prog(nc, stack)
        self._uid = 0
        c = cfg
        dt_in = lambda name, shape: nc.dram_tensor(name, shape, F32, kind="ExternalInput")
        self.xin = dt_in("xin", [c.D, c.T])
        self.cin = dt_in("cin", [128, c.DC * c.NM])
        self.w_ada = dt_in("w_ada", [c.DEPTH * c.D, 6 * c.D])
        self.w_in = dt_in("w_in", [c.DEPTH * c.D, c.IN])
        self.w_oa = dt_in("w_oa", [c.DEPTH * c.AW, c.D])
        self.w_os = dt_in("w_os", [c.DEPTH * c.DI, c.D])
        self.w_out = dt_in("w_out", [c.DEPTH * c.D, c.D])
        self.w_ff1 = dt_in("w_ff1", [c.DEPTH * c.D, c.DFF])
        self.w_ff2 = dt_in("w_ff2", [c.DEPTH * c.DFF, c.D])
        self.b_ada = dt_in("b_ada", [128, c.DEPTH * 6 * c.DC])
        self.gn = dt_in("gn", [128, (2 * c.DEPTH + 1) * c.DC])
        self.convp = dt_in("convp", [128, c.DEPTH * c.CB * 6])
        self.rows = dt_in("rows", [c.DEPTH, c.NH + 6 * c.H + c.DI])
        self.consts = dt_in("consts", [128, 7 * 128])
        self.rope = dt_in("rope", [128, 2 * c.S])
        self.outT = nc.dram_tensor("outT", [c.D, c.BPC * c.S], F32, kind="ExternalOutput")
        sc = lambda name, shape, dt: nc.dram_tensor(name, shape, dt)
        self.wb_in = sc("wb_in", [c.D, c.IN], BF16)
        self.wb_oa = sc("wb_oa", [c.AW, c.D], BF16)
        self.wb_os = sc("wb_os", [c.DI, c.D], BF16)
        self.wb_out = sc("wb_out", [c.D, c.D], BF16)
        self.wb_ff1 = sc("wb_ff1", [c.D, c.DFF], BF16)
        self.wb_ff2 = sc("wb_ff2", [c.DFF, c.D], BF16)
        self.xT = sc("xT", [c.D, c.T], F32)
        self.qT = sc("qT", [c.AW, c.T], BF16)
        self.kT = sc("kT", [c.KW, c.T], BF16)
        self.vtm = sc("vtm", [c.T, c.KW], BF16)
        self.xbcT = sc("xbcT", [c.CC, c.T], BF16)
        self.dtv = sc("dtv", [c.T, 2 * c.H], F32)
        self.sz = sc("sz", [c.T, c.DI], BF16)
        self.gT = sc("gT", [2 * c.D, c.T], BF16)
        self.xs = sc("xs", [c.T, c.DI], BF16)
        self.Btm = sc("Btm", [c.T, c.GN], BF16)
        self.BT = sc("BT", [c.GN, c.T], BF16)
        self.CT = sc("CT", [c.GN, c.T], BF16)
        self.yd = [sc("yf", [c.T, c.DI], F32), sc("yb", [c.T, c.DI], F32)]
        self.attT = sc("attT", [c.AW, c.T], BF16)
        self.ssdT = sc("ssdT", [c.DI, c.T], BF16)
        self.banks = [self.P.track(nc.alloc_psum_tensor("bank%d" % i, [128, 512], F32), "bank%d" % i) for i in range(8)]
        self.bank_rr = 0
        for bk_ in self.banks:
            bk_.excl = True

    def sb(self, stack, shape, dt, name):
        self._uid += 1
        t = stack.enter_context(self.nc.sbuf_tensor("%s_%d" % (name, self._uid), shape, dt))
        return self.P.track(t, name)

    def mm(self, ps_ap, lhsT, rhs, start, stop, r, w, inc=None):
        if inc is None:
            inc = stop
        self.P.op("pe", lambda e: e.matmul(ps_ap, lhsT=lhsT, rhs=rhs, start=start, stop=stop), r=r, w=w, inc=inc)

    def act(self, out, in_, func, r, w, bias=None, scale=None, accum=None):
        kw = {}
        if bias is not None:
            kw["bias"] = bias
        if scale is not None:
            kw["scale"] = scale
        if accum is not None:
            kw["accum_out"] = accum
        self.P.op("act", lambda e: e.activation(out=out, in_=in_, func=func, **kw), r=r, w=w)

    def tt(self, eng, out, in0, in1, op, r, w):
        self.P.op(eng, lambda e: e.tensor_tensor(out=out, in0=in0, in1=in1, op=op), r=r, w=w)

    def ts(self, eng, out, in0, s1, s2, op0, op1, r, w):
        if s2 is None:
            self.P.op(eng, lambda e: e.tensor_single_scalar(out=out, in_=in0, scalar=s1, op=op0), r=r, w=w)
        else:
            self.P.op(eng, lambda e: e.tensor_scalar(out=out, in0=in0, scalar1=s1, scalar2=s2, op0=op0, op1=op1), r=r, w=w)

    def stt(self, eng, out, in0, scalar, in1, op0, op1, r, w):
        self.P.op(eng, lambda e: e.scalar_tensor_tensor(out=out, in0=in0, scalar=scalar, in1=in1, op0=op0, op1=op1), r=r, w=w)

    def cp(self, eng, out, in_, r, w):
        if eng == "act":
            self.P.op("act", lambda e: e.copy(out=out, in_=in_), r=r, w=w)
        else:
            self.P.op(eng, lambda e: e.tensor_copy(out=out, in_=in_), r=r, w=w)

    def rsqrt(self, out, in_, inv_n, r, wt):
        self.act(out, in_, AF.Sqrt, bias=self.eps_c[:, 0:1], scale=inv_n, r=list(r) + [self.eps_c], w=[wt])
        self.P.op("dve", lambda e: e.reciprocal(out=out, in_=out), r=[wt], w=[wt])

    def memset(self, eng, ap, val, w):
        self.P.op(eng, lambda e: e.memset(ap, val), r=(), w=w)

    def load(self, out_ap, in_ap, w, q="sp", slow=False):
        self.P.dma(q, out_ap, in_ap, r=(), w=w, slow=slow)

    def store(self, out_ap, in_ap, r, q="pool", slow=False):
        self.P.dma(q, out_ap, in_ap, r=r, w=(), slow=slow)

    def build(self):
        from contextlib import ExitStack
        c = self.c
        P = self.P
        nc = self.nc
        with ExitStack() as gs:
            self.gs = gs
            self.setup_consts(gs)
            self.phase_mod(gs)
            step = max(128, (1 << 22) // (c.T * 4) // 128 * 128)
            for r0 in range(0, c.D, step):
                r1 = min(c.D, r0 + step)
                P.dma("sp", self.xT[r0:r1, :], self.xin[r0:r1, :])
            P.barrier()
            steps = []
            for l in range(c.DEPTH):
                last = l == c.DEPTH - 1
                steps.append(lambda l=l, last=last: (self.cast_weights(l), self.layer_consts(l)))
                steps.append(lambda l=l, last=last: self.phase1(l, last))
                steps.append(lambda l=l, last=last: self.phase_attn(l, last))
                steps.append(lambda l=l, last=last: self.phase_conv(l))
                steps.append(lambda l=l, last=last: self.phase_ssd(l, last))
                steps.append(lambda l=l, last=last: self.phase4(l, last))
                steps.append(lambda l=l, last=last: self.phase5a(l, last))
                steps.append(lambda l=l, last=last: self.phase5b(l, last))
            for i, f in enumerate(steps):
                if DEBUG_STOP is not None and i >= DEBUG_STOP:
                    break
                f()
                P.barrier()
            P.emit()

    def setup_consts(self, gs):
        c = self.c
        cf = self.sb(gs, [128, 7 * 128], F32, "constf")
        self.load(cf[:], self.consts[:, :], w=[cf])
        self.cf = cf
        cbf = self.sb(gs, [128, 7 * 128], BF16, "constb")
        self.cp("dve", cbf[:], cf[:], r=[cf], w=[cbf])
        self.cb16 = cbf
        k = lambda t, i: t[:, i * 128:(i + 1) * 128]
        self.ident_b = k(cbf, 0)
        self.U_f = [k(cf, 1), k(cf, 2)]
        self.U_b = [k(cbf, 1), k(cbf, 2)]
        self.A_f = [k(cf, 3), k(cf, 4)]
        self.ones_f = k(cf, 5)
        self.ones_b = k(cbf, 5)
        self.R_b = k(cbf, 6)
        self.mod = self.sb(gs, [128, c.DEPTH * 6 * c.DC * c.NM], F32, "mod")
        self.s1 = self.sb(gs, [128, c.DEPTH * c.DC * c.NM], F32, "s1")
        self.s2 = self.sb(gs, [128, c.DEPTH * c.DC * c.NM], F32, "s2")
        self.gnt = self.sb(gs, [128, (2 * c.DEPTH + 1) * c.DC], F32, "gnt")
        self.load(self.gnt[:], self.gn[:, :], w=[self.gnt])
        self.zero_c = self.sb(gs, [128, 1], F32, "zeroc")
        self.memset("dve", self.zero_c[:], 0.0, w=[self.zero_c])
        self.eps_c = self.sb(gs, [128, 1], F32, "epsc")
        self.memset("dve", self.eps_c[:], EPS, w=[self.eps_c])
        self.one_c = self.sb(gs, [128, 1], F32, "onec")
        self.memset("dve", self.one_c[:], 1.0, w=[self.one_c])

    def mod_ap(self, l, j, dc, m):
        c = self.c
        o = ((l * 6 + j) * c.DC + dc) * c.NM + m
        return self.mod[:, o:o + 1]

    def phase_mod(self, gs):
        from contextlib import ExitStack
        c = self.c
        P = self.P
        with ExitStack() as st:
            ct = self.sb(st, [128, c.DC * c.NM], F32, "ct")
            sct = self.sb(st, [128, c.DC * c.NM], F32, "sct")
            bt = self.sb(st, [128, c.DEPTH * 6 * c.DC], F32, "bt")
            self.load(ct[:], self.cin[:, :], w=[ct])
            self.load(bt[:], self.b_ada[:, :], w=[bt])
            self.act(sct[:], ct[:], AF.Silu, r=[ct], w=[sct])
            KC = c.DC
            ws = [self.sb(st, [128, KC * 512], F32, "wada%d" % i) for i in range(2)]
            wi = 0
            ncg = (6 * c.D) // 512
            for l in range(c.DEPTH):
                for cg in range(ncg):
                    wt = ws[wi % 2]
                    wi += 1
                    src = self.w_ada[l * c.D:(l + 1) * c.D, cg * 512:(cg + 1) * 512].rearrange("(kc p) n -> p kc n", p=128)
                    self.load(wt[:].rearrange("p (kc n) -> p kc n", n=512), src, w=[wt])
                    for cb in range(4):
                        bk = self.banks[self.bank_rr % 8]
                        self.bank_rr += 1
                        for kc in range(KC):
                            self.mm(bk[:, 0:c.NM], wt[:, kc * 512 + cb * 128: kc * 512 + (cb + 1) * 128],
                                    sct[:, kc * c.NM:(kc + 1) * c.NM], kc == 0, kc == KC - 1,
                                    r=[wt, sct], w=[bk])
                        t = cg * 4 + cb
                        o = (l * 6 * c.DC + t) * c.NM
                        self.act(self.mod[:, o:o + c.NM], bk[:, 0:c.NM], AF.Identity,
                                 bias=bt[:, l * 6 * c.DC + t: l * 6 * c.DC + t + 1], r=[bk, bt], w=[self.mod])
            for l in range(c.DEPTH):
                for (dst, j, gi) in ((self.s1, 1, l), (self.s2, 4, c.DEPTH + l)):
                    o = (l * 6 + j) * c.DC * c.NM
                    od = l * c.DC * c.NM
                    n = c.DC * c.NM
                    self.ts("dve", dst[:, od:od + n], self.mod[:, o:o + n], 1.0, None, ALU.add, None,
                            r=[self.mod], w=[dst])
                    g = self.gnt[:, gi * c.DC:(gi + 1) * c.DC]
                    gb = bc_ap(g, [[1, c.DC], [0, c.NM]])
                    d3 = dst[:, od:od + n].rearrange("p (a b) -> p a b", b=c.NM)
                    self.tt("dve", d3, d3, gb, ALU.mult, r=[dst, self.gnt], w=[dst])
            P.barrier()

    def cast_weights(self, l):
        c = self.c
        for (src, dst, R, C) in ((self.w_in, self.wb_in, c.D, c.IN), (self.w_oa, self.wb_oa, c.AW, c.D),
                                 (self.w_os, self.wb_os, c.DI, c.D), (self.w_out, self.wb_out, c.D, c.D),
                                 (self.w_ff1, self.wb_ff1, c.D, c.DFF), (self.w_ff2, self.wb_ff2, c.DFF, c.D)):
            step = max(128, ((1 << 21) // C) // 128 * 128)
            for r0 in range(0, R, step):
                r1 = min(R, r0 + step)
                self.P.dma("pool", dst[r0:r1, :], src[l * R + r0: l * R + r1, :])

    def layer_consts(self, l):
        c = self.c
        if l == 0:
            gs = self.gs
            self.sinkexp = self.sb(gs, [128, c.NH], F32, "sinkexp")
            self.dtb = self.sb(gs, [128, 2 * c.H], F32, "dtb")
            self.aneg = self.sb(gs, [128, 2 * c.H], F32, "aneg")
            self.dsk = self.sb(gs, [128, 2 * c.H], F32, "dsk")
            self.dsum = self.sb(gs, [128, c.H], F32, "dsum")
            self.cvp = self.sb(gs, [128, c.CB * 6], F32, "cvp")
        rows = self.rows
        H2 = 2 * c.H

        def brow(o, n):
            return bass.AP(rows, l * (c.NH + 6 * c.H + c.DI) + o, [[0, 128], [1, n]])
        self.load(self.sinkexp[:], brow(0, c.NH), w=[self.sinkexp])
        self.load(self.dtb[:], brow(c.NH, H2), w=[self.dtb])
        self.load(self.aneg[:], brow(c.NH + H2, H2), w=[self.aneg])
        self.load(self.dsk[:], brow(c.NH + 2 * H2, H2), w=[self.dsk])
        self.load(self.cvp[:], self.convp[:, l * c.CB * 6:(l + 1) * c.CB * 6], w=[self.cvp])
        self.act(self.sinkexp[:], self.sinkexp[:], AF.Exp, r=[self.sinkexp], w=[self.sinkexp])
        self.act(self.aneg[:], self.aneg[:], AF.Exp, r=[self.aneg], w=[self.aneg])
        self.ts("dve", self.aneg[:], self.aneg[:], -1.0, None, ALU.mult, None, r=[self.aneg], w=[self.aneg])
        self.tt("dve", self.dsum[:], self.dsk[:, 0:c.H], self.dsk[:, c.H:H2], ALU.add, r=[self.dsk], w=[self.dsum])
        self.gssd_off = l * (c.NH + 6 * c.H + c.DI) + c.NH + 3 * H2

    def tiles512(self):
        c = self.c
        out = []
        for t0 in range(0, c.BPC * c.L, 512):
            out.append((t0, "ctx", None, None))
        for b in range(c.BPC):
            for s0 in range(0, c.S, 512):
                out.append((c.lat_off(b) + s0, "lat", b, s0))
        return out

    def norm_tile(self, st, xt, l_idx, s_t, sh_fn, m, out_t, out_is_f32=False):
        c = self.c
        DC = c.DC
        N = 512
        bk = self.banks[6]
        if DEBUG_NORM == 0:
            return
        for dc in range(DC):
            sq = self.sqs[dc % 2]
            self.act(sq[:], xt[:, dc * N:(dc + 1) * N], AF.Square, r=[xt], w=[sq])
            if DEBUG_NORM == -1:
                continue
            self.mm(bk[:, 0:N], self.ones_f, sq[:], dc == 0, dc == DC - 1, r=[sq, self.cf], w=[bk], inc=True)
        rstd = self.rstd
        if DEBUG_NORM == 1 or DEBUG_NORM == -1:
            return
        if DEBUG_NORM == 2:
            self.act(rstd[:], bk[:, 0:N], AF.Sqrt, bias=self.eps_c[:, 0:1], scale=1.0 / c.D, r=[bk, self.eps_c], w=[rstd])
            return
        self.rsqrt(rstd[:], bk[:, 0:N], 1.0 / c.D, [bk], rstd)
        if DEBUG_NORM == 3:
            return
        for dc in range(DC):
            tmp = self.ntmp[dc % 2]
            o = (l_idx * DC + dc) * c.NM + m
            self.stt("dve", tmp[:], xt[:, dc * N:(dc + 1) * N], s_t[:, o:o + 1], rstd[:], ALU.mult, ALU.mult,
                     r=[xt, s_t, rstd], w=[tmp])
            if DEBUG_NORM == 4:
                continue
            sh = sh_fn(dc)
            self.act(out_t[:, dc * N:(dc + 1) * N], tmp[:], AF.Identity, bias=sh, r=[tmp, self.mod, self.zero_c], w=[out_t])

    def alloc_norm_tmps(self, st):
        self.sqs = [self.sb(st, [128, 512], F32, "sq%d" % i) for i in range(2)]
        self.ntmp = [self.sb(st, [128, 512], F32, "ntmp%d" % i) for i in range(2)]
        self.rstd = self.sb(st, [128, 512], F32, "rstd")

    def dense(self, wb, R, c0, ncols, act_t, act_kc_ap, N, banks, epilogue, wslots):
        KT = R // 128
        KC = min(16, KT)
        nkg = KT // KC
        bi = 0
        for g0 in range(0, ncols, 512):
            gw = min(512, ncols - g0)
            ncb = gw // 128
            bks = []
            for i in range(ncb):
                bks.append(banks[self._dense_rr % len(banks)])
                self._dense_rr += 1
            for kg in range(nkg):
                wt = wslots[self._ws_rr % len(wslots)]
                self._ws_rr += 1
                src = wb[kg * KC * 128:(kg + 1) * KC * 128, c0 + g0: c0 + g0 + gw].rearrange("(kc p) n -> p kc n", p=128)
                dst = wt[:, 0:KC * gw].rearrange("p (kc n) -> p kc n", n=gw)
                self.load(dst, src, w=[wt])
                for cb in range(ncb):
                    for kc in range(KC):
                        first = kg == 0 and kc == 0
                        lastk = kg == nkg - 1 and kc == KC - 1
                        self.mm(bks[cb][:, 0:N], wt[:, kc * gw + cb * 128: kc * gw + (cb + 1) * 128],
                                act_kc_ap(kg * KC + kc), first, lastk, r=[wt, act_t], w=[bks[cb]], inc=(kc == KC - 1))
            for cb in range(ncb):
                epilogue(g0 // 128 + cb, bks[cb])

    def dense_tm(self, wb, R, c0, ncols, act_t, N, banks, epilogue, wslots):
        KT = R // 128
        assert KT <= 16 and ncols <= 512
        wt = wslots[self._ws_rr % len(wslots)]
        self._ws_rr += 1
        src = wb[0:R, c0:c0 + ncols].rearrange("(kc p) n -> p kc n", p=128)
        dst = wt[:, 0:KT * ncols].rearrange("p (kc n) -> p kc n", n=ncols)
        self.load(dst, src, w=[wt])
        for s in range(N // 128):
            bk = banks[self._dense_rr % len(banks)]
            self._dense_rr += 1
            for kc in range(KT):
                self.mm(bk[:, 0:ncols], act_t[:, kc * N + s * 128: kc * N + (s + 1) * 128],
                        wt[:, kc * ncols:(kc + 1) * ncols], kc == 0, kc == KT - 1, r=[wt, act_t], w=[bk])
            epilogue(s, bk)

    def phase1(self, l, last):
        from contextlib import ExitStack
        c = self.c
        N = 512
        self._dense_rr = 0
        self._ws_rr = 0
        with ExitStack() as st:
            self.alloc_norm_tmps(st)
            xt = self.sb(st, [128, c.DC * N], F32, "xt")
            h = self.sb(st, [128, c.DC * N], BF16, "h")
            wslots = [self.sb(st, [128, min(16, c.DC) * 512], BF16, "ws%d" % i) for i in range(3)]
            fm = [self.sb(st, [128, 4 * N], BF16, "fm%d" % i) for i in range(3)]
            tm = [self.sb(st, [128, 512], BF16, "tm%d" % i) for i in range(3)]
            cos_t = self.sb(st, [128, N], F32, "cos")
            sin_t = self.sb(st, [128, N], F32, "sin")
            t1 = [self.sb(st, [128, N], F32, "t1_%d" % i) for i in range(2)]
            t2 = [self.sb(st, [128, N], F32, "t2_%d" % i) for i in range(2)]
            qb = [self.sb(st, [128, N], BF16, "qb%d" % i) for i in range(2)]
            dtx = [self.sb(st, [128, 2 * c.H], F32, "dtx%d" % i) for i in range(2)]
            dta = [self.sb(st, [128, 2 * c.H], F32, "dta%d" % i) for i in range(2)]
            dtl = [self.sb(st, [128, 2 * c.H], F32, "dtl%d" % i) for i in range(2)]
            dto = [self.sb(st, [128, 2 * c.H], F32, "dto%d" % i) for i in range(2)]
            dbanks = self.banks[0:6]
            rr = {"fm": 0, "tm": 0, "rope": 0, "dt": 0}
            wl = self.wb_in
            for ti_, (t0, kind, b, s0) in enumerate(self.tiles512()):
                if DEBUG_TILES is not None and ti_ >= DEBUG_TILES:
                    break
                is_ctx = kind == "ctx"
                m = c.BPC if is_ctx else b
                skip_q = last and is_ctx
                self.load(xt[:].rearrange("p (dc n) -> p dc n", n=N),
                          self.xT.ap().rearrange("(dc p) t -> p dc t", p=128)[:, :, t0:t0 + N], w=[xt])
                self.norm_tile(st, xt, l, self.s1, lambda dc: self.mod_ap(l, 0, dc, m), m, h)
                if not is_ctx:
                    self.load(cos_t[:], self.rope[:, s0:s0 + N], w=[cos_t])
                    self.load(sin_t[:], self.rope[:, c.S + s0: c.S + s0 + N], w=[sin_t])
                hk = lambda kc: h[:, kc * N:(kc + 1) * N]

                def fm_family(c0, ncols, dst, kindf):
                    state = {}

                    def epi(cbi, bk):
                        j = cbi % 4
                        if j == 0:
                            state["o"] = fm[rr["fm"] % 3]
                            rr["fm"] += 1
                        o = state["o"]
                        oap = o[:, j * N:(j + 1) * N]
                        if kindf == "rope" and not is_ctx and DEBUG_ROPE != 0:
                            i = rr["rope"] % 2
                            rr["rope"] += 1
                            self.cp("act", qb[i][:], bk[:, 0:N], r=[bk], w=[qb[i]])
                            rb = self.banks[7]
                            self.mm(rb[:, 0:N], self.R_b, qb[i][:], True, True, r=[qb[i], self.cb16], w=[rb])
                            if DEBUG_ROPE == 1:
                                self.cp("act", oap, rb[:, 0:N], r=[rb], w=[o])
                                return
                            self.tt("dve", t1[i][:], bk[:, 0:N], cos_t[:], ALU.mult, r=[bk, cos_t, qb[i]], w=[t1[i]])
                            if DEBUG_ROPE == 2:
                                self.cp("act", oap, t1[i][:], r=[t1[i]], w=[o])
                                return
                            self.tt("dve", t2[i][:], rb[:, 0:N], sin_t[:], ALU.mult, r=[rb, sin_t], w=[t2[i]])
                            if DEBUG_ROPE == 3:
                                self.cp("act", oap, t2[i][:], r=[t2[i], t1[i]], w=[o])
                                return
                            self.tt(POOL_ENG, oap, t1[i][:], t2[i][:], ALU.add, r=[t1[i], t2[i]], w=[o])
                        elif kindf == "sigmoid":
                            self.act(oap, bk[:, 0:N], AF.Sigmoid, r=[bk], w=[o])
                        else:
                            self.cp("act", oap, bk[:, 0:N], r=[bk], w=[o])
                        nb = min(4, ncols // 128 - (cbi // 4) * 4)
                        if j == nb - 1:
                            r0 = (cbi // 4) * 512
                            d = dst[r0:r0 + nb * 128, t0:t0 + N].rearrange("(j p) t -> p j t", p=128)
                            self.store(d, o[:, 0:nb * N].rearrange("p (j t) -> p j t", t=N), r=[o])
                    self.dense(wl, c.D, c0, ncols, h, hk, N, dbanks, epi, wslots)

                def tm_family(c0, ncols, dst, func):
                    for g0 in range(0, ncols, 512):
                        gw = min(512, ncols - g0)

                        def epi(s, bk, g0=g0, gw=gw):
                            o = tm[rr["tm"] % 3]
                            rr["tm"] += 1
                            if func is None:
                                self.cp("act", o[:, 0:gw], bk[:, 0:gw], r=[bk], w=[o])
                            else:
                                self.act(o[:, 0:gw], bk[:, 0:gw], func, r=[bk], w=[o])
                            self.store(dst[t0 + s * 128: t0 + (s + 1) * 128, g0:g0 + gw], o[:, 0:gw], r=[o])
                        self.dense_tm(wl, c.D, c0 + g0, gw, h, N, dbanks, epi, wslots)

                if DEBUG_SUB == 0:
                    return
                fm_family(c.COL_K, c.KW, self.kT, "rope")
                if DEBUG_SUB == 1:
                    return
                tm_family(c.COL_V, c.KW, self.vtm, None)
                if DEBUG_SUB == 2:
                    return
                fm_family(c.COL_XBC, c.CC, self.xbcT, "copy")
                if DEBUG_SUB == 3:
                    return

                H2 = 2 * c.H

                def epi_dt(s, bk):
                    i = rr["dt"] % 2
                    rr["dt"] += 1
                    self.tt("dve", dtx[i][:], bk[:, 0:H2], self.dtb[:], ALU.add, r=[bk, self.dtb], w=[dtx[i]])
                    self.act(dta[i][:], dtx[i][:], AF.Abs, r=[dtx[i]], w=[dta[i]])
                    self.act(dtl[i][:], dta[i][:], AF.Exp, scale=-1.0, r=[dta[i]], w=[dtl[i]])
                    self.act(dtl[i][:], dtl[i][:], AF.Ln, bias=self.one_c[:, 0:1], r=[dtl[i], self.one_c], w=[dtl[i]])
                    self.stt("dve", dto[i][:], dtx[i][:], 0.0, dtl[i][:], ALU.max, ALU.add, r=[dtx[i], dtl[i]], w=[dto[i]])
                    self.store(self.dtv[t0 + s * 128: t0 + (s + 1) * 128, :], dto[i][:], r=[dto[i]])
                self.dense_tm(wl, c.D, c.COL_DT, H2, h, N, dbanks, epi_dt, wslots)
                if DEBUG_SUB == 4:
                    return
                if not skip_q:
                    fm_family(c.COL_Q, c.AW, self.qT, "rope")
                    if DEBUG_SUB == 5:
                        continue
                    tm_family(c.COL_Z, c.DI, self.sz, AF.Silu)
                    if DEBUG_SUB == 6:
                        continue
                    fm_family(c.COL_GATE, 2 * c.D, self.gT, "sigmoid")

    def phase_attn(self, l, last):
        from contextlib import ExitStack
        c = self.c
        S, L = c.S, c.L
        scale = 1.0 / math.sqrt(128.0)
        NBL = S // 128
        NBC = L // 128
        with ExitStack() as st:
            kc_t = [self.sb(st, [128, L], BF16, "kc%d" % i) for i in range(2)]
            kl_t = [self.sb(st, [128, S], BF16, "kl%d" % i) for i in range(2)]
            vc_t = [self.sb(st, [128, NBC * 128], BF16, "vc%d" % i) for i in range(2)]
            vl_t = [self.sb(st, [128, NBL * 128], BF16, "vl%d" % i) for i in range(2)]
            q_t = [self.sb(st, [128, S], BF16, "q%d" % i) for i in range(2)]
            qc_t = [self.sb(st, [128, L], BF16, "qc%d" % i) for i in range(2)]
            o_t = [self.sb(st, [128, S], BF16, "o%d" % i) for i in range(2)]
            oc_t = [self.sb(st, [128, L], BF16, "oc%d" % i) for i in range(2)]
            pT = [self.sb(st, [128, 512], BF16, "pT%d" % i) for i in range(4)]
            rec = [self.sb(st, [128, 512], F32, "rec%d" % i) for i in range(2)]
            sbanks = self.banks[0:4]
            obanks = [(self.banks[4], self.banks[5]), (self.banks[6], self.banks[7])]
            cnt = {"s": 0, "o": 0, "p": 0, "g": 0, "h": 0}

            def score_block(kt, kap, qt, qap, nq, mask_specs):
                sb_ = sbanks[cnt["s"] % 4]
                cnt["s"] += 1
                self.mm(sb_[:, 0:nq], kap, qap, True, True, r=[kt, qt], w=[sb_])
                p = pT[cnt["p"] % 4]
                cnt["p"] += 1
                self.act(p[:, 0:nq], sb_[:, 0:nq], AF.Exp, scale=scale, r=[sb_], w=[p])
                for (off, which) in mask_specs:
                    self.tt("dve", p[:, off:off + 128], p[:, off:off + 128], self.U_b[which], ALU.mult,
                            r=[p, self.cb16], w=[p])
                return p

            def finish(ob, db, h, ot, ocol, nq):
                r_ = rec[cnt["o"] % 2]
                self.ts("dve", r_[:, 0:nq], db[:, 0:nq], self.sinkexp[:, h:h + 1], None, ALU.add, None,
                        r=[db, self.sinkexp], w=[r_])
                self.P.op("dve", lambda e: e.reciprocal(out=r_[:, 0:nq], in_=r_[:, 0:nq]), r=[r_], w=[r_])
                self.tt("dve", ot[:, ocol:ocol + nq], ob[:, 0:nq], r_[:, 0:nq], ALU.mult, r=[ob, r_], w=[ot])

            for b in range(c.BPC):
                co = c.ctx_off(b)
                lo = c.lat_off(b)
                for g in range(c.NKV):
                    gi = cnt["g"] % 2
                    cnt["g"] += 1
                    kc, kl, vc, vl = kc_t[gi], kl_t[gi], vc_t[gi], vl_t[gi]
                    self.load(kc[:], self.kT[g * 128:(g + 1) * 128, co:co + L], w=[kc])
                    self.load(kl[:], self.kT[g * 128:(g + 1) * 128, lo:lo + S], w=[kl])
                    self.load(vc[:].rearrange("p (n d) -> p n d", d=128),
                              self.vtm[co:co + L, g * 128:(g + 1) * 128].rearrange("(n p) d -> p n d", p=128), w=[vc])
                    self.load(vl[:].rearrange("p (n d) -> p n d", d=128),
                              self.vtm[lo:lo + S, g * 128:(g + 1) * 128].rearrange("(n p) d -> p n d", p=128), w=[vl])
                    for hh in range(c.REP):
                        h = g * c.REP + hh
                        hi = cnt["h"] % 2
                        cnt["h"] += 1
                        q, qc, ot, oc = q_t[hi], qc_t[hi], o_t[hi], oc_t[hi]
                        self.load(q[:], self.qT[h * 128:(h + 1) * 128, lo:lo + S], w=[q])
                        for qg in range(S // 512):
                            ob, db = obanks[cnt["o"] % 2]
                            first = True
                            for j in range(NBC):
                                p = score_block(kc, kc[:, j * 128:(j + 1) * 128], q, q[:, qg * 512:(qg + 1) * 512], 512, [])
                                self.mm(ob[:, 0:512], vc[:, j * 128:(j + 1) * 128], p[:, 0:512], first, False, r=[vc, p], w=[ob], inc=True)
                                self.mm(db[:, 0:512], self.ones_b, p[:, 0:512], first, False, r=[self.cb16, p], w=[db], inc=True)
                                first = False
                            for j in range(qg * 4 - 1, qg * 4 + 5):
                                if j < 0 or j >= NBL:
                                    continue
                                qlo = max(j - 1, qg * 4)
                                qhi = min(j + 1, qg * 4 + 3)
                                nq = (qhi - qlo + 1) * 128
                                masks = []
                                for qb_ in range(qlo, qhi + 1):
                                    if qb_ == j - 1:
                                        masks.append(((qb_ - qlo) * 128, 0))
                                    elif qb_ == j + 1:
                                        masks.append(((qb_ - qlo) * 128, 1))
                                p = score_block(kl, kl[:, j * 128:(j + 1) * 128], q, q[:, qlo * 128: qlo * 128 + nq], nq, masks)
                                c0 = (qlo - qg * 4) * 128
                                self.mm(ob[:, c0:c0 + nq], vl[:, j * 128:(j + 1) * 128], p[:, 0:nq], False, False, r=[vl, p], w=[ob], inc=True)
                                self.mm(db[:, c0:c0 + nq], self.ones_b, p[:, 0:nq], False, False, r=[self.cb16, p], w=[db], inc=True)
                            finish(ob, db, h, ot, qg * 512, 512)
                            cnt["o"] += 1
                        self.store(self.attT[h * 128:(h + 1) * 128, lo:lo + S], ot[:], r=[ot])
                        if not last:
                            self.load(qc[:], self.qT[h * 128:(h + 1) * 128, co:co + L], w=[qc])
                            ob, db = obanks[cnt["o"] % 2]
                            for j in range(NBC):
                                p = score_block(kc, kc[:, j * 128:(j + 1) * 128], qc, qc[:, 0:L], L, [])
                                self.mm(ob[:, 0:L], vc[:, j * 128:(j + 1) * 128], p[:, 0:L], j == 0, False, r=[vc, p], w=[ob], inc=True)
                                self.mm(db[:, 0:L], self.ones_b, p[:, 0:L], j == 0, False, r=[self.cb16, p], w=[db], inc=True)
                            finish(ob, db, h, oc, 0, L)
                            cnt["o"] += 1
                            self.store(self.attT[h * 128:(h + 1) * 128, co:co + L], oc[:], r=[oc])

    def phase_conv(self, l):
        from contextlib import ExitStack
        c = self.c
        XB = c.DI // 128
        GB = c.GN // 128
        with ExitStack() as st:
            xin = [self.sb(st, [128, 4 * 516], BF16, "cxin%d" % i) for i in range(2)]
            acc = [self.sb(st, [128, 512], F32, "cacc%d" % i) for i in range(2)]
            ysb = [self.sb(st, [128, 4 * 512], BF16, "cy%d" % i) for i in range(2)]
            xs_tm = self.sb(st, [128, 4 * c.DI], BF16, "xs_tm")
            b_tm = self.sb(st, [128, 4 * c.GN], BF16, "b_tm")
            tps = self.banks[0:4]
            cnt = {"x": 0, "a": 0, "y": 0, "t": 0}
            chunks = []
            for b in range(c.BPC):
                for s0 in range(0, c.L, 512):
                    Lc = min(512, c.L - s0)
                    chunks.append((c.ctx_off(b) + s0, Lc, s0 == 0, s0 + Lc == c.L))
            for b in range(c.BPC):
                for s0 in range(0, c.S, 512):
                    chunks.append((c.lat_off(b) + s0, 512, s0 == 0, s0 + 512 == c.S))
            for (t0, Lc, zl, zr) in chunks:
                nt = Lc // 128
                for cg in range(0, c.CB, 4):
                    ncb = min(4, c.CB - cg)
                    xi = xin[cnt["x"] % 2]
                    cnt["x"] += 1
                    x3 = xi[:].rearrange("p (j t) -> p j t", t=516)
                    a = t0 - 2 if not zl else t0
                    bnd = t0 + Lc + 2 if not zr else t0 + Lc
                    oa = 0 if not zl else 2
                    if zl:
                        self.memset("pool", x3[:, 0:ncb, 0:2], 0.0, w=[xi])
                    if zr:
                        self.memset("pool", x3[:, 0:ncb, Lc + 2:Lc + 4], 0.0, w=[xi])
                    src = self.xbcT[cg * 128:(cg + ncb) * 128, a:bnd].rearrange("(j p) t -> p j t", p=128)
                    self.P.dma("sp", x3[:, 0:ncb, oa:oa + (bnd - a)], src, r=(), w=[xi])
                    yt = ysb[cnt["y"] % 2]
                    cnt["y"] += 1
                    for j in range(ncb):
                        cb = cg + j
                        ac = acc[cnt["a"] % 2]
                        cnt["a"] += 1
                        wv = lambda k: self.cvp[:, cb * 6 + k: cb * 6 + k + 1]
                        self.ts("dve", ac[:, 0:Lc], x3[:, j, 0:Lc], wv(0), wv(5), ALU.mult, ALU.add,
                                r=[xi, self.cvp], w=[ac])
                        for k in range(1, 5):
                            self.stt("dve", ac[:, 0:Lc], x3[:, j, k:k + Lc], wv(k), ac[:, 0:Lc], ALU.mult, ALU.add,
                                     r=[xi, self.cvp, ac], w=[ac])
                        self.act(yt[:, j * 512: j * 512 + Lc], ac[:, 0:Lc], AF.Silu, r=[ac], w=[yt])
                        if cb < XB or cb < XB + GB:
                            bk = tps[cnt["t"] % 4]
                            cnt["t"] += 1
                            bkb = bk[:].bitcast(BF16)
                            for tb in range(nt):
                                self.P.op("pe", lambda e, tb=tb, j=j, bkb=bkb, yt=yt: e.transpose(
                                    bkb[:, tb * 128:(tb + 1) * 128], yt[:, j * 512 + tb * 128: j * 512 + (tb + 1) * 128], self.ident_b),
                                    r=[yt, self.cb16], w=[bk], inc=(tb == nt - 1))
                            if cb < XB:
                                dst = xs_tm[:].rearrange("p (tb ch) -> p tb ch", ch=c.DI)[:, 0:nt, cb * 128:(cb + 1) * 128]
                                dtile = xs_tm
                            else:
                                dst = b_tm[:].rearrange("p (tb ch) -> p tb ch", ch=c.GN)[:, 0:nt, (cb - XB) * 128:(cb - XB + 1) * 128]
                                dtile = b_tm
                            self.cp("dve", dst, bkb[:, 0:nt * 128].rearrange("p (tb ch) -> p tb ch", ch=128), r=[bk], w=[dtile])
                    for j in range(ncb):
                        cb = cg + j
                        if cb >= XB:
                            dstT = self.BT if cb < XB + GB else self.CT
                            rb = (cb - XB) if cb < XB + GB else (cb - XB - GB)
                            self.store(dstT[rb * 128:(rb + 1) * 128, t0:t0 + Lc], yt[:, j * 512: j * 512 + Lc], r=[yt])
                self.store(self.xs[t0:t0 + Lc, :].rearrange("(tb p) ch -> p tb ch", p=128),
                           xs_tm[:].rearrange("p (tb ch) -> p tb ch", ch=c.DI)[:, 0:nt, :], r=[xs_tm])
                self.store(self.Btm[t0:t0 + Lc, :].rearrange("(tb p) ch -> p tb ch", p=128),
                           b_tm[:].rearrange("p (tb ch) -> p tb ch", ch=c.GN)[:, 0:nt, :], r=[b_tm])

    def phase_ssd(self, l, last):
        from contextlib import ExitStack
        c = self.c
        H, E, G, DI, GN = c.H, c.E, c.G, c.DI, c.GN
        EW = E * 64
        with ExitStack() as st:
            dt_c = [self.sb(st, [128, H], F32, "dt_c%d" % i) for i in range(2)]
            dta = [self.sb(st, [128, H], F32, "dta_c%d" % i) for i in range(2)]
            ea = [self.sb(st, [128, H], F32, "ea%d" % i) for i in range(2)]
            eal = [self.sb(st, [128, H], F32, "eal%d" % i) for i in range(2)]
            xs_c = [self.sb(st, [128, DI], BF16, "xs_c%d" % i) for i in range(2)]
            xdt = [self.sb(st, [128, DI], BF16, "xdt%d" % i) for i in range(2)]
            b_c = [self.sb(st, [128, GN], BF16, "b_c%d" % i) for i in range(2)]
            bT_c = [self.sb(st, [128, GN], BF16, "bT_c%d" % i) for i in range(2)]
            cT_c = [self.sb(st, [128, GN], BF16, "cT_c%d" % i) for i in range(2)]
            y_c = [self.sb(st, [128, DI], F32, "y_c%d" % i) for i in range(2)]
            Lm = [self.sb(st, [128, E * 128], F32, "Lm%d" % i) for i in range(2)]
            expD = [self.sb(st, [128, E * 128], F32, "expD%d" % i) for i in range(2)]
            Gm = [self.sb(st, [128, 128], F32, "Gm%d" % i) for i in range(2)]
            MT = [self.sb(st, [128, E * 128], BF16, "MT%d" % i) for i in range(2)]
            ytmp = [self.sb(st, [128, EW], F32, "ytmp%d" % i) for i in range(2)]
            wend = [self.sb(st, [128, EW], BF16, "wend%d" % i) for i in range(2)]
            S_f = self.sb(st, [128, G * EW], F32, "S_f")
            S_b = self.sb(st, [128, G * EW], BF16, "S_b")
            bk_misc, bk_D0, bk_D1, bk_G, bk_Y, bk_Yo, bk_cs = self.banks[0:7]
            ci = 0
            gi = 0
            for b in range(c.BPC):
                for d in range(2):
                    self.memset("dve", S_f[:], 0.0, w=[S_f])
                    self.memset("pool", S_b[:], 0.0, w=[S_b])
                    seq = [(c.ctx_off(b) + i * 128, True) for i in range(c.L // 128)]
                    if d == 1:
                        seq = seq[::-1]
                    lat = [(c.lat_off(b) + i * 128, False) for i in range(c.S // 128)]
                    if d == 1:
                        lat = lat[::-1]
                    icol = 127 if d == 0 else 0
                    for (t0, is_ctx) in seq + lat:
                        want_y = not (last and is_ctx)
                        k = ci % 2
                        ci += 1
                        self.load(dt_c[k][:], self.dtv[t0:t0 + 128, d * H:(d + 1) * H], w=[dt_c[k]])
                        self.load(xs_c[k][:], self.xs[t0:t0 + 128, :], w=[xs_c[k]])
                        self.load(b_c[k][:], self.Btm[t0:t0 + 128, :], w=[b_c[k]])
                        self.load(bT_c[k][:].rearrange("p (g t) -> p g t", t=128),
                                  self.BT[:, t0:t0 + 128].rearrange("(g p) t -> p g t", p=128), w=[bT_c[k]])
                        self.load(cT_c[k][:].rearrange("p (g t) -> p g t", t=128),
                                  self.CT[:, t0:t0 + 128].rearrange("(g p) t -> p g t", p=128), w=[cT_c[k]])
                        self.tt("dve", dta[k][:], dt_c[k][:], self.aneg[:, d * H:(d + 1) * H], ALU.mult,
                                r=[dt_c[k], self.aneg], w=[dta[k]])
                        self.mm(bk_misc[:, 0:H], self.U_f[d], dta[k][:], True, True, r=[self.cf, dta[k]], w=[bk_misc])
                        self.mm(bk_misc[:, H:2 * H], self.ones_f, dta[k][:], True, True, r=[self.cf, dta[k]], w=[bk_misc])
                        self.act(ea[k][:], bk_misc[:, 0:H], AF.Exp, r=[bk_misc], w=[ea[k]])
                        self.act(eal[k][:], bk_misc[:, H:2 * H], AF.Exp, r=[bk_misc], w=[eal[k]])
                        self.tt("pool", xdt[k][:].rearrange("p (h q) -> p h q", q=64),
                                xs_c[k][:].rearrange("p (h q) -> p h q", q=64),
                                bc_ap(dt_c[k][:], [[1, H], [0, 64]]), ALU.mult, r=[xs_c[k], dt_c[k]], w=[xdt[k]])
                        for g in range(G):
                            u = gi % 2
                            gi += 1
                            self.tt("dve", Lm[u][:].rearrange("p (e j) -> p e j", j=128),
                                    bc_ap(self.A_f[d], [[0, E], [1, 128]]),
                                    bc_ap(dta[k][:, g * E:(g + 1) * E], [[1, E], [0, 128]]), ALU.mult,
                                    r=[self.cf, dta[k]], w=[Lm[u]])
                            for e_ in range(E):
                                bkD = bk_D0 if e_ < 4 else bk_D1
                                self.mm(bkD[:, (e_ % 4) * 128:(e_ % 4 + 1) * 128], Lm[u][:, e_ * 128:(e_ + 1) * 128], self.U_f[d],
                                        True, True, r=[Lm[u], self.cf], w=[bkD], inc=(e_ % 4 == 3))
                            self.act(expD[u][:, 0:512], bk_D0[:, 0:512], AF.Exp, r=[bk_D0], w=[expD[u]])
                            self.act(expD[u][:, 512:1024], bk_D1[:, 0:512], AF.Exp, r=[bk_D1], w=[expD[u]])
                            bTg = bT_c[k][:, g * 128:(g + 1) * 128]
                            cTg = cT_c[k][:, g * 128:(g + 1) * 128]
                            if want_y:
                                self.mm(bk_G[:, 0:128], bTg, cTg, True, True, r=[bT_c[k], cT_c[k]], w=[bk_G])
                                self.tt("dve", Gm[u][:], bk_G[:, 0:128], self.U_f[d], ALU.mult, r=[bk_G, self.cf], w=[Gm[u]])
                                self.tt("dve", MT[u][:].rearrange("p (e i) -> p e i", i=128),
                                        expD[u][:].rearrange("p (e i) -> p e i", i=128),
                                        bc_ap(Gm[u][:], [[0, E], [1, 128]]), ALU.mult, r=[expD[u], Gm[u]], w=[MT[u]])
                                for e_ in range(E):
                                    hh = g * E + e_
                                    self.mm(bk_Y[:, e_ * 64:(e_ + 1) * 64], MT[u][:, e_ * 128:(e_ + 1) * 128],
                                            xdt[k][:, hh * 64:(hh + 1) * 64], True, True, r=[MT[u], xdt[k]], w=[bk_Y],
                                            inc=(e_ == E - 1))
                                self.mm(bk_Yo[:, 0:EW], cTg, S_b[:, g * EW:(g + 1) * EW], True, True, r=[cT_c[k], S_b], w=[bk_Yo])
                                self.tt("dve", ytmp[u][:].rearrange("p (e q) -> p e q", q=64),
                                        bk_Yo[:, 0:EW].rearrange("p (e q) -> p e q", q=64),
                                        bc_ap(ea[k][:, g * E:(g + 1) * E], [[1, E], [0, 64]]), ALU.mult,
                                        r=[bk_Yo, ea[k]], w=[ytmp[u]])
                                self.tt("dve", y_c[k][:, g * EW:(g + 1) * EW], ytmp[u][:], bk_Y[:, 0:EW], ALU.add,
                                        r=[ytmp[u], bk_Y], w=[y_c[k]])
                            self.tt("pool", wend[u][:].rearrange("p (e q) -> p e q", q=64),
                                    xdt[k][:, g * EW:(g + 1) * EW].rearrange("p (e q) -> p e q", q=64),
                                    bc_ap(expD[u][:, icol:icol + 1], [[128, E], [0, 64]]), ALU.mult,
                                    r=[xdt[k], expD[u]], w=[wend[u]])
                            self.mm(bk_cs[:, 0:EW], b_c[k][:, g * 128:(g + 1) * 128], wend[u][:], True, True,
                                    r=[b_c[k], wend[u]], w=[bk_cs])
                            Sg = S_f[:, g * EW:(g + 1) * EW]
                            self.tt("dve", Sg.rearrange("p (e q) -> p e q", q=64), Sg.rearrange("p (e q) -> p e q", q=64),
                                    bc_ap(eal[k][:, g * E:(g + 1) * E], [[1, E], [0, 64]]), ALU.mult,
                                    r=[S_f, eal[k], S_b, bk_Yo], w=[S_f])
                            self.tt("dve", Sg, Sg, bk_cs[:, 0:EW], ALU.add, r=[S_f, bk_cs], w=[S_f])
                            self.cp("act", S_b[:, g * EW:(g + 1) * EW], Sg, r=[S_f, bk_Yo], w=[S_b])
                        if want_y:
                            self.store(self.yd[d][t0:t0 + 128, :], y_c[k][:], r=[y_c[k]])

    def phase4(self, l, last):
        from contextlib import ExitStack
        c = self.c
        DI, H = c.DI, c.H
        NBK = DI // 128
        with ExitStack() as st:
            dsum_bc = self.sb(st, [128, DI], F32, "dsum_bc")
            gssd_bc = self.sb(st, [128, DI], F32, "gssd_bc")
            self.cp("dve", dsum_bc[:].rearrange("p (h q) -> p h q", q=64), bc_ap(self.dsum[:], [[1, H], [0, 64]]),
                    r=[self.dsum], w=[dsum_bc])
            self.load(gssd_bc[:], bass.AP(self.rows, self.gssd_off, [[0, 128], [1, DI]]), w=[gssd_bc])
            yf = [self.sb(st, [128, DI], F32, "p4yf%d" % i) for i in range(2)]
            yb = [self.sb(st, [128, DI], F32, "p4yb%d" % i) for i in range(2)]
            xs_ = [self.sb(st, [128, DI], BF16, "p4xs%d" % i) for i in range(2)]
            sz_ = [self.sb(st, [128, DI], BF16, "p4sz%d" % i) for i in range(2)]
            ob = [self.sb(st, [128, DI], BF16, "p4o%d" % i) for i in range(1)] * 2
            ssq = [self.sb(st, [128, 1], F32, "p4ssq%d" % i) for i in range(2)]
            acc = [self.sb(st, [128, NBK * 512], BF16, "p4acc%d" % i) for i in range(1)] * 2
            tps = self.banks[0:8]
            tcount = 0
            ai = 0
            tbs = []
            for (t0, kind, b, s0) in self.tiles512():
                if last and kind == "ctx":
                    continue
                tbs.append(t0)
            for ti, t0 in enumerate(tbs):
                ac = acc[ai % 2]
                ai += 1
                for s in range(4):
                    k = (ti * 4 + s) % 2
                    r0 = t0 + s * 128
                    self.load(yf[k][:], self.yd[0][r0:r0 + 128, :], w=[yf[k]])
                    self.load(yb[k][:], self.yd[1][r0:r0 + 128, :], w=[yb[k]])
                    self.load(xs_[k][:], self.xs[r0:r0 + 128, :], w=[xs_[k]])
                    self.load(sz_[k][:], self.sz[r0:r0 + 128, :], w=[sz_[k]])
                    self.tt("dve", yf[k][:], yf[k][:], yb[k][:], ALU.add, r=[yf[k], yb[k]], w=[yf[k]])
                    self.tt("pool", yb[k][:], xs_[k][:], dsum_bc[:], ALU.mult, r=[xs_[k], dsum_bc, yf[k]], w=[yb[k]])
                    self.tt("dve", yf[k][:], yf[k][:], yb[k][:], ALU.add, r=[yf[k], yb[k]], w=[yf[k]])
                    self.tt("pool", yf[k][:], yf[k][:], sz_[k][:], ALU.mult, r=[yf[k], sz_[k]], w=[yf[k]])
                    self.memset("dve", ssq[k][:], 0.0, w=[ssq[k]])
                    self.act(yb[k][:], yf[k][:], AF.Square, accum=ssq[k][:], r=[yf[k], ssq[k]], w=[yb[k], ssq[k]])
                    self.rsqrt(ssq[k][:], ssq[k][:], 1.0 / DI, [ssq[k]], ssq[k])
                    self.stt("dve", ob[k][:], yf[k][:], ssq[k][:, 0:1], gssd_bc[:], ALU.mult, ALU.mult,
                             r=[yf[k], ssq[k], gssd_bc], w=[ob[k]])
                    for q0 in range(0, NBK, 8):
                        bk = tps[tcount % 8]
                        tcount += 1
                        bkb = bk[:].bitcast(BF16)
                        nq = min(8, NBK - q0)
                        for q in range(nq):
                            self.P.op("pe", lambda e, q=q, q0=q0, bkb=bkb, k=k: e.transpose(
                                bkb[:, q * 128:(q + 1) * 128], ob[k][:, (q0 + q) * 128:(q0 + q + 1) * 128], self.ident_b),
                                r=[ob[k], self.cb16], w=[bk], inc=(q == nq - 1))
                        dst = ac[:].rearrange("p (blk t) -> p blk t", t=512)[:, q0:q0 + nq, s * 128:(s + 1) * 128]
                        self.cp("act", dst, bkb[:, 0:nq * 128].rearrange("p (blk t) -> p blk t", t=128), r=[bk], w=[ac])
                self.store(self.ssdT[:, t0:t0 + 512].rearrange("(blk p) t -> p blk t", p=128),
                           ac[:].rearrange("p (blk t) -> p blk t", t=512), r=[ac])

    def phase5a(self, l, last):
        from contextlib import ExitStack
        c = self.c
        N = 512
        AC = c.AW // 128
        SC = c.DI // 128
        DC = c.DC
        self._dense_rr = 0
        self._ws_rr = 0
        with ExitStack() as st:
            at = self.sb(st, [128, AC * N], BF16, "p5at")
            sd = self.sb(st, [128, SC * N], BF16, "p5sd")
            mT = self.sb(st, [128, DC * N], BF16, "p5m")
            wslots = [self.sb(st, [128, 16 * 512], BF16, "p5ws%d" % i) for i in range(3)]
            gA = [self.sb(st, [128, 4 * N], BF16, "p5gA%d" % i) for i in range(2)]
            gB = [self.sb(st, [128, 4 * N], BF16, "p5gB%d" % i) for i in range(2)]
            tA = [self.sb(st, [128, 4 * N], F32, "p5tA%d" % i) for i in range(2)]
            tB = [self.sb(st, [128, N], F32, "p5tB%d" % i) for i in range(2)]
            xb = [self.sb(st, [128, 4 * N], F32, "p5xb%d" % i) for i in range(2)]
            banksA = self.banks[0:4]
            banksB = self.banks[4:8]
            cnt = {"g": 0, "t": 0, "x": 0}
            for (t0, kind, b, s0) in self.tiles512():
                is_ctx = kind == "ctx"
                if last and is_ctx:
                    continue
                m = c.BPC if is_ctx else b
                self.load(at[:].rearrange("p (k n) -> p k n", n=N),
                          self.attT.ap().rearrange("(k p) t -> p k t", p=128)[:, :, t0:t0 + N], w=[at])
                self.load(sd[:].rearrange("p (k n) -> p k n", n=N),
                          self.ssdT.ap().rearrange("(k p) t -> p k t", p=128)[:, :, t0:t0 + N], w=[sd])
                for cg in range(0, DC, 4):
                    ncb = min(4, DC - cg)
                    gi = cnt["g"] % 2
                    cnt["g"] += 1
                    self.load(gA[gi][:, 0:ncb * N].rearrange("p (j n) -> p j n", n=N),
                              self.gT[cg * 128:(cg + ncb) * 128, t0:t0 + N].rearrange("(j p) t -> p j t", p=128), w=[gA[gi]])
                    self.load(gB[gi][:, 0:ncb * N].rearrange("p (j n) -> p j n", n=N),
                              self.gT[c.D + cg * 128: c.D + (cg + ncb) * 128, t0:t0 + N].rearrange("(j p) t -> p j t", p=128), w=[gB[gi]])

                    def epiA(cbi, bk, gi=gi):
                        self.tt("dve", tA[gi][:, cbi * N:(cbi + 1) * N], bk[:, 0:N], gA[gi][:, cbi * N:(cbi + 1) * N], ALU.mult,
                                r=[bk, gA[gi]], w=[tA[gi]])

                    def epiB(cbi, bk, gi=gi, cg=cg):
                        i = cnt["t"] % 2
                        cnt["t"] += 1
                        self.tt("dve", tB[i][:], bk[:, 0:N], gB[gi][:, cbi * N:(cbi + 1) * N], ALU.mult, r=[bk, gB[gi]], w=[tB[i]])
                        self.tt("pool", mT[:, (cg + cbi) * N:(cg + cbi + 1) * N], tB[i][:], tA[gi][:, cbi * N:(cbi + 1) * N], ALU.add,
                                r=[tB[i], tA[gi]], w=[mT])
                    self._dense_rr = 0
                    self.dense(self.wb_oa, c.AW, cg * 128, ncb * 128, at, lambda kc: at[:, kc * N:(kc + 1) * N], N, banksA, epiA, wslots)
                    self._dense_rr = 0
                    self.dense(self.wb_os, c.DI, cg * 128, ncb * 128, sd, lambda kc: sd[:, kc * N:(kc + 1) * N], N, banksB, epiB, wslots)
                for cg in range(0, DC, 4):
                    ncb = min(4, DC - cg)
                    xi = cnt["x"] % 2
                    cnt["x"] += 1
                    xv = self.xT[cg * 128:(cg + ncb) * 128, t0:t0 + N].rearrange("(j p) t -> p j t", p=128)
                    self.load(xb[xi][:, 0:ncb * N].rearrange("p (j n) -> p j n", n=N), xv, w=[xb[xi]])

                    def epiO(cbi, bk, xi=xi, cg=cg):
                        self.stt("dve", xb[xi][:, cbi * N:(cbi + 1) * N], bk[:, 0:N], self.mod_ap(l, 2, cg + cbi, m),
                                 xb[xi][:, cbi * N:(cbi + 1) * N], ALU.mult, ALU.add, r=[bk, self.mod, xb[xi]], w=[xb[xi]])
                    self.dense(self.wb_out, c.D, cg * 128, ncb * 128, mT, lambda kc: mT[:, kc * N:(kc + 1) * N], N, self.banks[0:8], epiO, wslots)
                    self.store(xv, xb[xi][:, 0:ncb * N].rearrange("p (j n) -> p j n", n=N), r=[xb[xi]])

    def phase5b(self, l, last):
        from contextlib import ExitStack
        c = self.c
        N = 512
        DC = c.DC
        FC = c.DFF // 128
        self._dense_rr = 0
        self._ws_rr = 0
        with ExitStack() as st:
            self.alloc_norm_tmps(st)
            xt = self.sb(st, [128, DC * N], F32, "p6x")
            h2 = self.sb(st, [128, DC * N], BF16, "p6h")
            f1 = self.sb(st, [128, FC * N], BF16, "p6f")
            wslots = [self.sb(st, [128, 16 * 512], BF16, "p6ws%d" % i) for i in range(3)]
            rl = [self.sb(st, [128, N], F32, "p6r%d" % i) for i in range(2)]
            cnt = {"r": 0}
            dbanks = self.banks[0:6]
            for (t0, kind, b, s0) in self.tiles512():
                is_ctx = kind == "ctx"
                if last and is_ctx:
                    continue
                m = c.BPC if is_ctx else b
                xsrc = self.xT.ap().rearrange("(dc p) t -> p dc t", p=128)[:, :, t0:t0 + N]
                self.load(xt[:].rearrange("p (dc n) -> p dc n", n=N), xsrc, w=[xt])
                self.norm_tile(st, xt, l, self.s2, lambda dc: self.mod_ap(l, 3, dc, m), m, h2)

                def epi1(cbi, bk):
                    i = cnt["r"] % 2
                    cnt["r"] += 1
                    self.act(rl[i][:], bk[:, 0:N], AF.Relu, r=[bk], w=[rl[i]])
                    self.tt("pool", f1[:, cbi * N:(cbi + 1) * N], rl[i][:], rl[i][:], ALU.mult, r=[rl[i]], w=[f1])
                self.dense(self.wb_ff1, c.D, 0, c.DFF, h2, lambda kc: h2[:, kc * N:(kc + 1) * N], N, dbanks, epi1, wslots)

                def epi2(cbi, bk):
                    self.stt("dve", xt[:, cbi * N:(cbi + 1) * N], bk[:, 0:N], self.mod_ap(l, 5, cbi, m),
                             xt[:, cbi * N:(cbi + 1) * N], ALU.mult, ALU.add, r=[bk, self.mod, xt], w=[xt])
                self.dense(self.wb_ff2, c.DFF, 0, c.D, f1, lambda kc: f1[:, kc * N:(kc + 1) * N], N, dbanks, epi2, wslots)
                if not last:
                    self.store(xsrc, xt[:].rearrange("p (dc n) -> p dc n", n=N), r=[xt])
                else:
                    gf = self.sb(st, [128, DC * c.NM], F32, "gfin%d" % t0)
                    g = self.gnt[:, 2 * c.DEPTH * DC:(2 * c.DEPTH + 1) * DC]
                    self.cp("dve", gf[:].rearrange("p (a b) -> p a b", b=c.NM), bc_ap(g, [[1, DC], [0, c.NM]]), r=[self.gnt], w=[gf])
                    of = self.sb(st, [128, DC * N], F32, "ofin%d" % t0) if False else f1
                    ofv = f1[:, 0:2 * DC * N].bitcast(F32)
                    DCn = DC
                    bk = self.banks[6]
                    for dc in range(DCn):
                        sq = self.sqs[dc % 2]
                        self.act(sq[:], xt[:, dc * N:(dc + 1) * N], AF.Square, r=[xt], w=[sq])
                        self.mm(bk[:, 0:N], self.ones_f, sq[:], dc == 0, dc == DCn - 1, r=[sq, self.cf], w=[bk], inc=True)
                    rstd = self.rstd
                    self.rsqrt(rstd[:], bk[:, 0:N], 1.0 / c.D, [bk], rstd)
                    for dc in range(DCn):
                        self.stt("dve", ofv[:, dc * N:(dc + 1) * N], xt[:, dc * N:(dc + 1) * N], g[:, dc:dc + 1], rstd[:],
                                 ALU.mult, ALU.mult, r=[xt, self.gnt, rstd, f1], w=[f1])
                    lo = b * c.S + s0
                    self.store(self.outT.ap().rearrange("(dc p) t -> p dc t", p=128)[:, :, lo:lo + N],
                               ofv[:, 0:DC * N].rearrange("p (dc n) -> p dc n", n=N), r=[f1])


def host_consts():
    r = np.arange(128)[:, None]
    cc = np.arange(128)[None, :]
    ident = (r == cc).astype(np.float32)
    U0 = (r <= cc).astype(np.float32)
    U1 = (r >= cc).astype(np.float32)
    SL = (r > cc).astype(np.float32)
    SU = (r < cc).astype(np.float32)
    ones = np.ones((128, 128), np.float32)
    R = np.zeros((128, 128), np.float32)
    for m in range(128):
        if m % 64 < 32:
            R[m + 32, m] = -1.0
        else:
            R[m - 32, m] = 1.0
    return np.concatenate([ident, U0, U1, SL, SU, ones, R], axis=1)


def host_rope(S, grid_w=64, theta=10000.0):
    t = np.arange(S)
    row = (t // grid_w).astype(np.float32)
    col = (t % grid_w).astype(np.float32)
    axis_dim = 64
    inv = (np.float32(theta) ** (-np.arange(0, axis_dim, 2, dtype=np.float32) / np.float32(axis_dim))).astype(np.float32)
    ang_r = (row[:, None] * inv[None]).astype(np.float32)
    ang_c = (col[:, None] * inv[None]).astype(np.float32)
    cos = np.zeros((128, S), np.float32)
    sin = np.zeros((128, S), np.float32)
    for d in range(128):
        a = ang_r if d < 64 else ang_c
        cos[d] = np.cos(a[:, d % 32])
        sin[d] = np.sin(a[:, d % 32])
    return np.concatenate([cos, sin], axis=1)


def fm(v, nchunk):
    v = np.asarray(v, np.float32)
    lead = v.shape[:-1]
    v = v.reshape(lead + (nchunk, 128))
    return np.moveaxis(v, -1, 0)


def prep_inputs(cfg, inp):
    c = cfg
    f32 = lambda a: np.ascontiguousarray(np.asarray(a, np.float32))
    x = f32(inp["x"])
    ctx = f32(inp["ctx"])
    cc = f32(inp["c"])
    c_ctx = f32(inp["c_ctx"])
    shared = {
        "w_ada": f32(inp["w_ada"]).reshape(c.DEPTH * c.D, 6 * c.D),
        "w_in": f32(inp["w_in"]).reshape(c.DEPTH * c.D, c.IN),
        "w_oa": f32(inp["w_o_attn"]).reshape(c.DEPTH * c.AW, c.D),
        "w_os": f32(inp["w_o_ssd"]).reshape(c.DEPTH * c.DI, c.D),
        "w_out": f32(inp["w_out"]).reshape(c.DEPTH * c.D, c.D),
        "w_ff1": f32(inp["w_ff1"]).reshape(c.DEPTH * c.D, c.DFF),
        "w_ff2": f32(inp["w_ff2"]).reshape(c.DEPTH * c.DFF, c.D),
    }
    shared["b_ada"] = f32(fm(inp["b_ada"], 6 * c.DC).reshape(128, -1))
    gn = np.concatenate([f32(inp["g_norm1"]), f32(inp["g_norm2"]), f32(inp["g_final"])[None]], axis=0)
    shared["gn"] = f32(fm(gn, c.DC).reshape(128, -1))
    cw = f32(inp["conv_w"])
    cb = f32(inp["conv_b"])
    cv = np.concatenate([cw, cb[:, None, :]], axis=1)
    cv = cv.reshape(c.DEPTH, 6, c.CB, 128)
    shared["convp"] = f32(np.transpose(cv, (3, 0, 2, 1)).reshape(128, -1))
    rows = np.concatenate([f32(inp["attn_sink"]), f32(inp["dt_bias"]).reshape(c.DEPTH, -1),
                           f32(inp["a_log"]).reshape(c.DEPTH, -1), f32(inp["d_skip"]).reshape(c.DEPTH, -1),
                           f32(inp["g_ssd"])], axis=1)
    shared["rows"] = f32(rows)
    shared["consts"] = host_consts()
    shared["rope"] = host_rope(c.S, c.GRID_W)
    in_maps = []
    for core in range(c.NCORES):
        bs = [core * c.BPC + i for i in range(c.BPC)]
        xin = np.concatenate([ctx[b].T for b in bs] + [x[b].T for b in bs], axis=1)
        cvec = np.stack([cc[b] for b in bs] + [c_ctx], axis=0)
        cin = np.transpose(cvec.reshape(c.NM, c.DC, 128), (2, 1, 0)).reshape(128, -1)
        m = dict(shared)
        m["xin"] = f32(xin)
        m["cin"] = f32(cin)
        in_maps.append(m)
    return in_maps


def build_nc(cfg):
    from contextlib import ExitStack
    nc = bass.Bass("TRN2", target_bir_lowering=False)
    with ExitStack() as stack:
        b = Builder(cfg, nc, stack)
        b.build()
    return nc


def run_cfg(cfg, inp, trace=False):
    in_maps = prep_inputs(cfg, inp)
    nc = build_nc(cfg)
    res = run_bass_kernel_spmd(nc, in_maps, core_ids=list(range(cfg.NCORES)))
    outs = []
    for core in range(cfg.NCORES):
        oT = res.results[core]["outT"]
        o = oT.T.reshape(cfg.BPC, cfg.S, cfg.D)
        outs.append(o)
    return np.ascontiguousarray(np.concatenate(outs, axis=0).astype(np.float32))


def kernel(**inputs):
    cfg = Cfg()
    return run_cfg(cfg, inputs)
```

```python
import math
import numpy as np
import concourse.bass as bass
import concourse.mybir as mybir
from concourse.bass_utils import run_bass_kernel_spmd

F32 = mybir.dt.float32
BF16 = mybir.dt.bfloat16
AF = mybir.ActivationFunctionType
ALU = mybir.AluOpType
AX = mybir.AxisListType

SAME_ENGINE_SYNC = True
DEBUG_STOP = None
DEBUG_SUB = None
DEBUG_NORM = None
DEBUG_TILES = None
DEBUG_ROPE = None
POOL_ENG = "dve"
EPS = 1e-6


class Cfg:
    def __init__(s, D=2048, B=16, S=2048, DEPTH=4, L=256, NH=16, NKV=4, G=8, NCORES=8):
        s.D, s.B, s.S, s.DEPTH, s.L, s.NH, s.NKV, s.G, s.NCORES = D, B, S, DEPTH, L, NH, NKV, G, NCORES
        s.HD = 128
        s.AW = NH * 128
        s.KW = NKV * 128
        s.REP = NH // NKV
        s.DI = 2 * D
        s.P = 64
        s.H = s.DI // 64
        s.E = s.H // G
        s.N = 128
        s.K = 5
        s.GN = G * 128
        s.CC = s.DI + 2 * s.GN
        s.DFF = 4 * D
        s.GRID_W = 64
        s.COL_K = 0
        s.COL_V = s.KW
        s.COL_XBC = 2 * s.KW
        s.COL_DT = s.COL_XBC + s.CC
        s.COL_Q = s.COL_DT + 2 * s.H
        s.COL_Z = s.COL_Q + s.AW
        s.COL_GATE = s.COL_Z + s.DI
        s.IN = s.COL_GATE + 2 * D
        s.BPC = B // NCORES
        s.T = s.BPC * (L + S)
        s.DC = D // 128
        s.NM = s.BPC + 1
        s.CB = s.CC // 128
        assert (s.BPC * L) % 512 == 0 and S % 512 == 0
        assert s.E * 64 == 512

    def ctx_off(s, b):
        return b * s.L

    def lat_off(s, b):
        return s.BPC * s.L + b * s.S


class TT:
    def __init__(self, t, name):
        self.t = t
        self.name = name
        self.w = {}
        self.r = {}
        self.excl = False

    def __getitem__(self, k):
        return self.t[k]


ENGS = ["pe", "act", "dve", "pool", "sp"]
NDMA = 16


class Prog:
    def __init__(self, nc, stack):
        self.nc = nc
        self.stack = stack
        self.ops = {e: [] for e in ENGS}
        self.sems = []
        self.prog_sem = {}
        self.seq = {}
        for e in ["pe", "act", "dve", "pool"]:
            self.prog_sem[e] = self._new_sem("prog_" + e)
            self.seq[e] = 0
        self.waited = {e: {} for e in ENGS}
        self.dma_pool = {}
        self.dma_rr = {}
        self.sem_val = {}
        for q in ["sp", "pool", "act"]:
            self.dma_pool[q] = [self._new_sem("dma_%s_%d" % (q, i)) for i in range(NDMA)]
            self.dma_rr[q] = 0
            for sidx in self.dma_pool[q]:
                self.sem_val[sidx] = 0
        self.tiles = []
        self.nops = 0

    def _new_sem(self, name):
        h = self.stack.enter_context(self.nc.semaphore(name))
        self.sems.append(h)
        return len(self.sems) - 1

    def track(self, t, name):
        tt = TT(t, name)
        self.tiles.append(tt)
        return tt

    def _deps(self, r, w):
        evs = {}
        for t in r:
            for k, v in t.w.items():
                if evs.get(k, 0) < v:
                    evs[k] = v
        for t in w:
            for k, v in t.w.items():
                if evs.get(k, 0) < v:
                    evs[k] = v
            for k, v in t.r.items():
                if evs.get(k, 0) < v:
                    evs[k] = v
        return evs

    def _emit_waits(self, eng, evs):
        wd = self.waited[eng]
        for k, v in evs.items():
            if wd.get(k, 0) < v:
                wd[k] = v
                self.ops[eng].append(("wait", k, v))

    def op(self, eng, fn, r=(), w=(), inc=True):
        w = list(w) + [t for t in r if t.excl and t not in w]
        r = [t for t in r if not t.excl]
        evs = self._deps(r, w)
        own = self.prog_sem[eng]
        if not SAME_ENGINE_SYNC or eng == "pe":
            evs.pop(own, None)
        else:
            raw = 0
            for t in r:
                v = t.w.get(own, 0)
                if v > raw:
                    raw = v
            if raw > 0:
                evs[own] = raw
            else:
                evs.pop(own, None)
        self._emit_waits(eng, evs)
        if inc:
            self.seq[eng] += 1
            ev = self.seq[eng]
        else:
            ev = self.seq[eng] + 1
        self.ops[eng].append(("op", fn, own if inc else None))
        self.nops += 1
        for t in r:
            if t.r.get(own, 0) < ev:
                t.r[own] = ev
        for t in w:
            t.w = {own: ev}
            t.r = {}

    def dma(self, q, out_ap, in_ap, r=(), w=(), slow=False):
        i = self.dma_rr[q]
        self.dma_rr[q] = (i + 1) % NDMA
        sem = self.dma_pool[q][i]
        prev = self.sem_val[sem]
        evs = self._deps(r, w)
        if prev > 0 and evs.get(sem, 0) < prev:
            evs[sem] = prev
        self._emit_waits(q, evs)
        val = prev + 16
        self.sem_val[sem] = val
        self.ops[q].append(("dma", out_ap, in_ap, sem, slow))
        self.nops += 1
        for t in r:
            if t.r.get(sem, 0) < val:
                t.r[sem] = val
        for t in w:
            t.w = {sem: val}
            t.r = {}

    def barrier(self):
        evs = {}
        for e, sidx in self.prog_sem.items():
            if self.seq[e] > 0:
                evs[sidx] = self.seq[e]
        for sidx, v in self.sem_val.items():
            if v > 0:
                evs[sidx] = v
        for e in ENGS:
            ev2 = dict(evs)
            if e in self.prog_sem:
                ev2.pop(self.prog_sem[e], None)
            self._emit_waits(e, ev2)
        for t in self.tiles:
            t.w = {}
            t.r = {}

    def emit(self):
        nc = self.nc
        sems = self.sems

        def replay(eng_name):
            def body(e):
                for o in self.ops[eng_name]:
                    if o[0] == "wait":
                        e.wait_ge(sems[o[1]], o[2])
                    elif o[0] == "op":
                        ins = o[1](e)
                        if o[2] is not None:
                            ins.then_inc(sems[o[2]], 1)
                    else:
                        if o[4]:
                            e.dma_start(out=o[1], in_=o[2], allow_slow_non_contiguous=True).then_inc(sems[o[3]], 16)
                        else:
                            e.dma_start(out=o[1], in_=o[2]).then_inc(sems[o[3]], 16)
            return body

        with nc.Block() as block:
            block.tensor(replay("pe"))
            block.scalar(replay("act"))
            block.vector(replay("dve"))
            block.gpsimd(replay("pool"))
            block.sync(replay("sp"))


def bc_ap(ap, dims):
    a = ap.ap
    return bass.AP(ap.tensor, ap.offset, [list(a[0])] + [list(d) for d in dims])


class Builder:
    def __init__(self, cfg, nc, stack):
        self.c = cfg
        self.nc = nc
        self.stack = stack
        self.P = Prog(nc, stack)
        self._uid = 0
        c = cfg
        dt_in = lambda name, shape: nc.dram_tensor(name, shape, F32, kind="ExternalInput")
        self.xin = dt_in("xin", [c.D, c.T])
        self.cin = dt_in("cin", [128, c.DC * c.NM])
        self.w_ada = dt_in("w_ada", [c.DEPTH * c.D, 6 * c.D])
        self.w_in = dt_in("w_in", [c.DEPTH * c.D, c.IN])
        self.w_oa = dt_in("w_oa", [c.DEPTH * c.AW, c.D])
        self.w_os = dt_in("w_os", [c.DEPTH * c.DI, c.D])
        self.w_out = dt_in("w_out", [c.DEPTH * c.D, c.D])
        self.w_ff1 = dt_in("w_ff1", [c.DEPTH * c.D, c.DFF])
        self.w_ff2 = dt_in("w_ff2", [c.DEPTH * c.DFF, c.D])
        self.b_ada = dt_in("b_ada", [128, c.DEPTH * 6 * c.DC])
        self.gn = dt_in("gn", [128, (2 * c.DEPTH + 1) * c.DC])
        self.convp = dt_in("convp", [128, c.DEPTH * c.CB * 6])
        self.rows = dt_in("rows", [c.DEPTH, c.NH + 6 * c.H + c.DI])
        self.consts = dt_in("consts", [128, 7 * 128])
        self.rope = dt_in("rope", [128, 2 * c.S])
        self.outT = nc.dram_tensor("outT", [c.D, c.BPC * c.S], F32, kind="ExternalOutput")
        sc = lambda name, shape, dt: nc.dram_tensor(name, shape, dt)
        self.wb_in = sc("wb_in", [c.D, c.IN], BF16)
        self.wb_oa = sc("wb_oa", [c.AW, c.D], BF16)
        self.wb_os = sc("wb_os", [c.DI, c.D], BF16)
        self.wb_out = sc("wb_out", [c.D, c.D], BF16)
        self.wb_ff1 = sc("wb_ff1", [c.D, c.DFF], BF16)
        self.wb_ff2 = sc("wb_ff2", [c.DFF, c.D], BF16)
        self.xT = sc("xT", [c.D, c.T], F32)
        self.qT = sc("qT", [c.AW, c.T], BF16)
        self.kT = sc("kT", [c.KW, c.T], BF16)
        self.vtm = sc("vtm", [c.T, c.KW], BF16)
        self.xbcT = sc("xbcT", [c.CC, c.T], BF16)
        self.dtv = sc("dtv", [c.T, 2 * c.H], F32)
        self.sz = sc("sz", [c.T, c.DI], BF16)
        self.gT = sc("gT", [2 * c.D, c.T], BF16)
        self.xs = sc("xs", [c.T, c.DI], BF16)
        self.Btm = sc("Btm", [c.T, c.GN], BF16)
        self.BT = sc("BT", [c.GN, c.T], BF16)
        self.CT = sc("CT", [c.GN, c.T], BF16)
        self.yd = [sc("yf", [c.T, c.DI], F32), sc("yb", [c.T, c.DI], F32)]
        self.attT = sc("attT", [c.AW, c.T], BF16)
        self.ssdT = sc("ssdT", [c.DI, c.T], BF16)
        self.banks = [self.P.track(nc.alloc_psum_tensor("bank%d" % i, [128, 512], F32), "bank%d" % i) for i in range(8)]
        self.bank_rr = 0
        for bk_ in self.banks:
            bk_.excl = True

    def sb(self, stack, shape, dt, name):
        self._uid += 1
        t = stack.enter_context(self.nc.sbuf_tensor("%s_%d" % (name, self._uid), shape, dt))
        return self.P.track(t, name)

    def mm(self, ps_ap, lhsT, rhs, start, stop, r, w, inc=None):
        if inc is None:
            inc = stop
        self.P.op("pe", lambda e: e.matmul(ps_ap, lhsT=lhsT, rhs=rhs, start=start, stop=stop), r=r, w=w, inc=inc)

    def act(self, out, in_, func, r, w, bias=None, scale=None, accum=None):
        kw = {}
        if bias is not None:
            kw["bias"] = bias
        if scale is not None:
            kw["scale"] = scale
        if accum is not None:
            kw["accum_out"] = accum
        self.P.op("act", lambda e: e.activation(out=out, in_=in_, func=func, **kw), r=r, w=w)

    def tt(self, eng, out, in0, in1, op, r, w):
        self.P.op(eng, lambda e: e.tensor_tensor(out=out, in0=in0, in1=in1, op=op), r=r, w=w)

    def ts(self, eng, out, in0, s1, s2, op0, op1, r, w):
        if s2 is None:
            self.P.op(eng, lambda e: e.tensor_single_scalar(out=out, in_=in0, scalar=s1, op=op0), r=r, w=w)
        else:
            self.P.op(eng, lambda e: e.tensor_scalar(out=out, in0=in0, scalar1=s1, scalar2=s2, op0=op0, op1=op1), r=r, w=w)

    def stt(self, eng, out, in0, scalar, in1, op0, op1, r, w):
        self.P.op(eng, lambda e: e.scalar_tensor_tensor(out=out, in0=in0, scalar=scalar, in1=in1, op0=op0, op1=op1), r=r, w=w)

    def cp(self, eng, out, in_, r, w):
        if eng == "act":
            self.P.op("act", lambda e: e.copy(out=out, in_=in_), r=r, w=w)
        else:
            self.P.op(eng, lambda e: e.tensor_copy(out=out, in_=in_), r=r, w=w)

    def rsqrt(self, out, in_, inv_n, r, wt):
        self.act(out, in_, AF.Sqrt, bias=self.eps_c[:, 0:1], scale=inv_n, r=list(r) + [self.eps_c], w=[wt])
        self.P.op("dve", lambda e: e.reciprocal(out=out, in_=out), r=[wt], w=[wt])

    def memset(self, eng, ap, val, w):
        self.P.op(eng, lambda e: e.memset(ap, val), r=(), w=w)

    def load(self, out_ap, in_ap, w, q="sp", slow=False):
        self.P.dma(q, out_ap, in_ap, r=(), w=w, slow=slow)

    def store(self, out_ap, in_ap, r, q="pool", slow=False):
        self.P.dma(q, out_ap, in_ap, r=r, w=(), slow=slow)

    def build(self):
        from contextlib import ExitStack
        c = self.c
        P = self.P
        nc = self.nc
        with ExitStack() as gs:
            self.gs = gs
            self.setup_consts(gs)
            self.phase_mod(gs)
            step = max(128, (1 << 22) // (c.T * 4) // 128 * 128)
            for r0 in range(0, c.D, step):
                r1 = min(c.D, r0 + step)
                P.dma("sp", self.xT[r0:r1, :], self.xin[r0:r1, :])
            P.barrier()
            steps = []
            for l in range(c.DEPTH):
                last = l == c.DEPTH - 1
                steps.append(lambda l=l, last=last: (self.cast_weights(l), self.layer_consts(l)))
                steps.append(lambda l=l, last=last: self.phase1(l, last))
                steps.append(lambda l=l, last=last: self.phase_attn(l, last))
                steps.append(lambda l=l, last=last: self.phase_conv(l))
                steps.append(lambda l=l, last=last: self.phase_ssd(l, last))
                steps.append(lambda l=l, last=last: self.phase4(l, last))
                steps.append(lambda l=l, last=last: self.phase5a(l, last))
                steps.append(lambda l=l, last=last: self.phase5b(l, last))
            for i, f in enumerate(steps):
                if DEBUG_STOP is not None and i >= DEBUG_STOP:
                    break
                f()
                P.barrier()
            P.emit()

    def setup_consts(self, gs):
        c = self.c
        cf = self.sb(gs, [128, 7 * 128], F32, "constf")
        self.load(cf[:], self.consts[:, :], w=[cf])
        self.cf = cf
        cbf = self.sb(gs, [128, 7 * 128], BF16, "constb")
        self.cp("dve", cbf[:], cf[:], r=[cf], w=[cbf])
        self.cb16 = cbf
        k = lambda t, i: t[:, i * 128:(i + 1) * 128]
        self.ident_b = k(cbf, 0)
        self.U_f = [k(cf, 1), k(cf, 2)]
        self.U_b = [k(cbf, 1), k(cbf, 2)]
        self.A_f = [k(cf, 3), k(cf, 4)]
        self.ones_f = k(cf, 5)
        self.ones_b = k(cbf, 5)
        self.R_b = k(cbf, 6)
        self.mod = self.sb(gs, [128, c.DEPTH * 6 * c.DC * c.NM], F32, "mod")
        self.s1 = self.sb(gs, [128, c.DEPTH * c.DC * c.NM], F32, "s1")
        self.s2 = self.sb(gs, [128, c.DEPTH * c.DC * c.NM], F32, "s2")
        self.gnt = self.sb(gs, [128, (2 * c.DEPTH + 1) * c.DC], F32, "gnt")
        self.load(self.gnt[:], self.gn[:, :], w=[self.gnt])
        self.zero_c = self.sb(gs, [128, 1], F32, "zeroc")
        self.memset("dve", self.zero_c[:], 0.0, w=[self.zero_c])
        self.eps_c = self.sb(gs, [128, 1], F32, "epsc")
        self.memset("dve", self.eps_c[:], EPS, w=[self.eps_c])
        self.one_c = self.sb(gs, [128, 1], F32, "onec")
        self.memset("dve", self.one_c[:], 1.0, w=[self.one_c])

    def mod_ap(self, l, j, dc, m):
        c = self.c
        o = ((l * 6 + j) * c.DC + dc) * c.NM + m
        return self.mod[:, o:o + 1]

    def phase_mod(self, gs):
        from contextlib import ExitStack
        c = self.c
        P = self.P
        with ExitStack() as st:
            ct = self.sb(st, [128, c.DC * c.NM], F32, "ct")
            sct = self.sb(st, [128, c.DC * c.NM], F32, "sct")
            bt = self.sb(st, [128, c.DEPTH * 6 * c.DC], F32, "bt")
            self.load(ct[:], self.cin[:, :], w=[ct])
            self.load(bt[:], self.b_ada[:, :], w=[bt])
            self.act(sct[:], ct[:], AF.Silu, r=[ct], w=[sct])
            KC = c.DC
            ws = [self.sb(st, [128, KC * 512], F32, "wada%d" % i) for i in range(2)]
            wi = 0
            ncg = (6 * c.D) // 512
            for l in range(c.DEPTH):
                for cg in range(ncg):
                    wt = ws[wi % 2]
                    wi += 1
                    src = self.w_ada[l * c.D:(l + 1) * c.D, cg * 512:(cg + 1) * 512].rearrange("(kc p) n -> p kc n", p=128)
                    self.load(wt[:].rearrange("p (kc n) -> p kc n", n=512), src, w=[wt])
                    for cb in range(4):
                        bk = self.banks[self.bank_rr % 8]
                        self.bank_rr += 1
                        for kc in range(KC):
                            self.mm(bk[:, 0:c.NM], wt[:, kc * 512 + cb * 128: kc * 512 + (cb + 1) * 128],
                                    sct[:, kc * c.NM:(kc + 1) * c.NM], kc == 0, kc == KC - 1,
                                    r=[wt, sct], w=[bk])
                        t = cg * 4 + cb
                        o = (l * 6 * c.DC + t) * c.NM
                        self.act(self.mod[:, o:o + c.NM], bk[:, 0:c.NM], AF.Identity,
                                 bias=bt[:, l * 6 * c.DC + t: l * 6 * c.DC + t + 1], r=[bk, bt], w=[self.mod])
            for l in range(c.DEPTH):
                for (dst, j, gi) in ((self.s1, 1, l), (self.s2, 4, c.DEPTH + l)):
                    o = (l * 6 + j) * c.DC * c.NM
                    od = l * c.DC * c.NM
                    n = c.DC * c.NM
                    self.ts("dve", dst[:, od:od + n], self.mod[:, o:o + n], 1.0, None, ALU.add, None,
                            r=[self.mod], w=[dst])
                    g = self.gnt[:, gi * c.DC:(gi + 1) * c.DC]
                    gb = bc_ap(g, [[1, c.DC], [0, c.NM]])
                    d3 = dst[:, od:od + n].rearrange("p (a b) -> p a b", b=c.NM)
                    self.tt("dve", d3, d3, gb, ALU.mult, r=[dst, self.gnt], w=[dst])
            P.barrier()

    def cast_weights(self, l):
        c = self.c
        for (src, dst, R, C) in ((self.w_in, self.wb_in, c.D, c.IN), (self.w_oa, self.wb_oa, c.AW, c.D),
                                 (self.w_os, self.wb_os, c.DI, c.D), (self.w_out, self.wb_out, c.D, c.D),
                                 (self.w_ff1, self.wb_ff1, c.D, c.DFF), (self.w_ff2, self.wb_ff2, c.DFF, c.D)):
            step = max(128, ((1 << 21) // C) // 128 * 128)
            for r0 in range(0, R, step):
                r1 = min(R, r0 + step)
                self.P.dma("pool", dst[r0:r1, :], src[l * R + r0: l * R + r1, :])

    def layer_consts(self, l):
        c = self.c
        if l == 0:
            gs = self.gs
            self.sinkexp = self.sb(gs, [128, c.NH], F32, "sinkexp")
            self.dtb = self.sb(gs, [128, 2 * c.H], F32, "dtb")
            self.aneg = self.sb(gs, [128, 2 * c.H], F32, "aneg")
            self.dsk = self.sb(gs, [128, 2 * c.H], F32, "dsk")
            self.dsum = self.sb(gs, [128, c.H], F32, "dsum")
            self.cvp = self.sb(gs, [128, c.CB * 6], F32, "cvp")
        rows = self.rows
        H2 = 2 * c.H

        def brow(o, n):
            return bass.AP(rows, l * (c.NH + 6 * c.H + c.DI) + o, [[0, 128], [1, n]])
        self.load(self.sinkexp[:], brow(0, c.NH), w=[self.sinkexp])
        self.load(self.dtb[:], brow(c.NH, H2), w=[self.dtb])
        self.load(self.aneg[:], brow(c.NH + H2, H2), w=[self.aneg])
        self.load(self.dsk[:], brow(c.NH + 2 * H2, H2), w=[self.dsk])
        self.load(self.cvp[:], self.convp[:, l * c.CB * 6:(l + 1) * c.CB * 6], w=[self.cvp])
        self.act(self.sinkexp[:], self.sinkexp[:], AF.Exp, r=[self.sinkexp], w=[self.sinkexp])
        self.act(self.aneg[:], self.aneg[:], AF.Exp, r=[self.aneg], w=[self.aneg])
        self.ts("dve", self.aneg[:], self.aneg[:], -1.0, None, ALU.mult, None, r=[self.aneg], w=[self.aneg])
        self.tt("dve", self.dsum[:], self.dsk[:, 0:c.H], self.dsk[:, c.H:H2], ALU.add, r=[self.dsk], w=[self.dsum])
        self.gssd_off = l * (c.NH + 6 * c.H + c.DI) + c.NH + 3 * H2

    def tiles512(self):
        c = self.c
        out = []
        for t0 in range(0, c.BPC * c.L, 512):
            out.append((t0, "ctx", None, None))
        for b in range(c.BPC):
            for s0 in range(0, c.S, 512):
                out.append((c.lat_off(b) + s0, "lat", b, s0))
        return out

    def norm_tile(self, st, xt, l_idx, s_t, sh_fn, m, out_t, out_is_f32=False):
        c = self.c
        DC = c.DC
        N = 512
        bk = self.banks[6]
        if DEBUG_NORM == 0:
            return
        for dc in range(DC):
            sq = self.sqs[dc % 2]
            self.act(sq[:], xt[:, dc * N:(dc + 1) * N], AF.Square, r=[xt], w=[sq])
            if DEBUG_NORM == -1:
                continue
            self.mm(bk[:, 0:N], self.ones_f, sq[:], dc == 0, dc == DC - 1, r=[sq, self.cf], w=[bk], inc=True)
        rstd = self.rstd
        if DEBUG_NORM == 1 or DEBUG_NORM == -1:
            return
        if DEBUG_NORM == 2:
            self.act(rstd[:], bk[:, 0:N], AF.Sqrt, bias=self.eps_c[:, 0:1], scale=1.0 / c.D, r=[bk, self.eps_c], w=[rstd])
            return
        self.rsqrt(rstd[:], bk[:, 0:N], 1.0 / c.D, [bk], rstd)
        if DEBUG_NORM == 3:
            return
        for dc in range(DC):
            tmp = self.ntmp[dc % 2]
            o = (l_idx * DC + dc) * c.NM + m
            self.stt("dve", tmp[:], xt[:, dc * N:(dc + 1) * N], s_t[:, o:o + 1], rstd[:], ALU.mult, ALU.mult,
                     r=[xt, s_t, rstd], w=[tmp])
            if DEBUG_NORM == 4:
                continue
            sh = sh_fn(dc)
            self.act(out_t[:, dc * N:(dc + 1) * N], tmp[:], AF.Identity, bias=sh, r=[tmp, self.mod, self.zero_c], w=[out_t])

    def alloc_norm_tmps(self, st):
        self.sqs = [self.sb(st, [128, 512], F32, "sq%d" % i) for i in range(2)]
        self.ntmp = [self.sb(st, [128, 512], F32, "ntmp%d" % i) for i in range(2)]
        self.rstd = self.sb(st, [128, 512], F32, "rstd")

    def dense(self, wb, R, c0, ncols, act_t, act_kc_ap, N, banks, epilogue, wslots):
        KT = R // 128
        KC = min(16, KT)
        nkg = KT // KC
        bi = 0
        for g0 in range(0, ncols, 512):
            gw = min(512, ncols - g0)
            ncb = gw // 128
            bks = []
            for i in range(ncb):
                bks.append(banks[self._dense_rr % len(banks)])
                self._dense_rr += 1
            for kg in range(nkg):
                wt = wslots[self._ws_rr % len(wslots)]
                self._ws_rr += 1
                src = wb[kg * KC * 128:(kg + 1) * KC * 128, c0 + g0: c0 + g0 + gw].rearrange("(kc p) n -> p kc n", p=128)
                dst = wt[:, 0:KC * gw].rearrange("p (kc n) -> p kc n", n=gw)
                self.load(dst, src, w=[wt])
                for cb in range(ncb):
                    for kc in range(KC):
                        first = kg == 0 and kc == 0
                        lastk = kg == nkg - 1 and kc == KC - 1
                        self.mm(bks[cb][:, 0:N], wt[:, kc * gw + cb * 128: kc * gw + (cb + 1) * 128],
                                act_kc_ap(kg * KC + kc), first, lastk, r=[wt, act_t], w=[bks[cb]], inc=(kc == KC - 1))
            for cb in range(ncb):
                epilogue(g0 // 128 + cb, bks[cb])

    def dense_tm(self, wb, R, c0, ncols, act_t, N, banks, epilogue, wslots):
        KT = R // 128
        assert KT <= 16 and ncols <= 512
        wt = wslots[self._ws_rr % len(wslots)]
        self._ws_rr += 1
        src = wb[0:R, c0:c0 + ncols].rearrange("(kc p) n -> p kc n", p=128)
        dst = wt[:, 0:KT * ncols].rearrange("p (kc n) -> p kc n", n=ncols)
        self.load(dst, src, w=[wt])
        for s in range(N // 128):
            bk = banks[self._dense_rr % len(banks)]
            self._dense_rr += 1
            for kc in range(KT):
                self.mm(bk[:, 0:ncols], act_t[:, kc * N + s * 128: kc * N + (s + 1) * 128],
                        wt[:, kc * ncols:(kc + 1) * ncols], kc == 0, kc == KT - 1, r=[wt, act_t], w=[bk])
            epilogue(s, bk)

    def phase1(self, l, last):
        from contextlib import ExitStack
        c = self.c
        N = 512
        self._dense_rr = 0
        self._ws_rr = 0
        with ExitStack() as st:
            self.alloc_norm_tmps(st)
            xt = self.sb(st, [128, c.DC * N], F32, "xt")
            h = self.sb(st, [128, c.DC * N], BF16, "h")
            wslots = [self.sb(st, [128, min(16, c.DC) * 512], BF16, "ws%d" % i) for i in range(3)]
            fm = [self.sb(st, [128, 4 * N], BF16, "fm%d" % i) for i in range(3)]
            tm = [self.sb(st, [128, 512], BF16, "tm%d" % i) for i in range(3)]
            cos_t = self.sb(st, [128, N], F32, "cos")
            sin_t = self.sb(st, [128, N], F32, "sin")
            t1 = [self.sb(st, [128, N], F32, "t1_%d" % i) for i in range(2)]
            t2 = [self.sb(st, [128, N], F32, "t2_%d" % i) for i in range(2)]
            qb = [self.sb(st, [128, N], BF16, "qb%d" % i) for i in range(2)]
            dtx = [self.sb(st, [128, 2 * c.H], F32, "dtx%d" % i) for i in range(2)]
            dta = [self.sb(st, [128, 2 * c.H], F32, "dta%d" % i) for i in range(2)]
            dtl = [self.sb(st, [128, 2 * c.H], F32, "dtl%d" % i) for i in range(2)]
            dto = [self.sb(st, [128, 2 * c.H], F32, "dto%d" % i) for i in range(2)]
            dbanks = self.banks[0:6]
            rr = {"fm": 0, "tm": 0, "rope": 0, "dt": 0}
            wl = self.wb_in
            for ti_, (t0, kind, b, s0) in enumerate(self.tiles512()):
                if DEBUG_TILES is not None and ti_ >= DEBUG_TILES:
                    break
                is_ctx = kind == "ctx"
                m = c.BPC if is_ctx else b
                skip_q = last and is_ctx
                self.load(xt[:].rearrange("p (dc n) -> p dc n", n=N),
                          self.xT.ap().rearrange("(dc p) t -> p dc t", p=128)[:, :, t0:t0 + N], w=[xt])
                self.norm_tile(st, xt, l, self.s1, lambda dc: self.mod_ap(l, 0, dc, m), m, h)
                if not is_ctx:
                    self.load(cos_t[:], self.rope[:, s0:s0 + N], w=[cos_t])
                    self.load(sin_t[:], self.rope[:, c.S + s0: c.S + s0 + N], w=[sin_t])
                hk = lambda kc: h[:, kc * N:(kc + 1) * N]

                def fm_family(c0, ncols, dst, kindf):
                    state = {}

                    def epi(cbi, bk):
                        j = cbi % 4
                        if j == 0:
                            state["o"] = fm[rr["fm"] % 3]
                            rr["fm"] += 1
                        o = state["o"]
                        oap = o[:, j * N:(j + 1) * N]
                        if kindf == "rope" and not is_ctx and DEBUG_ROPE != 0:
                            i = rr["rope"] % 2
                            rr["rope"] += 1
                            self.cp("act", qb[i][:], bk[:, 0:N], r=[bk], w=[qb[i]])
                            rb = self.banks[7]
                            self.mm(rb[:, 0:N], self.R_b, qb[i][:], True, True, r=[qb[i], self.cb16], w=[rb])
                            if DEBUG_ROPE == 1:
                                self.cp("act", oap, rb[:, 0:N], r=[rb], w=[o])
                                return
                            self.tt("dve", t1[i][:], bk[:, 0:N], cos_t[:], ALU.mult, r=[bk, cos_t, qb[i]], w=[t1[i]])
                            if DEBUG_ROPE == 2:
                                self.cp("act", oap, t1[i][:], r=[t1[i]], w=[o])
                                return
                            self.tt("dve", t2[i][:], rb[:, 0:N], sin_t[:], ALU.mult, r=[rb, sin_t], w=[t2[i]])
                            if DEBUG_ROPE == 3:
                                self.cp("act", oap, t2[i][:], r=[t2[i], t1[i]], w=[o])
                                return
                            self.tt(POOL_ENG, oap, t1[i][:], t2[i][:], ALU.add, r=[t1[i], t2[i]], w=[o])
                        elif kindf == "sigmoid":
                            self.act(oap, bk[:, 0:N], AF.Sigmoid, r=[bk], w=[o])
                        else:
                            self.cp("act", oap, bk[:, 0:N], r=[bk], w=[o])
                        nb = min(4, ncols // 128 - (cbi // 4) * 4)
                        if j == nb - 1:
                            r0 = (cbi // 4) * 512
                            d = dst[r0:r0 + nb * 128, t0:t0 + N].rearrange("(j p) t -> p j t", p=128)
                            self.store(d, o[:, 0:nb * N].rearrange("p (j t) -> p j t", t=N), r=[o])
                    self.dense(wl, c.D, c0, ncols, h, hk, N, dbanks, epi, wslots)

                def tm_family(c0, ncols, dst, func):
                    for g0 in range(0, ncols, 512):
                        gw = min(512, ncols - g0)

                        def epi(s, bk, g0=g0, gw=gw):
                            o = tm[rr["tm"] % 3]
                            rr["tm"] += 1
                            if func is None:
                                self.cp("act", o[:, 0:gw], bk[:, 0:gw], r=[bk], w=[o])
                            else:
                                self.act(o[:, 0:gw], bk[:, 0:gw], func, r=[bk], w=[o])
                            self.store(dst[t0 + s * 128: t0 + (s + 1) * 128, g0:g0 + gw], o[:, 0:gw], r=[o])
                        self.dense_tm(wl, c.D, c0 + g0, gw, h, N, dbanks, epi, wslots)

                if DEBUG_SUB == 0:
                    return
                fm_family(c.COL_K, c.KW, self.kT, "rope")
                if DEBUG_SUB == 1:
                    return
                tm_family(c.COL_V, c.KW, self.vtm, None)
                if DEBUG_SUB == 2:
                    return
                fm_family(c.COL_XBC, c.CC, self.xbcT, "copy")
                if DEBUG_SUB == 3:
                    return

                H2 = 2 * c.H

                def epi_dt(s, bk):
                    i = rr["dt"] % 2
                    rr["dt"] += 1
                    self.tt("dve", dtx[i][:], bk[:, 0:H2], self.dtb[:], ALU.add, r=[bk, self.dtb], w=[dtx[i]])
                    self.act(dta[i][:], dtx[i][:], AF.Abs, r=[dtx[i]], w=[dta[i]])
                    self.act(dtl[i][:], dta[i][:], AF.Exp, scale=-1.0, r=[dta[i]], w=[dtl[i]])
                    self.act(dtl[i][:], dtl[i][:], AF.Ln, bias=self.one_c[:, 0:1], r=[dtl[i], self.one_c], w=[dtl[i]])
                    self.stt("dve", dto[i][:], dtx[i][:], 0.0, dtl[i][:], ALU.max, ALU.add, r=[dtx[i], dtl[i]], w=[dto[i]])
                    self.store(self.dtv[t0 + s * 128: t0 + (s + 1) * 128, :], dto[i][:], r=[dto[i]])
                self.dense_tm(wl, c.D, c.COL_DT, H2, h, N, dbanks, epi_dt, wslots)
                if DEBUG_SUB == 4:
                    return
                if not skip_q:
                    fm_family(c.COL_Q, c.AW, self.qT, "rope")
                    if DEBUG_SUB == 5:
                        continue
                    tm_family(c.COL_Z, c.DI, self.sz, AF.Silu)
                    if DEBUG_SUB == 6:
                        continue
                    fm_family(c.COL_GATE, 2 * c.D, self.gT, "sigmoid")

    def phase_attn(self, l, last):
        from contextlib import ExitStack
        c = self.c
        S, L = c.S, c.L
        scale = 1.0 / math.sqrt(128.0)
        NBL = S // 128
        NBC = L // 128
        with ExitStack() as st:
            kc_t = [self.sb(st, [128, L], BF16, "kc%d" % i) for i in range(2)]
            kl_t = [self.sb(st, [128, S], BF16, "kl%d" % i) for i in range(2)]
            vc_t = [self.sb(st, [128, NBC * 128], BF16, "vc%d" % i) for i in range(2)]
            vl_t = [self.sb(st, [128, NBL * 128], BF16, "vl%d" % i) for i in range(2)]
            q_t = [self.sb(st, [128, S], BF16, "q%d" % i) for i in range(2)]
            qc_t = [self.sb(st, [128, L], BF16, "qc%d" % i) for i in range(2)]
            o_t = [self.sb(st, [128, S], BF16, "o%d" % i) for i in range(2)]
            oc_t = [self.sb(st, [128, L], BF16, "oc%d" % i) for i in range(2)]
            pT = [self.sb(st, [128, 512], BF16, "pT%d" % i) for i in range(4)]
            rec = [self.sb(st, [128, 512], F32, "rec%d" % i) for i in range(2)]
            sbanks = self.banks[0:4]
            obanks = [(self.banks[4], self.banks[5]), (self.banks[6], self.banks[7])]
            cnt = {"s": 0, "o": 0, "p": 0, "g": 0, "h": 0}

            def score_block(kt, kap, qt, qap, nq, mask_specs):
                sb_ = sbanks[cnt["s"] % 4]
                cnt["s"] += 1
                self.mm(sb_[:, 0:nq], kap, qap, True, True, r=[kt, qt], w=[sb_])
                p = pT[cnt["p"] % 4]
                cnt["p"] += 1
                self.act(p[:, 0:nq], sb_[:, 0:nq], AF.Exp, scale=scale, r=[sb_], w=[p])
                for (off, which) in mask_specs:
                    self.tt("dve", p[:, off:off + 128], p[:, off:off + 128], self.U_b[which], ALU.mult,
                            r=[p, self.cb16], w=[p])
                return p

            def finish(ob, db, h, ot, ocol, nq):
                r_ = rec[cnt["o"] % 2]
                self.ts("dve", r_[:, 0:nq], db[:, 0:nq], self.sinkexp[:, h:h + 1], None, ALU.add, None,
                        r=[db, self.sinkexp], w=[r_])
                self.P.op("dve", lambda e: e.reciprocal(out=r_[:, 0:nq], in_=r_[:, 0:nq]), r=[r_], w=[r_])
                self.tt("dve", ot[:, ocol:ocol + nq], ob[:, 0:nq], r_[:, 0:nq], ALU.mult, r=[ob, r_], w=[ot])

            for b in range(c.BPC):
                co = c.ctx_off(b)
                lo = c.lat_off(b)
                for g in range(c.NKV):
                    gi = cnt["g"] % 2
                    cnt["g"] += 1
                    kc, kl, vc, vl = kc_t[gi], kl_t[gi], vc_t[gi], vl_t[gi]
                    self.load(kc[:], self.kT[g * 128:(g + 1) * 128, co:co + L], w=[kc])
                    self.load(kl[:], self.kT[g * 128:(g + 1) * 128, lo:lo + S], w=[kl])
                    self.load(vc[:].rearrange("p (n d) -> p n d", d=128),
                              self.vtm[co:co + L, g * 128:(g + 1) * 128].rearrange("(n p) d -> p n d", p=128), w=[vc])
                    self.load(vl[:].rearrange("p (n d) -> p n d", d=128),
                              self.vtm[lo:lo + S, g * 128:(g + 1) * 128].rearrange("(n p) d -> p n d", p=128), w=[vl])
                    for hh in range(c.REP):
                        h = g * c.REP + hh
                        hi = cnt["h"] % 2
                        cnt["h"] += 1
                        q, qc, ot, oc = q_t[hi], qc_t[hi], o_t[hi], oc_t[hi]
                        self.load(q[:], self.qT[h * 128:(h + 1) * 128, lo:lo + S], w=[q])
                        for qg in range(S // 512):
                            ob, db = obanks[cnt["o"] % 2]
                            first = True
                            for j in range(NBC):
                                p = score_block(kc, kc[:, j * 128:(j + 1) * 128], q, q[:, qg * 512:(qg + 1) * 512], 512, [])
                                self.mm(ob[:, 0:512], vc[:, j * 128:(j + 1) * 128], p[:, 0:512], first, False, r=[vc, p], w=[ob], inc=True)
                                self.mm(db[:, 0:512], self.ones_b, p[:, 0:512], first, False, r=[self.cb16, p], w=[db], inc=True)
                                first = False
                            for j in range(qg * 4 - 1, qg * 4 + 5):
                                if j < 0 or j >= NBL:
                                    continue
                                qlo = max(j - 1, qg * 4)
                                qhi = min(j + 1, qg * 4 + 3)
                                nq = (qhi - qlo + 1) * 128
                                masks = []
                                for qb_ in range(qlo, qhi + 1):
                                    if qb_ == j - 1:
                                        masks.append(((qb_ - qlo) * 128, 0))
                                    elif qb_ == j + 1:
                                        masks.append(((qb_ - qlo) * 128, 1))
                                p = score_block(kl, kl[:, j * 128:(j + 1) * 128], q, q[:, qlo * 128: qlo * 128 + nq], nq, masks)
                                c0 = (qlo - qg * 4) * 128
                                self.mm(ob[:, c0:c0 + nq], vl[:, j * 128:(j + 1) * 128], p[:, 0:nq], False, False, r=[vl, p], w=[ob], inc=True)
                                self.mm(db[:, c0:c0 + nq], self.ones_b, p[:, 0:nq], False, False, r=[self.cb16, p], w=[db], inc=True)
                            finish(ob, db, h, ot, qg * 512, 512)
                            cnt["o"] += 1
                        self.store(self.attT[h * 128:(h + 1) * 128, lo:lo + S], ot[:], r=[ot])
                        if not last:
                            self.load(qc[:], self.qT[h * 128:(h + 1) * 128, co:co + L], w=[qc])
                            ob, db = obanks[cnt["o"] % 2]
                            for j in range(NBC):
                                p = score_block(kc, kc[:, j * 128:(j + 1) * 128], qc, qc[:, 0:L], L, [])
                                self.mm(ob[:, 0:L], vc[:, j * 128:(j + 1) * 128], p[:, 0:L], j == 0, False, r=[vc, p], w=[ob], inc=True)
                                self.mm(db[:, 0:L], self.ones_b, p[:, 0:L], j == 0, False, r=[self.cb16, p], w=[db], inc=True)
                            finish(ob, db, h, oc, 0, L)
                            cnt["o"] += 1
                            self.store(self.attT[h * 128:(h + 1) * 128, co:co + L], oc[:], r=[oc])

    def phase_conv(self, l):
        from contextlib import ExitStack
        c = self.c
        XB = c.DI // 128
        GB = c.GN // 128
        with ExitStack() as st:
            dg = self.sb(st, [128, c.CB * 5 * 128], BF16, "dg")
            xin = [self.sb(st, [128, 4 * 516], BF16, "cxin%d" % i) for i in range(2)]
            ysb = [self.sb(st, [128, 4 * 512], BF16, "cy%d" % i) for i in range(2)]
            xs_tm = self.sb(st, [128, 4 * c.DI], BF16, "xs_tm")
            b_tm = self.sb(st, [128, 4 * c.GN], BF16, "b_tm")
            ident_f = self.cf[:, 0:128]
            q = 0
            for cb in range(c.CB):
                for k in range(5):
                    wv = self.cvp[:, cb * 6 + k: cb * 6 + k + 1]
                    o = (cb * 5 + k) * 128
                    self.act(dg[:, o:o + 128], ident_f, AF.Identity, scale=wv, r=[self.cf, self.cvp], w=[dg])
            cps = self.banks[0:4]
            tps = self.banks[4:8]
            cnt = {"x": 0, "a": 0, "y": 0, "t": 0}
            chunks = []
            for b in range(c.BPC):
                for s0 in range(0, c.L, 512):
                    Lc = min(512, c.L - s0)
                    chunks.append((c.ctx_off(b) + s0, Lc, s0 == 0, s0 + Lc == c.L))
            for b in range(c.BPC):
                for s0 in range(0, c.S, 512):
                    chunks.append((c.lat_off(b) + s0, 512, s0 == 0, s0 + 512 == c.S))
            for (t0, Lc, zl, zr) in chunks:
                nt = Lc // 128
                for cg in range(0, c.CB, 4):
                    ncb = min(4, c.CB - cg)
                    xi = xin[cnt["x"] % 2]
                    cnt["x"] += 1
                    x3 = xi[:].rearrange("p (j t) -> p j t", t=516)
                    a = t0 - 2 if not zl else t0
                    bnd = t0 + Lc + 2 if not zr else t0 + Lc
                    oa = 0 if not zl else 2
                    if zl:
                        self.memset("dve", x3[:, 0:ncb, 0:2], 0.0, w=[xi])
                    if zr:
                        self.memset("dve", x3[:, 0:ncb, Lc + 2:Lc + 4], 0.0, w=[xi])
                    src = self.xbcT[cg * 128:(cg + ncb) * 128, a:bnd].rearrange("(j p) t -> p j t", p=128)
                    self.P.dma("sp", x3[:, 0:ncb, oa:oa + (bnd - a)], src, r=(), w=[xi])
                    yt = ysb[cnt["y"] % 2]
                    cnt["y"] += 1
                    for j in range(ncb):
                        cb = cg + j
                        pb = cps[cnt["a"] % 4]
                        cnt["a"] += 1
                        for k in range(5):
                            o = (cb * 5 + k) * 128
                            self.mm(pb[:, 0:Lc], dg[:, o:o + 128], x3[:, j, k:k + Lc], k == 0, k == 4, r=[dg, xi], w=[pb])
                        self.act(yt[:, j * 512: j * 512 + Lc], pb[:, 0:Lc], AF.Silu, bias=self.cvp[:, cb * 6 + 5: cb * 6 + 6],
                                 r=[pb, self.cvp], w=[yt])
                        if cb < XB + GB:
                            bk = tps[cnt["t"] % 4]
                            cnt["t"] += 1
                            bkb = bk[:].bitcast(BF16)
                            for tb in range(nt):
                                self.P.op("pe", lambda e, tb=tb, j=j, bkb=bkb, yt=yt: e.transpose(
                                    bkb[:, tb * 128:(tb + 1) * 128], yt[:, j * 512 + tb * 128: j * 512 + (tb + 1) * 128], self.ident_b),
                                    r=[yt, self.cb16], w=[bk], inc=(tb == nt - 1))
                            if cb < XB:
                                dst = xs_tm[:].rearrange("p (tb ch) -> p tb ch", ch=c.DI)[:, 0:nt, cb * 128:(cb + 1) * 128]
                                dtile = xs_tm
                            else:
                                dst = b_tm[:].rearrange("p (tb ch) -> p tb ch", ch=c.GN)[:, 0:nt, (cb - XB) * 128:(cb - XB + 1) * 128]
                                dtile = b_tm
                            self.cp("dve", dst, bkb[:, 0:nt * 128].rearrange("p (tb ch) -> p tb ch", ch=128), r=[bk], w=[dtile])
                    for j in range(ncb):
                        cb = cg + j
                        if cb >= XB:
                            dstT = self.BT if cb < XB + GB else self.CT
                            rb = (cb - XB) if cb < XB + GB else (cb - XB - GB)
                            self.store(dstT[rb * 128:(rb + 1) * 128, t0:t0 + Lc], yt[:, j * 512: j * 512 + Lc], r=[yt])
                self.store(self.xs[t0:t0 + Lc, :].rearrange("(tb p) ch -> p tb ch", p=128),
                           xs_tm[:].rearrange("p (tb ch) -> p tb ch", ch=c.DI)[:, 0:nt, :], r=[xs_tm])
                self.store(self.Btm[t0:t0 + Lc, :].rearrange("(tb p) ch -> p tb ch", p=128),
                           b_tm[:].rearrange("p (tb ch) -> p tb ch", ch=c.GN)[:, 0:nt, :], r=[b_tm])

    def phase_ssd(self, l, last):
        from contextlib import ExitStack
        c = self.c
        H, E, G, DI, GN = c.H, c.E, c.G, c.DI, c.GN
        EW = E * 64
        NB3 = 3
        with ExitStack() as st:
            dt_c = [self.sb(st, [128, H], F32, "dt_c%d" % i) for i in range(2)]
            dta = [self.sb(st, [128, H], F32, "dta_c%d" % i) for i in range(2)]
            ea = [self.sb(st, [128, H], F32, "ea%d" % i) for i in range(2)]
            eal = [self.sb(st, [128, H], F32, "eal%d" % i) for i in range(2)]
            xs_c = [self.sb(st, [128, DI], BF16, "xs_c%d" % i) for i in range(2)]
            xdt = [self.sb(st, [128, DI], BF16, "xdt%d" % i) for i in range(2)]
            b_c = [self.sb(st, [128, GN], BF16, "b_c%d" % i) for i in range(2)]
            bT_c = [self.sb(st, [128, GN], BF16, "bT_c%d" % i) for i in range(2)]
            cT_c = [self.sb(st, [128, GN], BF16, "cT_c%d" % i) for i in range(2)]
            y_c = [self.sb(st, [128, DI], F32, "y_c%d" % i) for i in range(2)]
            Lm = [self.sb(st, [128, E * 128], F32, "Lm%d" % i) for i in range(NB3)]
            expD = [self.sb(st, [128, E * 128], F32, "expD%d" % i) for i in range(NB3)]
            Gm = [self.sb(st, [128, 128], F32, "Gm%d" % i) for i in range(NB3)]
            MT = [self.sb(st, [128, E * 128], BF16, "MT%d" % i) for i in range(2)]
            ytmp = [self.sb(st, [128, EW], F32, "ytmp%d" % i) for i in range(2)]
            wend = [self.sb(st, [128, EW], BF16, "wend%d" % i) for i in range(2)]
            S_f = [self.sb(st, [128, EW], F32, "S_f%d" % g) for g in range(G)]
            S_b = [self.sb(st, [128, EW], BF16, "S_b%d" % g) for g in range(G)]
            bk_Gm = self.banks[0]
            Dsets = [(self.banks[1], self.banks[2]), (self.banks[3], self.banks[4])]
            bk_Y, bk_Yo, bk_cs = self.banks[5], self.banks[6], self.banks[7]

            def prologue(ch):
                k, d, t0 = ch["k"], ch["d"], ch["t0"]
                self.load(dt_c[k][:], self.dtv[t0:t0 + 128, d * H:(d + 1) * H], w=[dt_c[k]])
                self.load(xs_c[k][:], self.xs[t0:t0 + 128, :], w=[xs_c[k]])
                self.load(b_c[k][:], self.Btm[t0:t0 + 128, :], w=[b_c[k]])
                self.load(bT_c[k][:].rearrange("p (g t) -> p g t", t=128),
                          self.BT[:, t0:t0 + 128].rearrange("(g p) t -> p g t", p=128), w=[bT_c[k]])
                self.load(cT_c[k][:].rearrange("p (g t) -> p g t", t=128),
                          self.CT[:, t0:t0 + 128].rearrange("(g p) t -> p g t", p=128), w=[cT_c[k]])
                self.tt("dve", dta[k][:], dt_c[k][:], self.aneg[:, d * H:(d + 1) * H], ALU.mult,
                        r=[dt_c[k], self.aneg], w=[dta[k]])
                self.mm(bk_Gm[:, 128:128 + H], self.U_f[d], dta[k][:], True, True, r=[self.cf, dta[k]], w=[bk_Gm])
                self.mm(bk_Gm[:, 128 + H:128 + 2 * H], self.ones_f, dta[k][:], True, True, r=[self.cf, dta[k]], w=[bk_Gm])
                self.act(ea[k][:], bk_Gm[:, 128:128 + H], AF.Exp, r=[bk_Gm], w=[ea[k]])
                self.act(eal[k][:], bk_Gm[:, 128 + H:128 + 2 * H], AF.Exp, r=[bk_Gm], w=[eal[k]])
                self.tt("dve", xdt[k][:].rearrange("p (h q) -> p h q", q=64),
                        xs_c[k][:].rearrange("p (h q) -> p h q", q=64),
                        bc_ap(dt_c[k][:], [[1, H], [0, 64]]), ALU.mult, r=[xs_c[k], dt_c[k]], w=[xdt[k]])

            def S1(u):
                k, d, g, n3 = u["k"], u["d"], u["g"], u["n"] % NB3
                for e_ in range(E):
                    hh = g * E + e_
                    self.act(Lm[n3][:, e_ * 128:(e_ + 1) * 128], self.U_f[d], AF.Identity,
                             scale=dta[k][:, hh:hh + 1], r=[self.cf, dta[k]], w=[Lm[n3]])

            def S2(u):
                k, d, g, n3 = u["k"], u["d"], u["g"], u["n"] % NB3
                D0, D1 = Dsets[u["n"] % 2]
                self.mm(D0[:, 0:512], self.A_f[d], Lm[n3][:, 0:512], True, True, r=[Lm[n3], self.cf], w=[D0])
                self.mm(D1[:, 0:512], self.A_f[d], Lm[n3][:, 512:1024], True, True, r=[Lm[n3], self.cf], w=[D1])
                if u["want_y"]:
                    bTg = bT_c[k][:, g * 128:(g + 1) * 128]
                    cTg = cT_c[k][:, g * 128:(g + 1) * 128]
                    self.mm(bk_Gm[:, 0:128], bTg, cTg, True, True, r=[bT_c[k], cT_c[k]], w=[bk_Gm])
                self.act(expD[n3][:, 0:512], D0[:, 0:512], AF.Exp, r=[D0], w=[expD[n3]])
                self.act(expD[n3][:, 512:1024], D1[:, 0:512], AF.Exp, r=[D1], w=[expD[n3]])
                if u["want_y"]:
                    self.tt("dve", Gm[n3][:], bk_Gm[:, 0:128], self.U_f[d], ALU.mult, r=[bk_Gm, self.cf], w=[Gm[n3]])

            def S3(u):
                k, d, g, n3, n2 = u["k"], u["d"], u["g"], u["n"] % NB3, u["n"] % 2
                icol = 127 if d == 0 else 0
                want_y = u["want_y"]
                cTg = cT_c[k][:, g * 128:(g + 1) * 128]
                self.tt("dve", wend[n2][:].rearrange("p (e q) -> p e q", q=64),
                        xdt[k][:, g * EW:(g + 1) * EW].rearrange("p (e q) -> p e q", q=64),
                        bc_ap(expD[n3][:, icol:icol + 1], [[128, E], [0, 64]]), ALU.mult,
                        r=[xdt[k], expD[n3]], w=[wend[n2]])
                if want_y:
                    self.tt("dve", MT[n2][:].rearrange("p (e i) -> p e i", i=128),
                            expD[n3][:].rearrange("p (e i) -> p e i", i=128),
                            bc_ap(Gm[n3][:], [[0, E], [1, 128]]), ALU.mult, r=[expD[n3], Gm[n3]], w=[MT[n2]])
                self.mm(bk_cs[:, 0:EW], b_c[k][:, g * 128:(g + 1) * 128], wend[n2][:], True, True,
                        r=[b_c[k], wend[n2]], w=[bk_cs])
                if want_y:
                    self.mm(bk_Yo[:, 0:EW], cTg, S_b[g][:], True, True, r=[cT_c[k], S_b[g]], w=[bk_Yo])
                    for e_ in range(E):
                        hh = g * E + e_
                        self.mm(bk_Y[:, e_ * 64:(e_ + 1) * 64], MT[n2][:, e_ * 128:(e_ + 1) * 128],
                                xdt[k][:, hh * 64:(hh + 1) * 64], True, True, r=[MT[n2], xdt[k]], w=[bk_Y],
                                inc=(e_ == E - 1))
                Sg = S_f[g][:]
                self.tt("dve", Sg.rearrange("p (e q) -> p e q", q=64), Sg.rearrange("p (e q) -> p e q", q=64),
                        bc_ap(eal[k][:, g * E:(g + 1) * E], [[1, E], [0, 64]]), ALU.mult,
                        r=[S_f[g], eal[k]], w=[S_f[g]])
                self.tt("dve", Sg, Sg, bk_cs[:, 0:EW], ALU.add, r=[S_f[g], bk_cs], w=[S_f[g]])
                if want_y:
                    self.tt("dve", ytmp[n2][:].rearrange("p (e q) -> p e q", q=64),
                            bk_Yo[:, 0:EW].rearrange("p (e q) -> p e q", q=64),
                            bc_ap(ea[k][:, g * E:(g + 1) * E], [[1, E], [0, 64]]), ALU.mult,
                            r=[bk_Yo, ea[k]], w=[ytmp[n2]])
                    self.tt("dve", y_c[k][:, g * EW:(g + 1) * EW], ytmp[n2][:], bk_Y[:, 0:EW], ALU.add,
                            r=[ytmp[n2], bk_Y], w=[y_c[k]])
                    if g == G - 1:
                        self.store(self.yd[d][u["t0"]:u["t0"] + 128, :], y_c[k][:], r=[y_c[k]])

            def S4(u):
                g = u["g"]
                self.cp("act", S_b[g][:], S_f[g][:], r=[S_f[g]], w=[S_b[g]])

            ci = 0
            for b in range(c.BPC):
                for d in range(2):
                    for g in range(G):
                        self.memset("dve", S_f[g][:], 0.0, w=[S_f[g]])
                        self.memset("dve", S_b[g][:], 0.0, w=[S_b[g]])
                    seq = [(c.ctx_off(b) + i * 128, True) for i in range(c.L // 128)]
                    lat = [(c.lat_off(b) + i * 128, False) for i in range(c.S // 128)]
                    if d == 1:
                        seq = seq[::-1]
                        lat = lat[::-1]
                    units = []
                    chunks = []
                    for (t0, is_ctx) in seq + lat:
                        ch = {"k": ci % 2, "d": d, "t0": t0}
                        ci += 1
                        chunks.append(ch)
                        for g in range(G):
                            units.append({"k": ch["k"], "d": d, "g": g, "t0": t0, "n": len(units),
                                          "want_y": not (last and is_ctx), "ch": ch if g == 0 else None})
                    NU = len(units)
                    for t in range(-2, NU + 1):
                        if 0 <= t + 2 < NU:
                            u = units[t + 2]
                            if u["ch"] is not None:
                                prologue(u["ch"])
                            S1(u)
                        if 0 <= t + 1 < NU:
                            S2(units[t + 1])
                        if 0 <= t < NU:
                            S3(units[t])
                        if 0 <= t - 1 < NU:
                            S4(units[t - 1])

    def phase4(self, l, last):
        from contextlib import ExitStack
        c = self.c
        DI, H = c.DI, c.H
        NBK = DI // 128
        with ExitStack() as st:
            dsum_bc = self.sb(st, [128, DI], F32, "dsum_bc")
            gssd_bc = self.sb(st, [128, DI], F32, "gssd_bc")
            self.cp("dve", dsum_bc[:].rearrange("p (h q) -> p h q", q=64), bc_ap(self.dsum[:], [[1, H], [0, 64]]),
                    r=[self.dsum], w=[dsum_bc])
            self.load(gssd_bc[:], bass.AP(self.rows, self.gssd_off, [[0, 128], [1, DI]]), w=[gssd_bc])
            yf = [self.sb(st, [128, DI], F32, "p4yf%d" % i) for i in range(2)]
            yb = [self.sb(st, [128, DI], F32, "p4yb%d" % i) for i in range(2)]
            xs_ = [self.sb(st, [128, DI], BF16, "p4xs%d" % i) for i in range(2)]
            sz_ = [self.sb(st, [128, DI], BF16, "p4sz%d" % i) for i in range(2)]
            ob = [self.sb(st, [128, DI], BF16, "p4o%d" % i) for i in range(1)] * 2
            ssq = [self.sb(st, [128, 1], F32, "p4ssq%d" % i) for i in range(2)]
            acc = [self.sb(st, [128, NBK * 512], BF16, "p4acc%d" % i) for i in range(1)] * 2
            tps = self.banks[0:8]
            tcount = 0
            ai = 0
            tbs = []
            for (t0, kind, b, s0) in self.tiles512():
                if last and kind == "ctx":
                    continue
                tbs.append(t0)
            for ti, t0 in enumerate(tbs):
                ac = acc[ai % 2]
                ai += 1
                for s in range(4):
                    k = (ti * 4 + s) % 2
                    r0 = t0 + s * 128
                    self.load(yf[k][:], self.yd[0][r0:r0 + 128, :], w=[yf[k]])
                    self.load(yb[k][:], self.yd[1][r0:r0 + 128, :], w=[yb[k]])
                    self.load(xs_[k][:], self.xs[r0:r0 + 128, :], w=[xs_[k]])
                    self.load(sz_[k][:], self.sz[r0:r0 + 128, :], w=[sz_[k]])
                    self.tt("dve", yf[k][:], yf[k][:], yb[k][:], ALU.add, r=[yf[k], yb[k]], w=[yf[k]])
                    self.tt("pool", yb[k][:], xs_[k][:], dsum_bc[:], ALU.mult, r=[xs_[k], dsum_bc, yf[k]], w=[yb[k]])
                    self.tt("dve", yf[k][:], yf[k][:], yb[k][:], ALU.add, r=[yf[k], yb[k]], w=[yf[k]])
                    self.tt("pool", yf[k][:], yf[k][:], sz_[k][:], ALU.mult, r=[yf[k], sz_[k]], w=[yf[k]])
                    self.memset("dve", ssq[k][:], 0.0, w=[ssq[k]])
                    self.act(yb[k][:], yf[k][:], AF.Square, accum=ssq[k][:], r=[yf[k], ssq[k]], w=[yb[k], ssq[k]])
                    self.rsqrt(ssq[k][:], ssq[k][:], 1.0 / DI, [ssq[k]], ssq[k])
                    self.stt("dve", ob[k][:], yf[k][:], ssq[k][:, 0:1], gssd_bc[:], ALU.mult, ALU.mult,
                             r=[yf[k], ssq[k], gssd_bc], w=[ob[k]])
                    for q0 in range(0, NBK, 8):
                        bk = tps[tcount % 8]
                        tcount += 1
                        bkb = bk[:].bitcast(BF16)
                        nq = min(8, NBK - q0)
                        for q in range(nq):
                            self.P.op("pe", lambda e, q=q, q0=q0, bkb=bkb, k=k: e.transpose(
                                bkb[:, q * 128:(q + 1) * 128], ob[k][:, (q0 + q) * 128:(q0 + q + 1) * 128], self.ident_b),
                                r=[ob[k], self.cb16], w=[bk], inc=(q == nq - 1))
                        dst = ac[:].rearrange("p (blk t) -> p blk t", t=512)[:, q0:q0 + nq, s * 128:(s + 1) * 128]
                        self.cp("act", dst, bkb[:, 0:nq * 128].rearrange("p (blk t) -> p blk t", t=128), r=[bk], w=[ac])
                self.store(self.ssdT[:, t0:t0 + 512].rearrange("(blk p) t -> p blk t", p=128),
                           ac[:].rearrange("p (blk t) -> p blk t", t=512), r=[ac])

    def phase5a(self, l, last):
        from contextlib import ExitStack
        c = self.c
        N = 512
        AC = c.AW // 128
        SC = c.DI // 128
        DC = c.DC
        self._dense_rr = 0
        self._ws_rr = 0
        with ExitStack() as st:
            at = self.sb(st, [128, AC * N], BF16, "p5at")
            sd = self.sb(st, [128, SC * N], BF16, "p5sd")
            mT = self.sb(st, [128, DC * N], BF16, "p5m")
            wslots = [self.sb(st, [128, 16 * 512], BF16, "p5ws%d" % i) for i in range(3)]
            gA = [self.sb(st, [128, 4 * N], BF16, "p5gA%d" % i) for i in range(2)]
            gB = [self.sb(st, [128, 4 * N], BF16, "p5gB%d" % i) for i in range(2)]
            tA = [self.sb(st, [128, 4 * N], F32, "p5tA%d" % i) for i in range(2)]
            tB = [self.sb(st, [128, N], F32, "p5tB%d" % i) for i in range(2)]
            xb = [self.sb(st, [128, 4 * N], F32, "p5xb%d" % i) for i in range(2)]
            banksA = self.banks[0:4]
            banksB = self.banks[4:8]
            cnt = {"g": 0, "t": 0, "x": 0}
            for (t0, kind, b, s0) in self.tiles512():
                is_ctx = kind == "ctx"
                if last and is_ctx:
                    continue
                m = c.BPC if is_ctx else b
                self.load(at[:].rearrange("p (k n) -> p k n", n=N),
                          self.attT.ap().rearrange("(k p) t -> p k t", p=128)[:, :, t0:t0 + N], w=[at])
                self.load(sd[:].rearrange("p (k n) -> p k n", n=N),
                          self.ssdT.ap().rearrange("(k p) t -> p k t", p=128)[:, :, t0:t0 + N], w=[sd])
                for cg in range(0, DC, 4):
                    ncb = min(4, DC - cg)
                    gi = cnt["g"] % 2
                    cnt["g"] += 1
                    self.load(gA[gi][:, 0:ncb * N].rearrange("p (j n) -> p j n", n=N),
                              self.gT[cg * 128:(cg + ncb) * 128, t0:t0 + N].rearrange("(j p) t -> p j t", p=128), w=[gA[gi]])
                    self.load(gB[gi][:, 0:ncb * N].rearrange("p (j n) -> p j n", n=N),
                              self.gT[c.D + cg * 128: c.D + (cg + ncb) * 128, t0:t0 + N].rearrange("(j p) t -> p j t", p=128), w=[gB[gi]])

                    def epiA(cbi, bk, gi=gi):
                        self.tt("dve", tA[gi][:, cbi * N:(cbi + 1) * N], bk[:, 0:N], gA[gi][:, cbi * N:(cbi + 1) * N], ALU.mult,
                                r=[bk, gA[gi]], w=[tA[gi]])

                    def epiB(cbi, bk, gi=gi, cg=cg):
                        i = cnt["t"] % 2
                        cnt["t"] += 1
                        self.tt("dve", tB[i][:], bk[:, 0:N], gB[gi][:, cbi * N:(cbi + 1) * N], ALU.mult, r=[bk, gB[gi]], w=[tB[i]])
                        self.tt("pool", mT[:, (cg + cbi) * N:(cg + cbi + 1) * N], tB[i][:], tA[gi][:, cbi * N:(cbi + 1) * N], ALU.add,
                                r=[tB[i], tA[gi]], w=[mT])
                    self._dense_rr = 0
                    self.dense(self.wb_oa, c.AW, cg * 128, ncb * 128, at, lambda kc: at[:, kc * N:(kc + 1) * N], N, banksA, epiA, wslots)
                    self._dense_rr = 0
                    self.dense(self.wb_os, c.DI, cg * 128, ncb * 128, sd, lambda kc: sd[:, kc * N:(kc + 1) * N], N, banksB, epiB, wslots)
                for cg in range(0, DC, 4):
                    ncb = min(4, DC - cg)
                    xi = cnt["x"] % 2
                    cnt["x"] += 1
                    xv = self.xT[cg * 128:(cg + ncb) * 128, t0:t0 + N].rearrange("(j p) t -> p j t", p=128)
                    self.load(xb[xi][:, 0:ncb * N].rearrange("p (j n) -> p j n", n=N), xv, w=[xb[xi]])

                    def epiO(cbi, bk, xi=xi, cg=cg):
                        self.stt("dve", xb[xi][:, cbi * N:(cbi + 1) * N], bk[:, 0:N], self.mod_ap(l, 2, cg + cbi, m),
                                 xb[xi][:, cbi * N:(cbi + 1) * N], ALU.mult, ALU.add, r=[bk, self.mod, xb[xi]], w=[xb[xi]])
                    self.dense(self.wb_out, c.D, cg * 128, ncb * 128, mT, lambda kc: mT[:, kc * N:(kc + 1) * N], N, self.banks[0:8], epiO, wslots)
                    self.store(xv, xb[xi][:, 0:ncb * N].rearrange("p (j n) -> p j n", n=N), r=[xb[xi]])

    def phase5b(self, l, last):
        from contextlib import ExitStack
        c = self.c
        N = 512
        DC = c.DC
        FC = c.DFF // 128
        self._dense_rr = 0
        self._ws_rr = 0
        with ExitStack() as st:
            self.alloc_norm_tmps(st)
            xt = self.sb(st, [128, DC * N], F32, "p6x")
            h2 = self.sb(st, [128, DC * N], BF16, "p6h")
            f1 = self.sb(st, [128, FC * N], BF16, "p6f")
            wslots = [self.sb(st, [128, 16 * 512], BF16, "p6ws%d" % i) for i in range(3)]
            rl = [self.sb(st, [128, N], F32, "p6r%d" % i) for i in range(2)]
            cnt = {"r": 0}
            dbanks = self.banks[0:6]
            for (t0, kind, b, s0) in self.tiles512():
                is_ctx = kind == "ctx"
                if last and is_ctx:
                    continue
                m = c.BPC if is_ctx else b
                xsrc = self.xT.ap().rearrange("(dc p) t -> p dc t", p=128)[:, :, t0:t0 + N]
                self.load(xt[:].rearrange("p (dc n) -> p dc n", n=N), xsrc, w=[xt])
                self.norm_tile(st, xt, l, self.s2, lambda dc: self.mod_ap(l, 3, dc, m), m, h2)

                def epi1(cbi, bk):
                    i = cnt["r"] % 2
                    cnt["r"] += 1
                    self.act(rl[i][:], bk[:, 0:N], AF.Relu, r=[bk], w=[rl[i]])
                    self.tt("pool", f1[:, cbi * N:(cbi + 1) * N], rl[i][:], rl[i][:], ALU.mult, r=[rl[i]], w=[f1])
                self.dense(self.wb_ff1, c.D, 0, c.DFF, h2, lambda kc: h2[:, kc * N:(kc + 1) * N], N, dbanks, epi1, wslots)

                def epi2(cbi, bk):
                    self.stt("dve", xt[:, cbi * N:(cbi + 1) * N], bk[:, 0:N], self.mod_ap(l, 5, cbi, m),
                             xt[:, cbi * N:(cbi + 1) * N], ALU.mult, ALU.add, r=[bk, self.mod, xt], w=[xt])
                self.dense(self.wb_ff2, c.DFF, 0, c.D, f1, lambda kc: f1[:, kc * N:(kc + 1) * N], N, dbanks, epi2, wslots)
                if not last:
                    self.store(xsrc, xt[:].rearrange("p (dc n) -> p dc n", n=N), r=[xt])
                else:
                    gf = self.sb(st, [128, DC * c.NM], F32, "gfin%d" % t0)
                    g = self.gnt[:, 2 * c.DEPTH * DC:(2 * c.DEPTH + 1) * DC]
                    self.cp("dve", gf[:].rearrange("p (a b) -> p a b", b=c.NM), bc_ap(g, [[1, DC], [0, c.NM]]), r=[self.gnt], w=[gf])
                    of = self.sb(st, [128, DC * N], F32, "ofin%d" % t0) if False else f1
                    ofv = f1[:, 0:2 * DC * N].bitcast(F32)
                    DCn = DC
                    bk = self.banks[6]
                    for dc in range(DCn):
                        sq = self.sqs[dc % 2]
                        self.act(sq[:], xt[:, dc * N:(dc + 1) * N], AF.Square, r=[xt], w=[sq])
                        self.mm(bk[:, 0:N], self.ones_f, sq[:], dc == 0, dc == DCn - 1, r=[sq, self.cf], w=[bk], inc=True)
                    rstd = self.rstd
                    self.rsqrt(rstd[:], bk[:, 0:N], 1.0 / c.D, [bk], rstd)
                    for dc in range(DCn):
                        self.stt("dve", ofv[:, dc * N:(dc + 1) * N], xt[:, dc * N:(dc + 1) * N], g[:, dc:dc + 1], rstd[:],
                                 ALU.mult, ALU.mult, r=[xt, self.gnt, rstd, f1], w=[f1])
                    lo = b * c.S + s0
                    self.store(self.outT.ap().rearrange("(dc p) t -> p dc t", p=128)[:, :, lo:lo + N],
                               ofv[:, 0:DC * N].rearrange("p (dc n) -> p dc n", n=N), r=[f1])


def host_consts():
    r = np.arange(128)[:, None]
    cc = np.arange(128)[None, :]
    ident = (r == cc).astype(np.float32)
    U0 = (r <= cc).astype(np.float32)
    U1 = (r >= cc).astype(np.float32)
    SL = (r > cc).astype(np.float32)
    SU = (r < cc).astype(np.float32)
    ones = np.ones((128, 128), np.float32)
    R = np.zeros((128, 128), np.float32)
    for m in range(128):
        if m % 64 < 32:
            R[m + 32, m] = -1.0
        else:
            R[m - 32, m] = 1.0
    return np.concatenate([ident, U0, U1, SL, SU, ones, R], axis=1)


def host_rope(S, grid_w=64, theta=10000.0):
    t = np.arange(S)
    row = (t // grid_w).astype(np.float32)
    col = (t % grid_w).astype(np.float32)
    axis_dim = 64
    inv = (np.float32(theta) ** (-np.arange(0, axis_dim, 2, dtype=np.float32) / np.float32(axis_dim))).astype(np.float32)
    ang_r = (row[:, None] * inv[None]).astype(np.float32)
    ang_c = (col[:, None] * inv[None]).astype(np.float32)
    cos = np.zeros((128, S), np.float32)
    sin = np.zeros((128, S), np.float32)
    for d in range(128):
        a = ang_r if d < 64 else ang_c
        cos[d] = np.cos(a[:, d % 32])
        sin[d] = np.sin(a[:, d % 32])
    return np.concatenate([cos, sin], axis=1)


def fm(v, nchunk):
    v = np.asarray(v, np.float32)
    lead = v.shape[:-1]
    v = v.reshape(lead + (nchunk, 128))
    return np.moveaxis(v, -1, 0)


def prep_inputs(cfg, inp):
    c = cfg
    f32 = lambda a: np.ascontiguousarray(np.asarray(a, np.float32))
    x = f32(inp["x"])
    ctx = f32(inp["ctx"])
    cc = f32(inp["c"])
    c_ctx = f32(inp["c_ctx"])
    shared = {
        "w_ada": f32(inp["w_ada"]).reshape(c.DEPTH * c.D, 6 * c.D),
        "w_in": f32(inp["w_in"]).reshape(c.DEPTH * c.D, c.IN),
        "w_oa": f32(inp["w_o_attn"]).reshape(c.DEPTH * c.AW, c.D),
        "w_os": f32(inp["w_o_ssd"]).reshape(c.DEPTH * c.DI, c.D),
        "w_out": f32(inp["w_out"]).reshape(c.DEPTH * c.D, c.D),
        "w_ff1": f32(inp["w_ff1"]).reshape(c.DEPTH * c.D, c.DFF),
        "w_ff2": f32(inp["w_ff2"]).reshape(c.DEPTH * c.DFF, c.D),
    }
    shared["b_ada"] = f32(fm(inp["b_ada"], 6 * c.DC).reshape(128, -1))
    gn = np.concatenate([f32(inp["g_norm1"]), f32(inp["g_norm2"]), f32(inp["g_final"])[None]], axis=0)
    shared["gn"] = f32(fm(gn, c.DC).reshape(128, -1))
    cw = f32(inp["conv_w"])
    cb = f32(inp["conv_b"])
    cv = np.concatenate([cw, cb[:, None, :]], axis=1)
    cv = cv.reshape(c.DEPTH, 6, c.CB, 128)
    shared["convp"] = f32(np.transpose(cv, (3, 0, 2, 1)).reshape(128, -1))
    rows = np.concatenate([f32(inp["attn_sink"]), f32(inp["dt_bias"]).reshape(c.DEPTH, -1),
                           f32(inp["a_log"]).reshape(c.DEPTH, -1), f32(inp["d_skip"]).reshape(c.DEPTH, -1),
                           f32(inp["g_ssd"])], axis=1)
    shared["rows"] = f32(rows)
    shared["consts"] = host_consts()
    shared["rope"] = host_rope(c.S, c.GRID_W)
    in_maps = []
    for core in range(c.NCORES):
        bs = [core * c.BPC + i for i in range(c.BPC)]
        xin = np.concatenate([ctx[b].T for b in bs] + [x[b].T for b in bs], axis=1)
        cvec = np.stack([cc[b] for b in bs] + [c_ctx], axis=0)
        cin = np.transpose(cvec.reshape(c.NM, c.DC, 128), (2, 1, 0)).reshape(128, -1)
        m = dict(shared)
        m["xin"] = f32(xin)
        m["cin"] = f32(cin)
        in_maps.append(m)
    return in_maps


def build_nc(cfg):
    from contextlib import ExitStack
    nc = bass.Bass("TRN2", target_bir_lowering=False)
    with ExitStack() as stack:
        b = Builder(cfg, nc, stack)
        b.build()
    return nc


LAST_EXEC_NS = [None]


def run_cfg(cfg, inp, trace=False):
    in_maps = prep_inputs(cfg, inp)
    nc = build_nc(cfg)
    if trace:
        res = run_bass_kernel_spmd(nc, in_maps, core_ids=list(range(cfg.NCORES)), trace=True)
        LAST_EXEC_NS[0] = res.exec_time_ns
    else:
        res = run_bass_kernel_spmd(nc, in_maps, core_ids=list(range(cfg.NCORES)))
    outs = []
    for core in range(cfg.NCORES):
        oT = res.results[core]["outT"]
        o = oT.T.reshape(cfg.BPC, cfg.S, cfg.D)
        outs.append(o)
    return np.ascontiguousarray(np.concatenate(outs, axis=0).astype(np.float32))


def kernel(**inputs):
    cfg = Cfg()
    return run_cfg(cfg, inputs)
```

```python
import math
import numpy as np
import concourse.bass as bass
import concourse.mybir as mybir
from concourse.bass_utils import run_bass_kernel_spmd

F32 = mybir.dt.float32
BF16 = mybir.dt.bfloat16
AF = mybir.ActivationFunctionType
ALU = mybir.AluOpType
AX = mybir.AxisListType

SAME_ENGINE_SYNC = True
DEBUG_STOP = None
DEBUG_SUB = None
DEBUG_NORM = None
DEBUG_TILES = None
DEBUG_ROPE = None
POOL_ENG = "dve"
EPS = 1e-6


class Cfg:
    def __init__(s, D=2048, B=16, S=2048, DEPTH=4, L=256, NH=16, NKV=4, G=8, NCORES=8):
        s.D, s.B, s.S, s.DEPTH, s.L, s.NH, s.NKV, s.G, s.NCORES = D, B, S, DEPTH, L, NH, NKV, G, NCORES
        s.HD = 128
        s.AW = NH * 128
        s.KW = NKV * 128
        s.REP = NH // NKV
        s.DI = 2 * D
        s.P = 64
        s.H = s.DI // 64
        s.E = s.H // G
        s.N = 128
        s.K = 5
        s.GN = G * 128
        s.CC = s.DI + 2 * s.GN
        s.DFF = 4 * D
        s.GRID_W = 64
        s.COL_K = 0
        s.COL_V = s.KW
        s.COL_XBC = 2 * s.KW
        s.COL_DT = s.COL_XBC + s.CC
        s.COL_Q = s.COL_DT + 2 * s.H
        s.COL_Z = s.COL_Q + s.AW
        s.COL_GATE = s.COL_Z + s.DI
        s.IN = s.COL_GATE + 2 * D
        s.BPC = B // NCORES
        s.T = s.BPC * (L + S)
        s.DC = D // 128
        s.NM = s.BPC + 1
        s.CB = s.CC // 128
        assert (s.BPC * L) % 512 == 0 and S % 512 == 0
        assert s.E * 64 == 512

    def ctx_off(s, b):
        return b * s.L

    def lat_off(s, b):
        return s.BPC * s.L + b * s.S


class TT:
    def __init__(self, t, name):
        self.t = t
        self.name = name
        self.w = {}
        self.r = {}
        self.excl = False

    def __getitem__(self, k):
        return self.t[k]


ENGS = ["pe", "act", "dve", "pool", "sp"]
NDMA = 16


class Prog:
    def __init__(self, nc, stack):
        self.nc = nc
        self.stack = stack
        self.ops = {e: [] for e in ENGS}
        self.sems = []
        self.prog_sem = {}
        self.seq = {}
        for e in ["pe", "act", "dve", "pool"]:
            self.prog_sem[e] = self._new_sem("prog_" + e)
            self.seq[e] = 0
        self.waited = {e: {} for e in ENGS}
        self.dma_pool = {}
        self.dma_rr = {}
        self.sem_val = {}
        for q in ["sp", "pool", "act"]:
            self.dma_pool[q] = [self._new_sem("dma_%s_%d" % (q, i)) for i in range(NDMA)]
            self.dma_rr[q] = 0
            for sidx in self.dma_pool[q]:
                self.sem_val[sidx] = 0
        self.tiles = []
        self.nops = 0

    def _new_sem(self, name):
        h = self.stack.enter_context(self.nc.semaphore(name))
        self.sems.append(h)
        return len(self.sems) - 1

    def track(self, t, name):
        tt = TT(t, name)
        self.tiles.append(tt)
        return tt

    def _deps(self, r, w):
        evs = {}
        for t in r:
            for k, v in t.w.items():
                if evs.get(k, 0) < v:
                    evs[k] = v
        for t in w:
            for k, v in t.w.items():
                if evs.get(k, 0) < v:
                    evs[k] = v
            for k, v in t.r.items():
                if evs.get(k, 0) < v:
                    evs[k] = v
        return evs

    def _emit_waits(self, eng, evs):
        wd = self.waited[eng]
        for k, v in evs.items():
            if wd.get(k, 0) < v:
                wd[k] = v
                self.ops[eng].append(("wait", k, v))

    def op(self, eng, fn, r=(), w=(), inc=True):
        w = list(w) + [t for t in r if t.excl and t not in w]
        r = [t for t in r if not t.excl]
        evs = self._deps(r, w)
        own = self.prog_sem[eng]
        if not SAME_ENGINE_SYNC or eng == "pe":
            evs.pop(own, None)
        else:
            raw = 0
            for t in r:
                v = t.w.get(own, 0)
                if v > raw:
                    raw = v
            if raw > 0:
                evs[own] = raw
            else:
                evs.pop(own, None)
        self._emit_waits(eng, evs)
        if inc:
            self.seq[eng] += 1
            ev = self.seq[eng]
        else:
            ev = self.seq[eng] + 1
        self.ops[eng].append(("op", fn, own if inc else None))
        self.nops += 1
        for t in r:
            if t.r.get(own, 0) < ev:
                t.r[own] = ev
        for t in w:
            t.w = {own: ev}
            t.r = {}

    def dma(self, q, out_ap, in_ap, r=(), w=(), slow=False):
        i = self.dma_rr[q]
        self.dma_rr[q] = (i + 1) % NDMA
        sem = self.dma_pool[q][i]
        prev = self.sem_val[sem]
        evs = self._deps(r, w)
        if prev > 0 and evs.get(sem, 0) < prev:
            evs[sem] = prev
        self._emit_waits(q, evs)
        val = prev + 16
        self.sem_val[sem] = val
        self.ops[q].append(("dma", out_ap, in_ap, sem, slow))
        self.nops += 1
        for t in r:
            if t.r.get(sem, 0) < val:
                t.r[sem] = val
        for t in w:
            t.w = {sem: val}
            t.r = {}

    def barrier(self):
        evs = {}
        for e, sidx in self.prog_sem.items():
            if self.seq[e] > 0:
                evs[sidx] = self.seq[e]
        for sidx, v in self.sem_val.items():
            if v > 0:
                evs[sidx] = v
        for e in ENGS:
            ev2 = dict(evs)
            if e in self.prog_sem:
                ev2.pop(self.prog_sem[e], None)
            self._emit_waits(e, ev2)
        for t in self.tiles:
            t.w = {}
            t.r = {}

    def emit(self):
        nc = self.nc
        sems = self.sems

        def replay(eng_name):
            def body(e):
                for o in self.ops[eng_name]:
                    if o[0] == "wait":
                        e.wait_ge(sems[o[1]], o[2])
                    elif o[0] == "op":
                        ins = o[1](e)
                        if o[2] is not None:
                            ins.then_inc(sems[o[2]], 1)
                    else:
                        if o[4]:
                            e.dma_start(out=o[1], in_=o[2], allow_slow_non_contiguous=True).then_inc(sems[o[3]], 16)
                        else:
                            e.dma_start(out=o[1], in_=o[2]).then_inc(sems[o[3]], 16)
            return body

        with nc.Block() as block:
            block.tensor(replay("pe"))
            block.scalar(replay("act"))
            block.vector(replay("dve"))
            block.gpsimd(replay("pool"))
            block.sync(replay("sp"))


def bc_ap(ap, dims):
    a = ap.ap
    return bass.AP(ap.tensor, ap.offset, [list(a[0])] + [list(d) for d in dims])


class Builder:
    def __init__(self, cfg, nc, stack):
        self.c = cfg
        self.nc = nc
        self.stack = stack
        self.P = Prog(nc, stack)
        self._uid = 0
        c = cfg
        dt_in = lambda name, shape: nc.dram_tensor(name, shape, F32, kind="ExternalInput")
        self.xin = dt_in("xin", [c.D, c.T])
        self.cin = dt_in("cin", [128, c.DC * c.NM])
        self.w_ada = dt_in("w_ada", [c.DEPTH * c.D, 6 * c.D])
        self.w_in = dt_in("w_in", [c.DEPTH * c.D, c.IN])
        self.w_oa = dt_in("w_oa", [c.DEPTH * c.AW, c.D])
        self.w_os = dt_in("w_os", [c.DEPTH * c.DI, c.D])
        self.w_out = dt_in("w_out", [c.DEPTH * c.D, c.D])
        self.w_ff1 = dt_in("w_ff1", [c.DEPTH * c.D, c.DFF])
        self.w_ff2 = dt_in("w_ff2", [c.DEPTH * c.DFF, c.D])
        self.b_ada = dt_in("b_ada", [128, c.DEPTH * 6 * c.DC])
        self.gn = dt_in("gn", [128, (2 * c.DEPTH + 1) * c.DC])
        self.convp = dt_in("convp", [128, c.DEPTH * c.CB * 6])
        self.rows = dt_in("rows", [c.DEPTH, c.NH + 6 * c.H + c.DI])
        self.consts = dt_in("consts", [128, 7 * 128])
        self.rope = dt_in("rope", [128, 2 * c.S])
        self.outT = nc.dram_tensor("outT", [c.D, c.BPC * c.S], F32, kind="ExternalOutput")
        sc = lambda name, shape, dt: nc.dram_tensor(name, shape, dt)
        self.wbs = []
        for i_ in range(2):
            self.wbs.append({
                "in": sc("wb_in%d" % i_, [c.D, c.IN], BF16), "oa": sc("wb_oa%d" % i_, [c.AW, c.D], BF16),
                "os": sc("wb_os%d" % i_, [c.DI, c.D], BF16), "out": sc("wb_out%d" % i_, [c.D, c.D], BF16),
                "ff1": sc("wb_ff1%d" % i_, [c.D, c.DFF], BF16), "ff2": sc("wb_ff2%d" % i_, [c.DFF, c.D], BF16)})
        self.xT = sc("xT", [c.D, c.T], F32)
        self.qT = sc("qT", [c.AW, c.T], BF16)
        self.kT = sc("kT", [c.KW, c.T], BF16)
        self.vtm = sc("vtm", [c.T, c.KW], BF16)
        self.xbcT = sc("xbcT", [c.CC, c.T], BF16)
        self.dtv = sc("dtv", [c.T, 2 * c.H], F32)
        self.sz = sc("sz", [c.T, c.DI], BF16)
        self.gT = sc("gT", [2 * c.D, c.T], BF16)
        self.xs = sc("xs", [c.T, c.DI], BF16)
        self.Btm = sc("Btm", [c.T, c.GN], BF16)
        self.BT = sc("BT", [c.GN, c.T], BF16)
        self.CT = sc("CT", [c.GN, c.T], BF16)
        self.yd = [sc("yf", [c.T, c.DI], F32), sc("yb", [c.T, c.DI], F32)]
        self.attT = sc("attT", [c.AW, c.T], BF16)
        self.ssdT = sc("ssdT", [c.DI, c.T], BF16)
        self.banks = [self.P.track(nc.alloc_psum_tensor("bank%d" % i, [128, 512], F32), "bank%d" % i) for i in range(8)]
        self.bank_rr = 0
        for bk_ in self.banks:
            bk_.excl = True

    def sb(self, stack, shape, dt, name):
        self._uid += 1
        t = stack.enter_context(self.nc.sbuf_tensor("%s_%d" % (name, self._uid), shape, dt))
        return self.P.track(t, name)

    def mm(self, ps_ap, lhsT, rhs, start, stop, r, w, inc=None):
        if inc is None:
            inc = stop
        self.P.op("pe", lambda e: e.matmul(ps_ap, lhsT=lhsT, rhs=rhs, start=start, stop=stop), r=r, w=w, inc=inc)

    def act(self, out, in_, func, r, w, bias=None, scale=None, accum=None):
        kw = {}
        if bias is not None:
            kw["bias"] = bias
        if scale is not None:
            kw["scale"] = scale
        if accum is not None:
            kw["accum_out"] = accum
        self.P.op("act", lambda e: e.activation(out=out, in_=in_, func=func, **kw), r=r, w=w)

    def tt(self, eng, out, in0, in1, op, r, w):
        self.P.op(eng, lambda e: e.tensor_tensor(out=out, in0=in0, in1=in1, op=op), r=r, w=w)

    def ts(self, eng, out, in0, s1, s2, op0, op1, r, w):
        if s2 is None:
            self.P.op(eng, lambda e: e.tensor_single_scalar(out=out, in_=in0, scalar=s1, op=op0), r=r, w=w)
        else:
            self.P.op(eng, lambda e: e.tensor_scalar(out=out, in0=in0, scalar1=s1, scalar2=s2, op0=op0, op1=op1), r=r, w=w)

    def stt(self, eng, out, in0, scalar, in1, op0, op1, r, w):
        self.P.op(eng, lambda e: e.scalar_tensor_tensor(out=out, in0=in0, scalar=scalar, in1=in1, op0=op0, op1=op1), r=r, w=w)

    def cp(self, eng, out, in_, r, w):
        if eng == "act":
            self.P.op("act", lambda e: e.copy(out=out, in_=in_), r=r, w=w)
        else:
            self.P.op(eng, lambda e: e.tensor_copy(out=out, in_=in_), r=r, w=w)

    def rsqrt(self, out, in_, inv_n, r, wt):
        self.act(out, in_, AF.Sqrt, bias=self.eps_c[:, 0:1], scale=inv_n, r=list(r) + [self.eps_c], w=[wt])
        self.P.op("dve", lambda e: e.reciprocal(out=out, in_=out), r=[wt], w=[wt])

    def memset(self, eng, ap, val, w):
        self.P.op(eng, lambda e: e.memset(ap, val), r=(), w=w)

    def load(self, out_ap, in_ap, w, q="sp", slow=False):
        self.P.dma(q, out_ap, in_ap, r=(), w=w, slow=slow)

    def store(self, out_ap, in_ap, r, q="pool", slow=False):
        self.P.dma(q, out_ap, in_ap, r=r, w=(), slow=slow)

    def build(self):
        from contextlib import ExitStack
        c = self.c
        P = self.P
        nc = self.nc
        with ExitStack() as gs:
            self.gs = gs
            self.setup_consts(gs)
            self.phase_mod(gs)
            step = max(128, (1 << 22) // (c.T * 4) // 128 * 128)
            for r0 in range(0, c.D, step):
                r1 = min(c.D, r0 + step)
                P.dma("sp", self.xT[r0:r1, :], self.xin[r0:r1, :])
            P.barrier()
            steps = []
            for l in range(c.DEPTH):
                last = l == c.DEPTH - 1
                nxt = self.cast_list(l + 1) if not last else []
                npart = 6
                per = (len(nxt) + npart - 1) // npart if nxt else 0
                parts = [nxt[i_ * per:(i_ + 1) * per] for i_ in range(npart)] if nxt else [[] for _ in range(npart)]
                if l == 0:
                    steps.append(lambda l=l: (self.issue_casts(self.cast_list(0)), self.layer_consts(l)))
                else:
                    steps.append(lambda l=l: self.layer_consts(l))
                steps.append(lambda l=l, last=last, p=parts[0]: (self.issue_casts(p), self.phase1(l, last)))
                steps.append(lambda l=l, last=last, p=parts[1]: (self.issue_casts(p), self.phase_attn(l, last)))
                steps.append(lambda l=l, last=last, p=parts[2]: (self.issue_casts(p), self.phase_conv(l)))
                steps.append(lambda l=l, last=last, p=parts[3]: (self.issue_casts(p), self.phase_ssd(l, last)))
                steps.append(lambda l=l, last=last, p=parts[4]: (self.issue_casts(p), self.phase4(l, last)))
                steps.append(lambda l=l, last=last, p=parts[5]: (self.issue_casts(p), self.phase5a(l, last)))
                steps.append(lambda l=l, last=last: self.phase5b(l, last))
            for i, f in enumerate(steps):
                if DEBUG_STOP is not None and i >= DEBUG_STOP:
                    break
                f()
                P.barrier()
            P.emit()

    def setup_consts(self, gs):
        c = self.c
        cf = self.sb(gs, [128, 7 * 128], F32, "constf")
        self.load(cf[:], self.consts[:, :], w=[cf])
        self.cf = cf
        cbf = self.sb(gs, [128, 7 * 128], BF16, "constb")
        self.cp("dve", cbf[:], cf[:], r=[cf], w=[cbf])
        self.cb16 = cbf
        k = lambda t, i: t[:, i * 128:(i + 1) * 128]
        self.ident_b = k(cbf, 0)
        self.U_f = [k(cf, 1), k(cf, 2)]
        self.U_b = [k(cbf, 1), k(cbf, 2)]
        self.A_f = [k(cf, 3), k(cf, 4)]
        self.ones_f = k(cf, 5)
        self.ones_b = k(cbf, 5)
        self.R_b = k(cbf, 6)
        self.mod = self.sb(gs, [128, c.DEPTH * 6 * c.DC * c.NM], F32, "mod")
        self.s1 = self.sb(gs, [128, c.DEPTH * c.DC * c.NM], F32, "s1")
        self.s2 = self.sb(gs, [128, c.DEPTH * c.DC * c.NM], F32, "s2")
        self.gnt = self.sb(gs, [128, (2 * c.DEPTH + 1) * c.DC], F32, "gnt")
        self.load(self.gnt[:], self.gn[:, :], w=[self.gnt])
        self.zero_c = self.sb(gs, [128, 1], F32, "zeroc")
        self.memset("dve", self.zero_c[:], 0.0, w=[self.zero_c])
        self.eps_c = self.sb(gs, [128, 1], F32, "epsc")
        self.memset("dve", self.eps_c[:], EPS, w=[self.eps_c])
        self.one_c = self.sb(gs, [128, 1], F32, "onec")
        self.memset("dve", self.one_c[:], 1.0, w=[self.one_c])

    def mod_ap(self, l, j, dc, m):
        c = self.c
        o = ((l * 6 + j) * c.DC + dc) * c.NM + m
        return self.mod[:, o:o + 1]

    def phase_mod(self, gs):
        from contextlib import ExitStack
        c = self.c
        P = self.P
        with ExitStack() as st:
            ct = self.sb(st, [128, c.DC * c.NM], F32, "ct")
            sct = self.sb(st, [128, c.DC * c.NM], F32, "sct")
            bt = self.sb(st, [128, c.DEPTH * 6 * c.DC], F32, "bt")
            self.load(ct[:], self.cin[:, :], w=[ct])
            self.load(bt[:], self.b_ada[:, :], w=[bt])
            self.act(sct[:], ct[:], AF.Silu, r=[ct], w=[sct])
            KC = c.DC
            ws = [self.sb(st, [128, KC * 512], F32, "wada%d" % i) for i in range(2)]
            wsb = [self.sb(st, [128, KC * 512], BF16, "wadab%d" % i) for i in range(2)]
            sctb = self.sb(st, [128, c.DC * c.NM], BF16, "sctb")
            self.cp("dve", sctb[:], sct[:], r=[sct], w=[sctb])
            wi = 0
            ncg = (6 * c.D) // 512
            half = (KC * 512) // 2
            for l in range(c.DEPTH):
                for cg in range(ncg):
                    wt = ws[wi % 2]
                    wtb = wsb[wi % 2]
                    wi += 1
                    src = self.w_ada[l * c.D:(l + 1) * c.D, cg * 512:(cg + 1) * 512].rearrange("(kc p) n -> p kc n", p=128)
                    self.load(wt[:].rearrange("p (kc n) -> p kc n", n=512), src, w=[wt])
                    self.cp("dve", wtb[:, 0:half], wt[:, 0:half], r=[wt], w=[wtb])
                    self.cp("act", wtb[:, half:], wt[:, half:], r=[wt], w=[wtb])
                    for cb in range(4):
                        bk = self.banks[self.bank_rr % 8]
                        self.bank_rr += 1
                        for kc in range(KC):
                            self.mm(bk[:, 0:c.NM], wtb[:, kc * 512 + cb * 128: kc * 512 + (cb + 1) * 128],
                                    sctb[:, kc * c.NM:(kc + 1) * c.NM], kc == 0, kc == KC - 1,
                                    r=[wtb, sctb], w=[bk])
                        t = cg * 4 + cb
                        o = (l * 6 * c.DC + t) * c.NM
                        self.act(self.mod[:, o:o + c.NM], bk[:, 0:c.NM], AF.Identity,
                                 bias=bt[:, l * 6 * c.DC + t: l * 6 * c.DC + t + 1], r=[bk, bt], w=[self.mod])
            for l in range(c.DEPTH):
                for (dst, j, gi) in ((self.s1, 1, l), (self.s2, 4, c.DEPTH + l)):
                    o = (l * 6 + j) * c.DC * c.NM
                    od = l * c.DC * c.NM
                    n = c.DC * c.NM
                    self.ts("dve", dst[:, od:od + n], self.mod[:, o:o + n], 1.0, None, ALU.add, None,
                            r=[self.mod], w=[dst])
                    g = self.gnt[:, gi * c.DC:(gi + 1) * c.DC]
                    gb = bc_ap(g, [[1, c.DC], [0, c.NM]])
                    d3 = dst[:, od:od + n].rearrange("p (a b) -> p a b", b=c.NM)
                    self.tt("dve", d3, d3, gb, ALU.mult, r=[dst, self.gnt], w=[dst])
            P.barrier()

    def cast_list(self, l):
        c = self.c
        wb = self.wbs[l % 2]
        out = []
        for (src, dst, R, C) in ((self.w_in, wb["in"], c.D, c.IN), (self.w_oa, wb["oa"], c.AW, c.D),
                                 (self.w_os, wb["os"], c.DI, c.D), (self.w_out, wb["out"], c.D, c.D),
                                 (self.w_ff1, wb["ff1"], c.D, c.DFF), (self.w_ff2, wb["ff2"], c.DFF, c.D)):
            step = max(128, ((1 << 21) // C) // 128 * 128)
            for r0 in range(0, R, step):
                r1 = min(R, r0 + step)
                out.append((dst[r0:r1, :], src[l * R + r0: l * R + r1, :]))
        return out

    def issue_casts(self, lst):
        for (dst, src) in lst:
            self.P.dma("pool", dst, src)

    def layer_consts(self, l):
        c = self.c
        if l == 0:
            gs = self.gs
            self.sinkexp = self.sb(gs, [128, c.NH], F32, "sinkexp")
            self.dtb = self.sb(gs, [128, 2 * c.H], F32, "dtb")
            self.aneg = self.sb(gs, [128, 2 * c.H], F32, "aneg")
            self.dsk = self.sb(gs, [128, 2 * c.H], F32, "dsk")
            self.dsum = self.sb(gs, [128, c.H], F32, "dsum")
            self.cvp = self.sb(gs, [128, c.CB * 6], F32, "cvp")
        rows = self.rows
        H2 = 2 * c.H

        def brow(o, n):
            return bass.AP(rows, l * (c.NH + 6 * c.H + c.DI) + o, [[0, 128], [1, n]])
        self.load(self.sinkexp[:], brow(0, c.NH), w=[self.sinkexp])
        self.load(self.dtb[:], brow(c.NH, H2), w=[self.dtb])
        self.load(self.aneg[:], brow(c.NH + H2, H2), w=[self.aneg])
        self.load(self.dsk[:], brow(c.NH + 2 * H2, H2), w=[self.dsk])
        self.load(self.cvp[:], self.convp[:, l * c.CB * 6:(l + 1) * c.CB * 6], w=[self.cvp])
        self.act(self.sinkexp[:], self.sinkexp[:], AF.Exp, r=[self.sinkexp], w=[self.sinkexp])
        self.act(self.aneg[:], self.aneg[:], AF.Exp, r=[self.aneg], w=[self.aneg])
        self.ts("dve", self.aneg[:], self.aneg[:], -1.0, None, ALU.mult, None, r=[self.aneg], w=[self.aneg])
        self.tt("dve", self.dsum[:], self.dsk[:, 0:c.H], self.dsk[:, c.H:H2], ALU.add, r=[self.dsk], w=[self.dsum])
        self.gssd_off = l * (c.NH + 6 * c.H + c.DI) + c.NH + 3 * H2

    def tiles512(self):
        c = self.c
        out = []
        for t0 in range(0, c.BPC * c.L, 512):
            out.append((t0, "ctx", None, None))
        for b in range(c.BPC):
            for s0 in range(0, c.S, 512):
                out.append((c.lat_off(b) + s0, "lat", b, s0))
        return out

    def norm_tile(self, st, xt, l_idx, s_t, sh_fn, m, out_t, out_is_f32=False):
        c = self.c
        DC = c.DC
        N = 512
        bk = self.banks[6]
        if DEBUG_NORM == 0:
            return
        for dc in range(DC):
            sq = self.sqs[dc % 2]
            self.act(sq[:], xt[:, dc * N:(dc + 1) * N], AF.Square, r=[xt], w=[sq])
            if DEBUG_NORM == -1:
                continue
            self.mm(bk[:, 0:N], self.ones_f, sq[:], dc == 0, dc == DC - 1, r=[sq, self.cf], w=[bk], inc=True)
        rstd = self.rstd
        if DEBUG_NORM == 1 or DEBUG_NORM == -1:
            return
        if DEBUG_NORM == 2:
            self.act(rstd[:], bk[:, 0:N], AF.Sqrt, bias=self.eps_c[:, 0:1], scale=1.0 / c.D, r=[bk, self.eps_c], w=[rstd])
            return
        self.rsqrt(rstd[:], bk[:, 0:N], 1.0 / c.D, [bk], rstd)
        if DEBUG_NORM == 3:
            return
        for dc in range(DC):
            tmp = self.ntmp[dc % 2]
            o = (l_idx * DC + dc) * c.NM + m
            self.stt("dve", tmp[:], xt[:, dc * N:(dc + 1) * N], s_t[:, o:o + 1], rstd[:], ALU.mult, ALU.mult,
                     r=[xt, s_t, rstd], w=[tmp])
            if DEBUG_NORM == 4:
                continue
            sh = sh_fn(dc)
            self.act(out_t[:, dc * N:(dc + 1) * N], tmp[:], AF.Identity, bias=sh, r=[tmp, self.mod, self.zero_c], w=[out_t])

    def alloc_norm_tmps(self, st):
        self.sqs = [self.sb(st, [128, 512], F32, "sq%d" % i) for i in range(2)]
        self.ntmp = [self.sb(st, [128, 512], F32, "ntmp%d" % i) for i in range(2)]
        self.rstd = self.sb(st, [128, 512], F32, "rstd")

    def dense(self, wb, R, c0, ncols, act_t, act_kc_ap, N, banks, epilogue, wslots):
        KT = R // 128
        KC = min(16, KT)
        nkg = KT // KC
        bi = 0
        for g0 in range(0, ncols, 512):
            gw = min(512, ncols - g0)
            ncb = gw // 128
            bks = []
            for i in range(ncb):
                bks.append(banks[self._dense_rr % len(banks)])
                self._dense_rr += 1
            for kg in range(nkg):
                wt = wslots[self._ws_rr % len(wslots)]
                self._ws_rr += 1
                src = wb[kg * KC * 128:(kg + 1) * KC * 128, c0 + g0: c0 + g0 + gw].rearrange("(kc p) n -> p kc n", p=128)
                dst = wt[:, 0:KC * gw].rearrange("p (kc n) -> p kc n", n=gw)
                self.load(dst, src, w=[wt])
                for cb in range(ncb):
                    for kc in range(KC):
                        first = kg == 0 and kc == 0
                        lastk = kg == nkg - 1 and kc == KC - 1
                        self.mm(bks[cb][:, 0:N], wt[:, kc * gw + cb * 128: kc * gw + (cb + 1) * 128],
                                act_kc_ap(kg * KC + kc), first, lastk, r=[wt, act_t], w=[bks[cb]], inc=(kc == KC - 1))
            for cb in range(ncb):
                epilogue(g0 // 128 + cb, bks[cb])

    def dense_tm(self, wb, R, c0, ncols, act_t, N, banks, epilogue, wslots):
        KT = R // 128
        assert KT <= 16 and ncols <= 512
        wt = wslots[self._ws_rr % len(wslots)]
        self._ws_rr += 1
        src = wb[0:R, c0:c0 + ncols].rearrange("(kc p) n -> p kc n", p=128)
        dst = wt[:, 0:KT * ncols].rearrange("p (kc n) -> p kc n", n=ncols)
        self.load(dst, src, w=[wt])
        for s in range(N // 128):
            bk = banks[self._dense_rr % len(banks)]
            self._dense_rr += 1
            for kc in range(KT):
                self.mm(bk[:, 0:ncols], act_t[:, kc * N + s * 128: kc * N + (s + 1) * 128],
                        wt[:, kc * ncols:(kc + 1) * ncols], kc == 0, kc == KT - 1, r=[wt, act_t], w=[bk])
            epilogue(s, bk)

    def phase1(self, l, last):
        from contextlib import ExitStack
        c = self.c
        N = 512
        self._dense_rr = 0
        self._ws_rr = 0
        with ExitStack() as st:
            self.alloc_norm_tmps(st)
            xt = self.sb(st, [128, c.DC * N], F32, "xt")
            h = self.sb(st, [128, c.DC * N], BF16, "h")
            wslots = [self.sb(st, [128, min(16, c.DC) * 512], BF16, "ws%d" % i) for i in range(3)]
            fm = [self.sb(st, [128, 4 * N], BF16, "fm%d" % i) for i in range(3)]
            tm = [self.sb(st, [128, 512], BF16, "tm%d" % i) for i in range(3)]
            cos_t = self.sb(st, [128, N], F32, "cos")
            sin_t = self.sb(st, [128, N], F32, "sin")
            t1 = [self.sb(st, [128, N], F32, "t1_%d" % i) for i in range(2)]
            t2 = [self.sb(st, [128, N], F32, "t2_%d" % i) for i in range(2)]
            qb = [self.sb(st, [128, N], BF16, "qb%d" % i) for i in range(2)]
            dtx = [self.sb(st, [128, 2 * c.H], F32, "dtx%d" % i) for i in range(2)]
            dta = [self.sb(st, [128, 2 * c.H], F32, "dta%d" % i) for i in range(2)]
            dtl = [self.sb(st, [128, 2 * c.H], F32, "dtl%d" % i) for i in range(2)]
            dto = [self.sb(st, [128, 2 * c.H], F32, "dto%d" % i) for i in range(2)]
            dbanks = self.banks[0:6]
            rr = {"fm": 0, "tm": 0, "rope": 0, "dt": 0}
            wl = self.wbs[l % 2]["in"]
            for ti_, (t0, kind, b, s0) in enumerate(self.tiles512()):
                if DEBUG_TILES is not None and ti_ >= DEBUG_TILES:
                    break
                is_ctx = kind == "ctx"
                m = c.BPC if is_ctx else b
                skip_q = last and is_ctx
                self.load(xt[:].rearrange("p (dc n) -> p dc n", n=N),
                          self.xT.ap().rearrange("(dc p) t -> p dc t", p=128)[:, :, t0:t0 + N], w=[xt])
                self.norm_tile(st, xt, l, self.s1, lambda dc: self.mod_ap(l, 0, dc, m), m, h)
                if not is_ctx:
                    self.load(cos_t[:], self.rope[:, s0:s0 + N], w=[cos_t])
                    self.load(sin_t[:], self.rope[:, c.S + s0: c.S + s0 + N], w=[sin_t])
                hk = lambda kc: h[:, kc * N:(kc + 1) * N]

                def fm_family(c0, ncols, dst, kindf):
                    state = {}

                    def epi(cbi, bk):
                        j = cbi % 4
                        if j == 0:
                            state["o"] = fm[rr["fm"] % 3]
                            rr["fm"] += 1
                        o = state["o"]
                        oap = o[:, j * N:(j + 1) * N]
                        if kindf == "rope" and not is_ctx and DEBUG_ROPE != 0:
                            i = rr["rope"] % 2
                            rr["rope"] += 1
                            self.cp("act", qb[i][:], bk[:, 0:N], r=[bk], w=[qb[i]])
                            rb = self.banks[7]
                            self.mm(rb[:, 0:N], self.R_b, qb[i][:], True, True, r=[qb[i], self.cb16], w=[rb])
                            if DEBUG_ROPE == 1:
                                self.cp("act", oap, rb[:, 0:N], r=[rb], w=[o])
                                return
                            self.tt("dve", t1[i][:], bk[:, 0:N], cos_t[:], ALU.mult, r=[bk, cos_t, qb[i]], w=[t1[i]])
                            if DEBUG_ROPE == 2:
                                self.cp("act", oap, t1[i][:], r=[t1[i]], w=[o])
                                return
                            self.tt("dve", t2[i][:], rb[:, 0:N], sin_t[:], ALU.mult, r=[rb, sin_t], w=[t2[i]])
                            if DEBUG_ROPE == 3:
                                self.cp("act", oap, t2[i][:], r=[t2[i], t1[i]], w=[o])
                                return
                            self.tt(POOL_ENG, oap, t1[i][:], t2[i][:], ALU.add, r=[t1[i], t2[i]], w=[o])
                        elif kindf == "sigmoid":
                            self.act(oap, bk[:, 0:N], AF.Sigmoid, r=[bk], w=[o])
                        else:
                            self.cp("act", oap, bk[:, 0:N], r=[bk], w=[o])
                        nb = min(4, ncols // 128 - (cbi // 4) * 4)
                        if j == nb - 1:
                            r0 = (cbi // 4) * 512
                            d = dst[r0:r0 + nb * 128, t0:t0 + N].rearrange("(j p) t -> p j t", p=128)
                            self.store(d, o[:, 0:nb * N].rearrange("p (j t) -> p j t", t=N), r=[o])
                    self.dense(wl, c.D, c0, ncols, h, hk, N, dbanks, epi, wslots)

                def tm_family(c0, ncols, dst, func):
                    for g0 in range(0, ncols, 512):
                        gw = min(512, ncols - g0)

                        def epi(s, bk, g0=g0, gw=gw):
                            o = tm[rr["tm"] % 3]
                            rr["tm"] += 1
                            if func is None:
                                self.cp("act", o[:, 0:gw], bk[:, 0:gw], r=[bk], w=[o])
                            else:
                                self.act(o[:, 0:gw], bk[:, 0:gw], func, r=[bk], w=[o])
                            self.store(dst[t0 + s * 128: t0 + (s + 1) * 128, g0:g0 + gw], o[:, 0:gw], r=[o])
                        self.dense_tm(wl, c.D, c0 + g0, gw, h, N, dbanks, epi, wslots)

                if DEBUG_SUB == 0:
                    return
                fm_family(c.COL_K, c.KW, self.kT, "rope")
                if DEBUG_SUB == 1:
                    return
                tm_family(c.COL_V, c.KW, self.vtm, None)
                if DEBUG_SUB == 2:
                    return
                fm_family(c.COL_XBC, c.CC, self.xbcT, "copy")
                if DEBUG_SUB == 3:
                    return

                H2 = 2 * c.H

                def epi_dt(s, bk):
                    i = rr["dt"] % 2
                    rr["dt"] += 1
                    self.tt("dve", dtx[i][:], bk[:, 0:H2], self.dtb[:], ALU.add, r=[bk, self.dtb], w=[dtx[i]])
                    self.act(dta[i][:], dtx[i][:], AF.Abs, r=[dtx[i]], w=[dta[i]])
                    self.act(dtl[i][:], dta[i][:], AF.Exp, scale=-1.0, r=[dta[i]], w=[dtl[i]])
                    self.act(dtl[i][:], dtl[i][:], AF.Ln, bias=self.one_c[:, 0:1], r=[dtl[i], self.one_c], w=[dtl[i]])
                    self.stt("dve", dto[i][:], dtx[i][:], 0.0, dtl[i][:], ALU.max, ALU.add, r=[dtx[i], dtl[i]], w=[dto[i]])
                    self.store(self.dtv[t0 + s * 128: t0 + (s + 1) * 128, :], dto[i][:], r=[dto[i]])
                self.dense_tm(wl, c.D, c.COL_DT, H2, h, N, dbanks, epi_dt, wslots)
                if DEBUG_SUB == 4:
                    return
                if not skip_q:
                    fm_family(c.COL_Q, c.AW, self.qT, "rope")
                    if DEBUG_SUB == 5:
                        continue
                    tm_family(c.COL_Z, c.DI, self.sz, AF.Silu)
                    if DEBUG_SUB == 6:
                        continue
                    fm_family(c.COL_GATE, 2 * c.D, self.gT, "sigmoid")

    def phase_attn(self, l, last):
        from contextlib import ExitStack
        c = self.c
        S, L = c.S, c.L
        scale = 1.0 / math.sqrt(128.0)
        NBL = S // 128
        NBC = L // 128
        with ExitStack() as st:
            kc_t = [self.sb(st, [128, L], BF16, "kc%d" % i) for i in range(2)]
            kl_t = [self.sb(st, [128, S], BF16, "kl%d" % i) for i in range(2)]
            vc_t = [self.sb(st, [128, NBC * 128], BF16, "vc%d" % i) for i in range(2)]
            vl_t = [self.sb(st, [128, NBL * 128], BF16, "vl%d" % i) for i in range(2)]
            q_t = [self.sb(st, [128, S], BF16, "q%d" % i) for i in range(2)]
            qc_t = [self.sb(st, [128, L], BF16, "qc%d" % i) for i in range(2)]
            o_t = [self.sb(st, [128, S], BF16, "o%d" % i) for i in range(2)]
            oc_t = [self.sb(st, [128, L], BF16, "oc%d" % i) for i in range(2)]
            pT = [self.sb(st, [128, 512], BF16, "pT%d" % i) for i in range(4)]
            rec = [self.sb(st, [128, 512], F32, "rec%d" % i) for i in range(2)]
            sbanks = self.banks[0:4]
            obanks = [(self.banks[4], self.banks[5]), (self.banks[6], self.banks[7])]
            cnt = {"s": 0, "o": 0, "p": 0, "g": 0, "h": 0}

            def score_block(kt, kap, qt, qap, nq, mask_specs):
                sb_ = sbanks[cnt["s"] % 4]
                cnt["s"] += 1
                self.mm(sb_[:, 0:nq], kap, qap, True, True, r=[kt, qt], w=[sb_])
                p = pT[cnt["p"] % 4]
                cnt["p"] += 1
                self.act(p[:, 0:nq], sb_[:, 0:nq], AF.Exp, scale=scale, r=[sb_], w=[p])
                for (off, which) in mask_specs:
                    self.tt("dve", p[:, off:off + 128], p[:, off:off + 128], self.U_b[which], ALU.mult,
                            r=[p, self.cb16], w=[p])
                return p

            def finish(ob, db, h, ot, ocol, nq):
                r_ = rec[cnt["o"] % 2]
                self.ts("dve", r_[:, 0:nq], db[:, 0:nq], self.sinkexp[:, h:h + 1], None, ALU.add, None,
                        r=[db, self.sinkexp], w=[r_])
                self.P.op("dve", lambda e: e.reciprocal(out=r_[:, 0:nq], in_=r_[:, 0:nq]), r=[r_], w=[r_])
                self.tt("dve", ot[:, ocol:ocol + nq], ob[:, 0:nq], r_[:, 0:nq], ALU.mult, r=[ob, r_], w=[ot])

            for b in range(c.BPC):
                co = c.ctx_off(b)
                lo = c.lat_off(b)
                for g in range(c.NKV):
                    gi = cnt["g"] % 2
                    cnt["g"] += 1
                    kc, kl, vc, vl = kc_t[gi], kl_t[gi], vc_t[gi], vl_t[gi]
                    self.load(kc[:], self.kT[g * 128:(g + 1) * 128, co:co + L], w=[kc])
                    self.load(kl[:], self.kT[g * 128:(g + 1) * 128, lo:lo + S], w=[kl])
                    self.load(vc[:].rearrange("p (n d) -> p n d", d=128),
                              self.vtm[co:co + L, g * 128:(g + 1) * 128].rearrange("(n p) d -> p n d", p=128), w=[vc])
                    self.load(vl[:].rearrange("p (n d) -> p n d", d=128),
                              self.vtm[lo:lo + S, g * 128:(g + 1) * 128].rearrange("(n p) d -> p n d", p=128), w=[vl])
                    for hh in range(c.REP):
                        h = g * c.REP + hh
                        hi = cnt["h"] % 2
                        cnt["h"] += 1
                        q, qc, ot, oc = q_t[hi], qc_t[hi], o_t[hi], oc_t[hi]
                        self.load(q[:], self.qT[h * 128:(h + 1) * 128, lo:lo + S], w=[q])
                        for qg in range(S // 512):
                            ob, db = obanks[cnt["o"] % 2]
                            blocks = []
                            for j in range(NBC):
                                blocks.append(("c", j, 0, 512, []))
                            for j in range(qg * 4 - 1, qg * 4 + 5):
                                if j < 0 or j >= NBL:
                                    continue
                                qlo = max(j - 1, qg * 4)
                                qhi = min(j + 1, qg * 4 + 3)
                                nq = (qhi - qlo + 1) * 128
                                masks = []
                                for qb_ in range(qlo, qhi + 1):
                                    if qb_ == j - 1:
                                        masks.append(((qb_ - qlo) * 128, 0))
                                    elif qb_ == j + 1:
                                        masks.append(((qb_ - qlo) * 128, 1))
                                blocks.append(("l", j, qlo, nq, masks))

                            def do_score(bl):
                                kind_, j, qlo, nq, masks = bl
                                if kind_ == "c":
                                    return score_block(kc, kc[:, j * 128:(j + 1) * 128], q, q[:, qg * 512:(qg + 1) * 512], 512, [])
                                return score_block(kl, kl[:, j * 128:(j + 1) * 128], q, q[:, qlo * 128: qlo * 128 + nq], nq, masks)

                            pcur = do_score(blocks[0])
                            for bi_, bl in enumerate(blocks):
                                pnext = do_score(blocks[bi_ + 1]) if bi_ + 1 < len(blocks) else None
                                kind_, j, qlo, nq, masks = bl
                                if kind_ == "c":
                                    vt, vap, c0 = vc, vc[:, j * 128:(j + 1) * 128], 0
                                else:
                                    vt, vap, c0 = vl, vl[:, j * 128:(j + 1) * 128], (qlo - qg * 4) * 128
                                first = bi_ == 0
                                self.mm(ob[:, c0:c0 + nq], vap, pcur[:, 0:nq], first, False, r=[vt, pcur], w=[ob], inc=True)
                                self.mm(db[:, c0:c0 + nq], self.ones_b, pcur[:, 0:nq], first, False, r=[self.cb16, pcur], w=[db], inc=True)
                                pcur = pnext
                            finish(ob, db, h, ot, qg * 512, 512)
                            cnt["o"] += 1
                        self.store(self.attT[h * 128:(h + 1) * 128, lo:lo + S], ot[:], r=[ot])
                        if not last:
                            self.load(qc[:], self.qT[h * 128:(h + 1) * 128, co:co + L], w=[qc])
                            ob, db = obanks[cnt["o"] % 2]
                            for j in range(NBC):
                                p = score_block(kc, kc[:, j * 128:(j + 1) * 128], qc, qc[:, 0:L], L, [])
                                self.mm(ob[:, 0:L], vc[:, j * 128:(j + 1) * 128], p[:, 0:L], j == 0, False, r=[vc, p], w=[ob], inc=True)
                                self.mm(db[:, 0:L], self.ones_b, p[:, 0:L], j == 0, False, r=[self.cb16, p], w=[db], inc=True)
                            finish(ob, db, h, oc, 0, L)
                            cnt["o"] += 1
                            self.store(self.attT[h * 128:(h + 1) * 128, co:co + L], oc[:], r=[oc])

    def phase_conv(self, l):
        from contextlib import ExitStack
        c = self.c
        XB = c.DI // 128
        GB = c.GN // 128
        with ExitStack() as st:
            dg = self.sb(st, [128, c.CB * 5 * 128], BF16, "dg")
            xin = [self.sb(st, [128, 4 * 516], BF16, "cxin%d" % i) for i in range(2)]
            ysb = [self.sb(st, [128, 4 * 512], BF16, "cy%d" % i) for i in range(2)]
            xs_tm = self.sb(st, [128, 4 * c.DI], BF16, "xs_tm")
            b_tm = self.sb(st, [128, 4 * c.GN], BF16, "b_tm")
            ident_f = self.cf[:, 0:128]
            q = 0
            for cb in range(c.CB):
                for k in range(5):
                    wv = self.cvp[:, cb * 6 + k: cb * 6 + k + 1]
                    o = (cb * 5 + k) * 128
                    self.act(dg[:, o:o + 128], ident_f, AF.Identity, scale=wv, r=[self.cf, self.cvp], w=[dg])
            cps = self.banks[0:4]
            tps = self.banks[4:8]
            cnt = {"x": 0, "a": 0, "y": 0, "t": 0}
            chunks = []
            for b in range(c.BPC):
                for s0 in range(0, c.L, 512):
                    Lc = min(512, c.L - s0)
                    chunks.append((c.ctx_off(b) + s0, Lc, s0 == 0, s0 + Lc == c.L))
            for b in range(c.BPC):
                for s0 in range(0, c.S, 512):
                    chunks.append((c.lat_off(b) + s0, 512, s0 == 0, s0 + 512 == c.S))
            for (t0, Lc, zl, zr) in chunks:
                nt = Lc // 128
                for cg in range(0, c.CB, 4):
                    ncb = min(4, c.CB - cg)
                    xi = xin[cnt["x"] % 2]
                    cnt["x"] += 1
                    x3 = xi[:].rearrange("p (j t) -> p j t", t=516)
                    a = t0 - 2 if not zl else t0
                    bnd = t0 + Lc + 2 if not zr else t0 + Lc
                    oa = 0 if not zl else 2
                    if zl:
                        self.memset("dve", x3[:, 0:ncb, 0:2], 0.0, w=[xi])
                    if zr:
                        self.memset("dve", x3[:, 0:ncb, Lc + 2:Lc + 4], 0.0, w=[xi])
                    src = self.xbcT[cg * 128:(cg + ncb) * 128, a:bnd].rearrange("(j p) t -> p j t", p=128)
                    self.P.dma("sp", x3[:, 0:ncb, oa:oa + (bnd - a)], src, r=(), w=[xi])
                    yt = ysb[cnt["y"] % 2]
                    cnt["y"] += 1
                    for j in range(ncb):
                        cb = cg + j
                        pb = cps[cnt["a"] % 4]
                        cnt["a"] += 1
                        for k in range(5):
                            o = (cb * 5 + k) * 128
                            self.mm(pb[:, 0:Lc], dg[:, o:o + 128], x3[:, j, k:k + Lc], k == 0, k == 4, r=[dg, xi], w=[pb])
                        self.act(yt[:, j * 512: j * 512 + Lc], pb[:, 0:Lc], AF.Silu, bias=self.cvp[:, cb * 6 + 5: cb * 6 + 6],
                                 r=[pb, self.cvp], w=[yt])
                        if cb < XB + GB:
                            bk = tps[cnt["t"] % 4]
                            cnt["t"] += 1
                            bkb = bk[:].bitcast(BF16)
                            for tb in range(nt):
                                self.P.op("pe", lambda e, tb=tb, j=j, bkb=bkb, yt=yt: e.transpose(
                                    bkb[:, tb * 128:(tb + 1) * 128], yt[:, j * 512 + tb * 128: j * 512 + (tb + 1) * 128], self.ident_b),
                                    r=[yt, self.cb16], w=[bk], inc=(tb == nt - 1))
                            if cb < XB:
                                dst = xs_tm[:].rearrange("p (tb ch) -> p tb ch", ch=c.DI)[:, 0:nt, cb * 128:(cb + 1) * 128]
                                dtile = xs_tm
                            else:
                                dst = b_tm[:].rearrange("p (tb ch) -> p tb ch", ch=c.GN)[:, 0:nt, (cb - XB) * 128:(cb - XB + 1) * 128]
                                dtile = b_tm
                            self.cp("dve", dst, bkb[:, 0:nt * 128].rearrange("p (tb ch) -> p tb ch", ch=128), r=[bk], w=[dtile])
                    for j in range(ncb):
                        cb = cg + j
                        if cb >= XB:
                            dstT = self.BT if cb < XB + GB else self.CT
                            rb = (cb - XB) if cb < XB + GB else (cb - XB - GB)
                            self.store(dstT[rb * 128:(rb + 1) * 128, t0:t0 + Lc], yt[:, j * 512: j * 512 + Lc], r=[yt])
                self.store(self.xs[t0:t0 + Lc, :].rearrange("(tb p) ch -> p tb ch", p=128),
                           xs_tm[:].rearrange("p (tb ch) -> p tb ch", ch=c.DI)[:, 0:nt, :], r=[xs_tm])
                self.store(self.Btm[t0:t0 + Lc, :].rearrange("(tb p) ch -> p tb ch", p=128),
                           b_tm[:].rearrange("p (tb ch) -> p tb ch", ch=c.GN)[:, 0:nt, :], r=[b_tm])

    def phase_ssd(self, l, last):
        from contextlib import ExitStack
        c = self.c
        H, E, G, DI, GN = c.H, c.E, c.G, c.DI, c.GN
        EW = E * 64
        NB3 = 3
        with ExitStack() as st:
            dt_c = [self.sb(st, [128, H], F32, "dt_c%d" % i) for i in range(2)]
            dta = [self.sb(st, [128, H], F32, "dta_c%d" % i) for i in range(2)]
            ea = [self.sb(st, [128, H], F32, "ea%d" % i) for i in range(2)]
            eal = [self.sb(st, [128, H], F32, "eal%d" % i) for i in range(2)]
            xs_c = [self.sb(st, [128, DI], BF16, "xs_c%d" % i) for i in range(2)]
            xdt = [self.sb(st, [128, DI], BF16, "xdt%d" % i) for i in range(2)]
            b_c = [self.sb(st, [128, GN], BF16, "b_c%d" % i) for i in range(2)]
            bT_c = [self.sb(st, [128, GN], BF16, "bT_c%d" % i) for i in range(2)]
            cT_c = [self.sb(st, [128, GN], BF16, "cT_c%d" % i) for i in range(2)]
            y_c = [self.sb(st, [128, DI], F32, "y_c%d" % i) for i in range(2)]
            Lm = [self.sb(st, [128, E * 128], F32, "Lm%d" % i) for i in range(NB3)]
            expD = [self.sb(st, [128, E * 128], F32, "expD%d" % i) for i in range(NB3)]
            Gm = [self.sb(st, [128, 128], F32, "Gm%d" % i) for i in range(NB3)]
            MT = [self.sb(st, [128, E * 128], BF16, "MT%d" % i) for i in range(2)]
            ytmp = [self.sb(st, [128, EW], F32, "ytmp%d" % i) for i in range(2)]
            wend = [self.sb(st, [128, EW], BF16, "wend%d" % i) for i in range(2)]
            S_f = [self.sb(st, [128, EW], F32, "S_f%d" % g) for g in range(G)]
            S_b = [self.sb(st, [128, EW], BF16, "S_b%d" % g) for g in range(G)]
            bk_Gm = self.banks[0]
            Dsets = [(self.banks[1], self.banks[2]), (self.banks[3], self.banks[4])]
            bk_Y, bk_Yo, bk_cs = self.banks[5], self.banks[6], self.banks[7]

            def prologue(ch):
                k, d, t0 = ch["k"], ch["d"], ch["t0"]
                self.load(dt_c[k][:], self.dtv[t0:t0 + 128, d * H:(d + 1) * H], w=[dt_c[k]])
                self.load(xs_c[k][:], self.xs[t0:t0 + 128, :], w=[xs_c[k]])
                self.load(b_c[k][:], self.Btm[t0:t0 + 128, :], w=[b_c[k]])
                self.load(bT_c[k][:].rearrange("p (g t) -> p g t", t=128),
                          self.BT[:, t0:t0 + 128].rearrange("(g p) t -> p g t", p=128), w=[bT_c[k]])
                self.load(cT_c[k][:].rearrange("p (g t) -> p g t", t=128),
                          self.CT[:, t0:t0 + 128].rearrange("(g p) t -> p g t", p=128), w=[cT_c[k]])
                self.tt("dve", dta[k][:], dt_c[k][:], self.aneg[:, d * H:(d + 1) * H], ALU.mult,
                        r=[dt_c[k], self.aneg], w=[dta[k]])
                self.mm(bk_Gm[:, 128:128 + H], self.U_f[d], dta[k][:], True, True, r=[self.cf, dta[k]], w=[bk_Gm])
                self.mm(bk_Gm[:, 128 + H:128 + 2 * H], self.ones_f, dta[k][:], True, True, r=[self.cf, dta[k]], w=[bk_Gm])
                self.act(ea[k][:], bk_Gm[:, 128:128 + H], AF.Exp, r=[bk_Gm], w=[ea[k]])
                self.act(eal[k][:], bk_Gm[:, 128 + H:128 + 2 * H], AF.Exp, r=[bk_Gm], w=[eal[k]])
                self.tt("dve", xdt[k][:].rearrange("p (h q) -> p h q", q=64),
                        xs_c[k][:].rearrange("p (h q) -> p h q", q=64),
                        bc_ap(dt_c[k][:], [[1, H], [0, 64]]), ALU.mult, r=[xs_c[k], dt_c[k]], w=[xdt[k]])

            def S1(u):
                k, d, g, n3 = u["k"], u["d"], u["g"], u["n"] % NB3
                for e_ in range(E):
                    hh = g * E + e_
                    self.act(Lm[n3][:, e_ * 128:(e_ + 1) * 128], self.U_f[d], AF.Identity,
                             scale=dta[k][:, hh:hh + 1], r=[self.cf, dta[k]], w=[Lm[n3]])

            def S2(u):
                k, d, g, n3 = u["k"], u["d"], u["g"], u["n"] % NB3
                D0, D1 = Dsets[u["n"] % 2]
                self.mm(D0[:, 0:512], self.A_f[d], Lm[n3][:, 0:512], True, True, r=[Lm[n3], self.cf], w=[D0])
                self.mm(D1[:, 0:512], self.A_f[d], Lm[n3][:, 512:1024], True, True, r=[Lm[n3], self.cf], w=[D1])
                if u["want_y"]:
                    bTg = bT_c[k][:, g * 128:(g + 1) * 128]
                    cTg = cT_c[k][:, g * 128:(g + 1) * 128]
                    self.mm(bk_Gm[:, 0:128], bTg, cTg, True, True, r=[bT_c[k], cT_c[k]], w=[bk_Gm])
                self.act(expD[n3][:, 0:512], D0[:, 0:512], AF.Exp, r=[D0], w=[expD[n3]])
                self.act(expD[n3][:, 512:1024], D1[:, 0:512], AF.Exp, r=[D1], w=[expD[n3]])
                if u["want_y"]:
                    self.tt("dve", Gm[n3][:], bk_Gm[:, 0:128], self.U_f[d], ALU.mult, r=[bk_Gm, self.cf], w=[Gm[n3]])

            def S3(u):
                k, d, g, n3, n2 = u["k"], u["d"], u["g"], u["n"] % NB3, u["n"] % 2
                icol = 127 if d == 0 else 0
                want_y = u["want_y"]
                cTg = cT_c[k][:, g * 128:(g + 1) * 128]
                self.tt("dve", wend[n2][:].rearrange("p (e q) -> p e q", q=64),
                        xdt[k][:, g * EW:(g + 1) * EW].rearrange("p (e q) -> p e q", q=64),
                        bc_ap(expD[n3][:, icol:icol + 1], [[128, E], [0, 64]]), ALU.mult,
                        r=[xdt[k], expD[n3]], w=[wend[n2]])
                if want_y:
                    self.tt("dve", MT[n2][:].rearrange("p (e i) -> p e i", i=128),
                            expD[n3][:].rearrange("p (e i) -> p e i", i=128),
                            bc_ap(Gm[n3][:], [[0, E], [1, 128]]), ALU.mult, r=[expD[n3], Gm[n3]], w=[MT[n2]])
                self.mm(bk_cs[:, 0:EW], b_c[k][:, g * 128:(g + 1) * 128], wend[n2][:], True, True,
                        r=[b_c[k], wend[n2]], w=[bk_cs])
                if want_y:
                    self.mm(bk_Yo[:, 0:EW], cTg, S_b[g][:], True, True, r=[cT_c[k], S_b[g]], w=[bk_Yo])
                    for e_ in range(E):
                        hh = g * E + e_
                        self.mm(bk_Y[:, e_ * 64:(e_ + 1) * 64], MT[n2][:, e_ * 128:(e_ + 1) * 128],
                                xdt[k][:, hh * 64:(hh + 1) * 64], True, True, r=[MT[n2], xdt[k]], w=[bk_Y],
                                inc=(e_ == E - 1))
                Sg = S_f[g][:]
                self.tt("dve", Sg.rearrange("p (e q) -> p e q", q=64), Sg.rearrange("p (e q) -> p e q", q=64),
                        bc_ap(eal[k][:, g * E:(g + 1) * E], [[1, E], [0, 64]]), ALU.mult,
                        r=[S_f[g], eal[k]], w=[S_f[g]])
                self.tt("dve", Sg, Sg, bk_cs[:, 0:EW], ALU.add, r=[S_f[g], bk_cs], w=[S_f[g]])
                if want_y:
                    self.tt("dve", ytmp[n2][:].rearrange("p (e q) -> p e q", q=64),
                            bk_Yo[:, 0:EW].rearrange("p (e q) -> p e q", q=64),
                            bc_ap(ea[k][:, g * E:(g + 1) * E], [[1, E], [0, 64]]), ALU.mult,
                            r=[bk_Yo, ea[k]], w=[ytmp[n2]])
                    self.tt("dve", y_c[k][:, g * EW:(g + 1) * EW], ytmp[n2][:], bk_Y[:, 0:EW], ALU.add,
                            r=[ytmp[n2], bk_Y], w=[y_c[k]])
                    if g == G - 1:
                        self.store(self.yd[d][u["t0"]:u["t0"] + 128, :], y_c[k][:], r=[y_c[k]])

            def S4(u):
                g = u["g"]
                self.cp("act", S_b[g][:], S_f[g][:], r=[S_f[g]], w=[S_b[g]])

            ci = 0
            for b in range(c.BPC):
                for d in range(2):
                    for g in range(G):
                        self.memset("dve", S_f[g][:], 0.0, w=[S_f[g]])
                        self.memset("dve", S_b[g][:], 0.0, w=[S_b[g]])
                    seq = [(c.ctx_off(b) + i * 128, True) for i in range(c.L // 128)]
                    lat = [(c.lat_off(b) + i * 128, False) for i in range(c.S // 128)]
                    if d == 1:
                        seq = seq[::-1]
                        lat = lat[::-1]
                    units = []
                    chunks = []
                    for (t0, is_ctx) in seq + lat:
                        ch = {"k": ci % 2, "d": d, "t0": t0}
                        ci += 1
                        chunks.append(ch)
                        for g in range(G):
                            units.append({"k": ch["k"], "d": d, "g": g, "t0": t0, "n": len(units),
                                          "want_y": not (last and is_ctx), "ch": ch if g == 0 else None})
                    NU = len(units)
                    for t in range(-2, NU + 1):
                        if 0 <= t + 2 < NU:
                            u = units[t + 2]
                            if u["ch"] is not None:
                                prologue(u["ch"])
                            S1(u)
                        if 0 <= t + 1 < NU:
                            S2(units[t + 1])
                        if 0 <= t < NU:
                            S3(units[t])
                        if 0 <= t - 1 < NU:
                            S4(units[t - 1])

    def phase4(self, l, last):
        from contextlib import ExitStack
        c = self.c
        DI, H = c.DI, c.H
        NBK = DI // 128
        with ExitStack() as st:
            dsum_bc = self.sb(st, [128, DI], F32, "dsum_bc")
            gssd_bc = self.sb(st, [128, DI], F32, "gssd_bc")
            self.cp("dve", dsum_bc[:].rearrange("p (h q) -> p h q", q=64), bc_ap(self.dsum[:], [[1, H], [0, 64]]),
                    r=[self.dsum], w=[dsum_bc])
            self.load(gssd_bc[:], bass.AP(self.rows, self.gssd_off, [[0, 128], [1, DI]]), w=[gssd_bc])
            yf = [self.sb(st, [128, DI], F32, "p4yf%d" % i) for i in range(2)]
            yb = [self.sb(st, [128, DI], F32, "p4yb%d" % i) for i in range(2)]
            xs_ = [self.sb(st, [128, DI], BF16, "p4xs%d" % i) for i in range(2)]
            sz_ = [self.sb(st, [128, DI], BF16, "p4sz%d" % i) for i in range(2)]
            ob = [self.sb(st, [128, DI], BF16, "p4o%d" % i) for i in range(1)] * 2
            ssq = [self.sb(st, [128, 1], F32, "p4ssq%d" % i) for i in range(2)]
            acc = [self.sb(st, [128, NBK * 512], BF16, "p4acc%d" % i) for i in range(1)] * 2
            tps = self.banks[0:8]
            tcount = 0
            ai = 0
            tbs = []
            for (t0, kind, b, s0) in self.tiles512():
                if last and kind == "ctx":
                    continue
                tbs.append(t0)
            for ti, t0 in enumerate(tbs):
                ac = acc[ai % 2]
                ai += 1
                for s in range(4):
                    k = (ti * 4 + s) % 2
                    r0 = t0 + s * 128
                    self.load(yf[k][:], self.yd[0][r0:r0 + 128, :], w=[yf[k]])
                    self.load(yb[k][:], self.yd[1][r0:r0 + 128, :], w=[yb[k]])
                    self.load(xs_[k][:], self.xs[r0:r0 + 128, :], w=[xs_[k]])
                    self.load(sz_[k][:], self.sz[r0:r0 + 128, :], w=[sz_[k]])
                    self.tt("dve", yf[k][:], yf[k][:], yb[k][:], ALU.add, r=[yf[k], yb[k]], w=[yf[k]])
                    self.tt("pool", yb[k][:], xs_[k][:], dsum_bc[:], ALU.mult, r=[xs_[k], dsum_bc, yf[k]], w=[yb[k]])
                    self.tt("dve", yf[k][:], yf[k][:], yb[k][:], ALU.add, r=[yf[k], yb[k]], w=[yf[k]])
                    self.tt("pool", yf[k][:], yf[k][:], sz_[k][:], ALU.mult, r=[yf[k], sz_[k]], w=[yf[k]])
                    self.memset("dve", ssq[k][:], 0.0, w=[ssq[k]])
                    self.act(yb[k][:], yf[k][:], AF.Square, accum=ssq[k][:], r=[yf[k], ssq[k]], w=[yb[k], ssq[k]])
                    self.rsqrt(ssq[k][:], ssq[k][:], 1.0 / DI, [ssq[k]], ssq[k])
                    self.stt("dve", ob[k][:], yf[k][:], ssq[k][:, 0:1], gssd_bc[:], ALU.mult, ALU.mult,
                             r=[yf[k], ssq[k], gssd_bc], w=[ob[k]])
                    for q0 in range(0, NBK, 8):
                        bk = tps[tcount % 8]
                        tcount += 1
                        bkb = bk[:].bitcast(BF16)
                        nq = min(8, NBK - q0)
                        for q in range(nq):
                            self.P.op("pe", lambda e, q=q, q0=q0, bkb=bkb, k=k: e.transpose(
                                bkb[:, q * 128:(q + 1) * 128], ob[k][:, (q0 + q) * 128:(q0 + q + 1) * 128], self.ident_b),
                                r=[ob[k], self.cb16], w=[bk], inc=(q == nq - 1))
                        dst = ac[:].rearrange("p (blk t) -> p blk t", t=512)[:, q0:q0 + nq, s * 128:(s + 1) * 128]
                        self.cp("act", dst, bkb[:, 0:nq * 128].rearrange("p (blk t) -> p blk t", t=128), r=[bk], w=[ac])
                self.store(self.ssdT[:, t0:t0 + 512].rearrange("(blk p) t -> p blk t", p=128),
                           ac[:].rearrange("p (blk t) -> p blk t", t=512), r=[ac])

    def phase5a(self, l, last):
        from contextlib import ExitStack
        c = self.c
        N = 512
        AC = c.AW // 128
        SC = c.DI // 128
        DC = c.DC
        self._dense_rr = 0
        self._ws_rr = 0
        with ExitStack() as st:
            at = self.sb(st, [128, AC * N], BF16, "p5at")
            sd = self.sb(st, [128, SC * N], BF16, "p5sd")
            mT = self.sb(st, [128, DC * N], BF16, "p5m")
            wslots = [self.sb(st, [128, 16 * 512], BF16, "p5ws%d" % i) for i in range(3)]
            gA = [self.sb(st, [128, 4 * N], BF16, "p5gA%d" % i) for i in range(2)]
            gB = [self.sb(st, [128, 4 * N], BF16, "p5gB%d" % i) for i in range(2)]
            tA = [self.sb(st, [128, 4 * N], F32, "p5tA%d" % i) for i in range(2)]
            tB = [self.sb(st, [128, N], F32, "p5tB%d" % i) for i in range(2)]
            xb = [self.sb(st, [128, 4 * N], F32, "p5xb%d" % i) for i in range(2)]
            banksA = self.banks[0:4]
            banksB = self.banks[4:8]
            cnt = {"g": 0, "t": 0, "x": 0}
            for (t0, kind, b, s0) in self.tiles512():
                is_ctx = kind == "ctx"
                if last and is_ctx:
                    continue
                m = c.BPC if is_ctx else b
                self.load(at[:].rearrange("p (k n) -> p k n", n=N),
                          self.attT.ap().rearrange("(k p) t -> p k t", p=128)[:, :, t0:t0 + N], w=[at])
                self.load(sd[:].rearrange("p (k n) -> p k n", n=N),
                          self.ssdT.ap().rearrange("(k p) t -> p k t", p=128)[:, :, t0:t0 + N], w=[sd])
                for cg in range(0, DC, 4):
                    ncb = min(4, DC - cg)
                    gi = cnt["g"] % 2
                    cnt["g"] += 1
                    self.load(gA[gi][:, 0:ncb * N].rearrange("p (j n) -> p j n", n=N),
                              self.gT[cg * 128:(cg + ncb) * 128, t0:t0 + N].rearrange("(j p) t -> p j t", p=128), w=[gA[gi]])
                    self.load(gB[gi][:, 0:ncb * N].rearrange("p (j n) -> p j n", n=N),
                              self.gT[c.D + cg * 128: c.D + (cg + ncb) * 128, t0:t0 + N].rearrange("(j p) t -> p j t", p=128), w=[gB[gi]])

                    def epiA(cbi, bk, gi=gi):
                        self.tt("dve", tA[gi][:, cbi * N:(cbi + 1) * N], bk[:, 0:N], gA[gi][:, cbi * N:(cbi + 1) * N], ALU.mult,
                                r=[bk, gA[gi]], w=[tA[gi]])

                    def epiB(cbi, bk, gi=gi, cg=cg):
                        i = cnt["t"] % 2
                        cnt["t"] += 1
                        self.tt("dve", tB[i][:], bk[:, 0:N], gB[gi][:, cbi * N:(cbi + 1) * N], ALU.mult, r=[bk, gB[gi]], w=[tB[i]])
                        self.tt("pool", mT[:, (cg + cbi) * N:(cg + cbi + 1) * N], tB[i][:], tA[gi][:, cbi * N:(cbi + 1) * N], ALU.add,
                                r=[tB[i], tA[gi]], w=[mT])
                    self._dense_rr = 0
                    self.dense(self.wbs[l % 2]["oa"], c.AW, cg * 128, ncb * 128, at, lambda kc: at[:, kc * N:(kc + 1) * N], N, banksA, epiA, wslots)
                    self._dense_rr = 0
                    self.dense(self.wbs[l % 2]["os"], c.DI, cg * 128, ncb * 128, sd, lambda kc: sd[:, kc * N:(kc + 1) * N], N, banksB, epiB, wslots)
                for cg in range(0, DC, 4):
                    ncb = min(4, DC - cg)
                    xi = cnt["x"] % 2
                    cnt["x"] += 1
                    xv = self.xT[cg * 128:(cg + ncb) * 128, t0:t0 + N].rearrange("(j p) t -> p j t", p=128)
                    self.load(xb[xi][:, 0:ncb * N].rearrange("p (j n) -> p j n", n=N), xv, w=[xb[xi]])

                    def epiO(cbi, bk, xi=xi, cg=cg):
                        self.stt("dve", xb[xi][:, cbi * N:(cbi + 1) * N], bk[:, 0:N], self.mod_ap(l, 2, cg + cbi, m),
                                 xb[xi][:, cbi * N:(cbi + 1) * N], ALU.mult, ALU.add, r=[bk, self.mod, xb[xi]], w=[xb[xi]])
                    self.dense(self.wbs[l % 2]["out"], c.D, cg * 128, ncb * 128, mT, lambda kc: mT[:, kc * N:(kc + 1) * N], N, self.banks[0:8], epiO, wslots)
                    self.store(xv, xb[xi][:, 0:ncb * N].rearrange("p (j n) -> p j n", n=N), r=[xb[xi]])

    def phase5b(self, l, last):
        from contextlib import ExitStack
        c = self.c
        N = 512
        DC = c.DC
        FC = c.DFF // 128
        self._dense_rr = 0
        self._ws_rr = 0
        with ExitStack() as st:
            self.alloc_norm_tmps(st)
            xt = self.sb(st, [128, DC * N], F32, "p6x")
            h2 = self.sb(st, [128, DC * N], BF16, "p6h")
            f1 = self.sb(st, [128, FC * N], BF16, "p6f")
            wslots = [self.sb(st, [128, 16 * 512], BF16, "p6ws%d" % i) for i in range(3)]
            rl = [self.sb(st, [128, N], F32, "p6r%d" % i) for i in range(2)]
            cnt = {"r": 0}
            dbanks = self.banks[0:6]
            for (t0, kind, b, s0) in self.tiles512():
                is_ctx = kind == "ctx"
                if last and is_ctx:
                    continue
                m = c.BPC if is_ctx else b
                xsrc = self.xT.ap().rearrange("(dc p) t -> p dc t", p=128)[:, :, t0:t0 + N]
                self.load(xt[:].rearrange("p (dc n) -> p dc n", n=N), xsrc, w=[xt])
                self.norm_tile(st, xt, l, self.s2, lambda dc: self.mod_ap(l, 3, dc, m), m, h2)

                def epi1(cbi, bk):
                    i = cnt["r"] % 2
                    cnt["r"] += 1
                    self.act(rl[i][:], bk[:, 0:N], AF.Relu, r=[bk], w=[rl[i]])
                    self.tt("pool", f1[:, cbi * N:(cbi + 1) * N], rl[i][:], rl[i][:], ALU.mult, r=[rl[i]], w=[f1])
                self.dense(self.wbs[l % 2]["ff1"], c.D, 0, c.DFF, h2, lambda kc: h2[:, kc * N:(kc + 1) * N], N, dbanks, epi1, wslots)

                def epi2(cbi, bk):
                    self.stt("dve", xt[:, cbi * N:(cbi + 1) * N], bk[:, 0:N], self.mod_ap(l, 5, cbi, m),
                             xt[:, cbi * N:(cbi + 1) * N], ALU.mult, ALU.add, r=[bk, self.mod, xt], w=[xt])
                self.dense(self.wbs[l % 2]["ff2"], c.DFF, 0, c.D, f1, lambda kc: f1[:, kc * N:(kc + 1) * N], N, dbanks, epi2, wslots)
                if not last:
                    self.store(xsrc, xt[:].rearrange("p (dc n) -> p dc n", n=N), r=[xt])
                else:
                    gf = self.sb(st, [128, DC * c.NM], F32, "gfin%d" % t0)
                    g = self.gnt[:, 2 * c.DEPTH * DC:(2 * c.DEPTH + 1) * DC]
                    self.cp("dve", gf[:].rearrange("p (a b) -> p a b", b=c.NM), bc_ap(g, [[1, DC], [0, c.NM]]), r=[self.gnt], w=[gf])
                    of = self.sb(st, [128, DC * N], F32, "ofin%d" % t0) if False else f1
                    ofv = f1[:, 0:2 * DC * N].bitcast(F32)
                    DCn = DC
                    bk = self.banks[6]
                    for dc in range(DCn):
                        sq = self.sqs[dc % 2]
                        self.act(sq[:], xt[:, dc * N:(dc + 1) * N], AF.Square, r=[xt], w=[sq])
                        self.mm(bk[:, 0:N], self.ones_f, sq[:], dc == 0, dc == DCn - 1, r=[sq, self.cf], w=[bk], inc=True)
                    rstd = self.rstd
                    self.rsqrt(rstd[:], bk[:, 0:N], 1.0 / c.D, [bk], rstd)
                    for dc in range(DCn):
                        self.stt("dve", ofv[:, dc * N:(dc + 1) * N], xt[:, dc * N:(dc + 1) * N], g[:, dc:dc + 1], rstd[:],
                                 ALU.mult, ALU.mult, r=[xt, self.gnt, rstd, f1], w=[f1])
                    lo = b * c.S + s0
                    self.store(self.outT.ap().rearrange("(dc p) t -> p dc t", p=128)[:, :, lo:lo + N],
                               ofv[:, 0:DC * N].rearrange("p (dc n) -> p dc n", n=N), r=[f1])


def host_consts():
    r = np.arange(128)[:, None]
    cc = np.arange(128)[None, :]
    ident = (r == cc).astype(np.float32)
    U0 = (r <= cc).astype(np.float32)
    U1 = (r >= cc).astype(np.float32)
    SL = (r > cc).astype(np.float32)
    SU = (r < cc).astype(np.float32)
    ones = np.ones((128, 128), np.float32)
    R = np.zeros((128, 128), np.float32)
    for m in range(128):
        if m % 64 < 32:
            R[m + 32, m] = -1.0
        else:
            R[m - 32, m] = 1.0
    return np.concatenate([ident, U0, U1, SL, SU, ones, R], axis=1)


def host_rope(S, grid_w=64, theta=10000.0):
    t = np.arange(S)
    row = (t // grid_w).astype(np.float32)
    col = (t % grid_w).astype(np.float32)
    axis_dim = 64
    inv = (np.float32(theta) ** (-np.arange(0, axis_dim, 2, dtype=np.float32) / np.float32(axis_dim))).astype(np.float32)
    ang_r = (row[:, None] * inv[None]).astype(np.float32)
    ang_c = (col[:, None] * inv[None]).astype(np.float32)
    cos = np.zeros((128, S), np.float32)
    sin = np.zeros((128, S), np.float32)
    for d in range(128):
        a = ang_r if d < 64 else ang_c
        cos[d] = np.cos(a[:, d % 32])
        sin[d] = np.sin(a[:, d % 32])
    return np.concatenate([cos, sin], axis=1)


def fm(v, nchunk):
    v = np.asarray(v, np.float32)
    lead = v.shape[:-1]
    v = v.reshape(lead + (nchunk, 128))
    return np.moveaxis(v, -1, 0)


def prep_inputs(cfg, inp):
    c = cfg
    f32 = lambda a: np.ascontiguousarray(np.asarray(a, np.float32))
    x = f32(inp["x"])
    ctx = f32(inp["ctx"])
    cc = f32(inp["c"])
    c_ctx = f32(inp["c_ctx"])
    shared = {
        "w_ada": f32(inp["w_ada"]).reshape(c.DEPTH * c.D, 6 * c.D),
        "w_in": f32(inp["w_in"]).reshape(c.DEPTH * c.D, c.IN),
        "w_oa": f32(inp["w_o_attn"]).reshape(c.DEPTH * c.AW, c.D),
        "w_os": f32(inp["w_o_ssd"]).reshape(c.DEPTH * c.DI, c.D),
        "w_out": f32(inp["w_out"]).reshape(c.DEPTH * c.D, c.D),
        "w_ff1": f32(inp["w_ff1"]).reshape(c.DEPTH * c.D, c.DFF),
        "w_ff2": f32(inp["w_ff2"]).reshape(c.DEPTH * c.DFF, c.D),
    }
    shared["b_ada"] = f32(fm(inp["b_ada"], 6 * c.DC).reshape(128, -1))
    gn = np.concatenate([f32(inp["g_norm1"]), f32(inp["g_norm2"]), f32(inp["g_final"])[None]], axis=0)
    shared["gn"] = f32(fm(gn, c.DC).reshape(128, -1))
    cw = f32(inp["conv_w"])
    cb = f32(inp["conv_b"])
    cv = np.concatenate([cw, cb[:, None, :]], axis=1)
    cv = cv.reshape(c.DEPTH, 6, c.CB, 128)
    shared["convp"] = f32(np.transpose(cv, (3, 0, 2, 1)).reshape(128, -1))
    rows = np.concatenate([f32(inp["attn_sink"]), f32(inp["dt_bias"]).reshape(c.DEPTH, -1),
                           f32(inp["a_log"]).reshape(c.DEPTH, -1), f32(inp["d_skip"]).reshape(c.DEPTH, -1),
                           f32(inp["g_ssd"])], axis=1)
    shared["rows"] = f32(rows)
    shared["consts"] = host_consts()
    shared["rope"] = host_rope(c.S, c.GRID_W)
    in_maps = []
    for core in range(c.NCORES):
        bs = [core * c.BPC + i for i in range(c.BPC)]
        xin = np.concatenate([ctx[b].T for b in bs] + [x[b].T for b in bs], axis=1)
        cvec = np.stack([cc[b] for b in bs] + [c_ctx], axis=0)
        cin = np.transpose(cvec.reshape(c.NM, c.DC, 128), (2, 1, 0)).reshape(128, -1)
        m = dict(shared)
        m["xin"] = f32(xin)
        m["cin"] = f32(cin)
        in_maps.append(m)
    return in_maps


def build_nc(cfg):
    from contextlib import ExitStack
    nc = bass.Bass("TRN2", target_bir_lowering=False)
    with ExitStack() as stack:
        b = Builder(cfg, nc, stack)
        b.build()
    return nc


LAST_EXEC_NS = [None]


def run_cfg(cfg, inp, trace=False):
    in_maps = prep_inputs(cfg, inp)
    nc = build_nc(cfg)
    if trace:
        res = run_bass_kernel_spmd(nc, in_maps, core_ids=list(range(cfg.NCORES)), trace=True)
        LAST_EXEC_NS[0] = res.exec_time_ns
    else:
        res = run_bass_kernel_spmd(nc, in_maps, core_ids=list(range(cfg.NCORES)))
    outs = []
    for core in range(cfg.NCORES):
        oT = res.results[core]["outT"]
        o = oT.T.reshape(cfg.BPC, cfg.S, cfg.D)
        outs.append(o)
    return np.ascontiguousarray(np.concatenate(outs, axis=0).astype(np.float32))


def kernel(**inputs):
    cfg = Cfg()
    return run_cfg(cfg, inputs)
```

```python
import math
import numpy as np
import concourse.bass as bass
import concourse.mybir as mybir
from concourse.bass_utils import run_bass_kernel_spmd

F32 = mybir.dt.float32
BF16 = mybir.dt.bfloat16
AF = mybir.ActivationFunctionType
ALU = mybir.AluOpType
AX = mybir.AxisListType

SAME_ENGINE_SYNC = True
DEBUG_STOP = None
DEBUG_SUB = None
DEBUG_NORM = None
DEBUG_TILES = None
DEBUG_ROPE = None
POOL_ENG = "dve"
EPS = 1e-6


class Cfg:
    def __init__(s, D=2048, B=16, S=2048, DEPTH=4, L=256, NH=16, NKV=4, G=8, NCORES=8):
        s.D, s.B, s.S, s.DEPTH, s.L, s.NH, s.NKV, s.G, s.NCORES = D, B, S, DEPTH, L, NH, NKV, G, NCORES
        s.HD = 128
        s.AW = NH * 128
        s.KW = NKV * 128
        s.REP = NH // NKV
        s.DI = 2 * D
        s.P = 64
        s.H = s.DI // 64
        s.E = s.H // G
        s.N = 128
        s.K = 5
        s.GN = G * 128
        s.CC = s.DI + 2 * s.GN
        s.DFF = 4 * D
        s.GRID_W = 64
        s.COL_K = 0
        s.COL_V = s.KW
        s.COL_XBC = 2 * s.KW
        s.COL_DT = s.COL_XBC + s.CC
        s.COL_Q = s.COL_DT + 2 * s.H
        s.COL_Z = s.COL_Q + s.AW
        s.COL_GATE = s.COL_Z + s.DI
        s.IN = s.COL_GATE + 2 * D
        s.BPC = B // NCORES
        s.T = s.BPC * (L + S)
        s.DC = D // 128
        s.NM = s.BPC + 1
        s.CB = s.CC // 128
        assert (s.BPC * L) % 512 == 0 and S % 512 == 0
        assert s.E * 64 == 512

    def ctx_off(s, b):
        return b * s.L

    def lat_off(s, b):
        return s.BPC * s.L + b * s.S


class TT:
    def __init__(self, t, name):
        self.t = t
        self.name = name
        self.w = {}
        self.r = {}
        self.excl = False

    def __getitem__(self, k):
        return self.t[k]


ENGS = ["pe", "act", "dve", "pool", "sp"]
NDMA = 16


class Prog:
    def __init__(self, nc, stack):
        self.nc = nc
        self.stack = stack
        self.ops = {e: [] for e in ENGS}
        self.sems = []
        self.prog_sem = {}
        self.seq = {}
        for e in ["pe", "act", "dve", "pool"]:
            self.prog_sem[e] = self._new_sem("prog_" + e)
            self.seq[e] = 0
        self.waited = {e: {} for e in ENGS}
        self.dma_pool = {}
        self.dma_rr = {}
        self.sem_val = {}
        for q in ["sp", "pool", "act"]:
            self.dma_pool[q] = [self._new_sem("dma_%s_%d" % (q, i)) for i in range(NDMA)]
            self.dma_rr[q] = 0
            for sidx in self.dma_pool[q]:
                self.sem_val[sidx] = 0
        self.tiles = []
        self.nops = 0
        self.pending = []
        self.pool_cnt = 0

    def _new_sem(self, name):
        h = self.stack.enter_context(self.nc.semaphore(name))
        self.sems.append(h)
        return len(self.sems) - 1

    def track(self, t, name):
        tt = TT(t, name)
        self.tiles.append(tt)
        return tt

    def _deps(self, r, w):
        evs = {}
        for t in r:
            for k, v in t.w.items():
                if evs.get(k, 0) < v:
                    evs[k] = v
        for t in w:
            for k, v in t.w.items():
                if evs.get(k, 0) < v:
                    evs[k] = v
            for k, v in t.r.items():
                if evs.get(k, 0) < v:
                    evs[k] = v
        return evs

    def _emit_waits(self, eng, evs):
        wd = self.waited[eng]
        for k, v in evs.items():
            if wd.get(k, 0) < v:
                wd[k] = v
                self.ops[eng].append(("wait", k, v))

    def op(self, eng, fn, r=(), w=(), inc=True):
        w = list(w) + [t for t in r if t.excl and t not in w]
        r = [t for t in r if not t.excl]
        evs = self._deps(r, w)
        own = self.prog_sem[eng]
        if not SAME_ENGINE_SYNC or eng == "pe":
            evs.pop(own, None)
        else:
            raw = 0
            for t in r:
                v = t.w.get(own, 0)
                if v > raw:
                    raw = v
            if raw > 0:
                evs[own] = raw
            else:
                evs.pop(own, None)
        self._emit_waits(eng, evs)
        if inc:
            self.seq[eng] += 1
            ev = self.seq[eng]
        else:
            ev = self.seq[eng] + 1
        self.ops[eng].append(("op", fn, own if inc else None))
        self.nops += 1
        for t in r:
            if t.r.get(own, 0) < ev:
                t.r[own] = ev
        for t in w:
            t.w = {own: ev}
            t.r = {}

    def dma(self, q, out_ap, in_ap, r=(), w=(), slow=False):
        i = self.dma_rr[q]
        self.dma_rr[q] = (i + 1) % NDMA
        sem = self.dma_pool[q][i]
        prev = self.sem_val[sem]
        evs = self._deps(r, w)
        if prev > 0 and evs.get(sem, 0) < prev:
            evs[sem] = prev
        self._emit_waits(q, evs)
        val = prev + 16
        self.sem_val[sem] = val
        self.ops[q].append(("dma", out_ap, in_ap, sem, slow))
        self.nops += 1
        for t in r:
            if t.r.get(sem, 0) < val:
                t.r[sem] = val
        for t in w:
            t.w = {sem: val}
            t.r = {}
        if q == "pool" and self.pending and (r or w):
            self.pool_cnt += 1
            if self.pool_cnt % 6 == 0:
                dst, src = self.pending.pop(0)
                self.dma("pool", dst, src)

    def flush_pending(self):
        while self.pending:
            dst, src = self.pending.pop(0)
            self.dma("pool", dst, src)

    def barrier(self):
        evs = {}
        for e, sidx in self.prog_sem.items():
            if self.seq[e] > 0:
                evs[sidx] = self.seq[e]
        for sidx, v in self.sem_val.items():
            if v > 0:
                evs[sidx] = v
        for e in ENGS:
            ev2 = dict(evs)
            if e in self.prog_sem:
                ev2.pop(self.prog_sem[e], None)
            self._emit_waits(e, ev2)
        for t in self.tiles:
            t.w = {}
            t.r = {}

    def emit(self):
        nc = self.nc
        sems = self.sems

        def replay(eng_name):
            def body(e):
                for o in self.ops[eng_name]:
                    if o[0] == "wait":
                        e.wait_ge(sems[o[1]], o[2])
                    elif o[0] == "op":
                        ins = o[1](e)
                        if o[2] is not None:
                            ins.then_inc(sems[o[2]], 1)
                    else:
                        if o[4]:
                            e.dma_start(out=o[1], in_=o[2], allow_slow_non_contiguous=True).then_inc(sems[o[3]], 16)
                        else:
                            e.dma_start(out=o[1], in_=o[2]).then_inc(sems[o[3]], 16)
            return body

        with nc.Block() as block:
            block.tensor(replay("pe"))
            block.scalar(replay("act"))
            block.vector(replay("dve"))
            block.gpsimd(replay("pool"))
            block.sync(replay("sp"))


def bc_ap(ap, dims):
    a = ap.ap
    return bass.AP(ap.tensor, ap.offset, [list(a[0])] + [list(d) for d in dims])


class Builder:
    def __init__(self, cfg, nc, stack):
        self.c = cfg
        self.nc = nc
        self.stack = stack
        self.P = Prog(nc, stack)
        self._uid = 0
        c = cfg
        dt_in = lambda name, shape: nc.dram_tensor(name, shape, F32, kind="ExternalInput")
        self.xin = dt_in("xin", [c.D, c.T])
        self.cin = dt_in("cin", [128, c.DC * c.NM])
        self.w_ada = dt_in("w_ada", [c.DEPTH * c.D, 6 * c.D])
        self.w_in = dt_in("w_in", [c.DEPTH * c.D, c.IN])
        self.w_oa = dt_in("w_oa", [c.DEPTH * c.AW, c.D])
        self.w_os = dt_in("w_os", [c.DEPTH * c.DI, c.D])
        self.w_out = dt_in("w_out", [c.DEPTH * c.D, c.D])
        self.w_ff1 = dt_in("w_ff1", [c.DEPTH * c.D, c.DFF])
        self.w_ff2 = dt_in("w_ff2", [c.DEPTH * c.DFF, c.D])
        self.b_ada = dt_in("b_ada", [128, c.DEPTH * 6 * c.DC])
        self.gn = dt_in("gn", [128, (2 * c.DEPTH + 1) * c.DC])
        self.convp = dt_in("convp", [128, c.DEPTH * c.CB * 6])
        self.rows = dt_in("rows", [c.DEPTH, c.NH + 6 * c.H + c.DI])
        self.consts = dt_in("consts", [128, 7 * 128])
        self.rope = dt_in("rope", [128, 2 * c.S])
        self.outT = nc.dram_tensor("outT", [c.D, c.BPC * c.S], F32, kind="ExternalOutput")
        sc = lambda name, shape, dt: nc.dram_tensor(name, shape, dt)
        self.wbs = []
        for i_ in range(2):
            self.wbs.append({
                "in": sc("wb_in%d" % i_, [c.D, c.IN], BF16), "oa": sc("wb_oa%d" % i_, [c.AW, c.D], BF16),
                "os": sc("wb_os%d" % i_, [c.DI, c.D], BF16), "out": sc("wb_out%d" % i_, [c.D, c.D], BF16),
                "ff1": sc("wb_ff1%d" % i_, [c.D, c.DFF], BF16), "ff2": sc("wb_ff2%d" % i_, [c.DFF, c.D], BF16)})
        self.xT = sc("xT", [c.D, c.T], F32)
        self.qT = sc("qT", [c.AW, c.T], BF16)
        self.kT = sc("kT", [c.KW, c.T], BF16)
        self.vtm = sc("vtm", [c.T, c.KW], BF16)
        self.xbcT = sc("xbcT", [c.CC, c.T], BF16)
        self.dtv = sc("dtv", [c.T, 2 * c.H], F32)
        self.sz = sc("sz", [c.T, c.DI], BF16)
        self.gT = sc("gT", [2 * c.D, c.T], BF16)
        self.xs = sc("xs", [c.T, c.DI], BF16)
        self.Btm = sc("Btm", [c.T, c.GN], BF16)
        self.BT = sc("BT", [c.GN, c.T], BF16)
        self.CT = sc("CT", [c.GN, c.T], BF16)
        self.yd = [sc("yf", [c.T, c.DI], F32), sc("yb", [c.T, c.DI], F32)]
        self.attT = sc("attT", [c.AW, c.T], BF16)
        self.ssdT = sc("ssdT", [c.DI, c.T], BF16)
        self.banks = [self.P.track(nc.alloc_psum_tensor("bank%d" % i, [128, 512], F32), "bank%d" % i) for i in range(8)]
        self.bank_rr = 0
        for bk_ in self.banks:
            bk_.excl = True

    def sb(self, stack, shape, dt, name):
        self._uid += 1
        t = stack.enter_context(self.nc.sbuf_tensor("%s_%d" % (name, self._uid), shape, dt))
        return self.P.track(t, name)

    def mm(self, ps_ap, lhsT, rhs, start, stop, r, w, inc=None):
        if inc is None:
            inc = stop
        self.P.op("pe", lambda e: e.matmul(ps_ap, lhsT=lhsT, rhs=rhs, start=start, stop=stop), r=r, w=w, inc=inc)

    def act(self, out, in_, func, r, w, bias=None, scale=None, accum=None):
        kw = {}
        if bias is not None:
            kw["bias"] = bias
        if scale is not None:
            kw["scale"] = scale
        if accum is not None:
            kw["accum_out"] = accum
        self.P.op("act", lambda e: e.activation(out=out, in_=in_, func=func, **kw), r=r, w=w)

    def tt(self, eng, out, in0, in1, op, r, w):
        self.P.op(eng, lambda e: e.tensor_tensor(out=out, in0=in0, in1=in1, op=op), r=r, w=w)

    def ts(self, eng, out, in0, s1, s2, op0, op1, r, w):
        if s2 is None:
            self.P.op(eng, lambda e: e.tensor_single_scalar(out=out, in_=in0, scalar=s1, op=op0), r=r, w=w)
        else:
            self.P.op(eng, lambda e: e.tensor_scalar(out=out, in0=in0, scalar1=s1, scalar2=s2, op0=op0, op1=op1), r=r, w=w)

    def stt(self, eng, out, in0, scalar, in1, op0, op1, r, w):
        self.P.op(eng, lambda e: e.scalar_tensor_tensor(out=out, in0=in0, scalar=scalar, in1=in1, op0=op0, op1=op1), r=r, w=w)

    def cp(self, eng, out, in_, r, w):
        if eng == "act":
            self.P.op("act", lambda e: e.copy(out=out, in_=in_), r=r, w=w)
        else:
            self.P.op(eng, lambda e: e.tensor_copy(out=out, in_=in_), r=r, w=w)

    def rsqrt(self, out, in_, inv_n, r, wt):
        self.act(out, in_, AF.Sqrt, bias=self.eps_c[:, 0:1], scale=inv_n, r=list(r) + [self.eps_c], w=[wt])
        self.P.op("dve", lambda e: e.reciprocal(out=out, in_=out), r=[wt], w=[wt])

    def memset(self, eng, ap, val, w):
        self.P.op(eng, lambda e: e.memset(ap, val), r=(), w=w)

    def load(self, out_ap, in_ap, w, q="sp", slow=False):
        self.P.dma(q, out_ap, in_ap, r=(), w=w, slow=slow)

    def store(self, out_ap, in_ap, r, q="pool", slow=False):
        self.P.dma(q, out_ap, in_ap, r=r, w=(), slow=slow)

    def build(self):
        from contextlib import ExitStack
        c = self.c
        P = self.P
        nc = self.nc
        with ExitStack() as gs:
            self.gs = gs
            self.setup_consts(gs)
            self.phase_mod(gs)
            step = max(128, (1 << 22) // (c.T * 4) // 128 * 128)
            for r0 in range(0, c.D, step):
                r1 = min(c.D, r0 + step)
                P.dma("sp", self.xT[r0:r1, :], self.xin[r0:r1, :])
            P.barrier()
            steps = []
            for l in range(c.DEPTH):
                last = l == c.DEPTH - 1
                if l == 0:
                    steps.append(lambda l=l: (self.issue_casts(self.cast_list(0)), self.layer_consts(l)))
                else:
                    steps.append(lambda l=l: (P.flush_pending(), self.layer_consts(l)))

                def p1(l=l, last=last):
                    if not last:
                        P.pending.extend(self.cast_list(l + 1))
                    self.phase1(l, last)
                steps.append(p1)
                steps.append(lambda l=l, last=last: self.phase_attn(l, last))
                steps.append(lambda l=l, last=last: self.phase_conv(l))
                steps.append(lambda l=l, last=last: self.phase_ssd(l, last))
                steps.append(lambda l=l, last=last: self.phase4(l, last))
                steps.append(lambda l=l, last=last: self.phase5a(l, last))
                steps.append(lambda l=l, last=last: self.phase5b(l, last))
            for i, f in enumerate(steps):
                if DEBUG_STOP is not None and i >= DEBUG_STOP:
                    break
                f()
                P.barrier()
            P.emit()

    def setup_consts(self, gs):
        c = self.c
        cf = self.sb(gs, [128, 7 * 128], F32, "constf")
        self.load(cf[:], self.consts[:, :], w=[cf])
        self.cf = cf
        cbf = self.sb(gs, [128, 7 * 128], BF16, "constb")
        self.cp("dve", cbf[:], cf[:], r=[cf], w=[cbf])
        self.cb16 = cbf
        k = lambda t, i: t[:, i * 128:(i + 1) * 128]
        self.ident_b = k(cbf, 0)
        self.U_f = [k(cf, 1), k(cf, 2)]
        self.U_b = [k(cbf, 1), k(cbf, 2)]
        self.A_f = [k(cf, 3), k(cf, 4)]
        self.ones_f = k(cf, 5)
        self.ones_b = k(cbf, 5)
        self.R_b = k(cbf, 6)
        self.mod = self.sb(gs, [128, c.DEPTH * 6 * c.DC * c.NM], F32, "mod")
        self.s1 = self.sb(gs, [128, c.DEPTH * c.DC * c.NM], F32, "s1")
        self.s2 = self.sb(gs, [128, c.DEPTH * c.DC * c.NM], F32, "s2")
        self.gnt = self.sb(gs, [128, (2 * c.DEPTH + 1) * c.DC], F32, "gnt")
        self.load(self.gnt[:], self.gn[:, :], w=[self.gnt])
        self.zero_c = self.sb(gs, [128, 1], F32, "zeroc")
        self.memset("dve", self.zero_c[:], 0.0, w=[self.zero_c])
        self.eps_c = self.sb(gs, [128, 1], F32, "epsc")
        self.memset("dve", self.eps_c[:], EPS, w=[self.eps_c])
        self.one_c = self.sb(gs, [128, 1], F32, "onec")
        self.memset("dve", self.one_c[:], 1.0, w=[self.one_c])

    def mod_ap(self, l, j, dc, m):
        c = self.c
        o = ((l * 6 + j) * c.DC + dc) * c.NM + m
        return self.mod[:, o:o + 1]

    def phase_mod(self, gs):
        from contextlib import ExitStack
        c = self.c
        P = self.P
        with ExitStack() as st:
            ct = self.sb(st, [128, c.DC * c.NM], F32, "ct")
            sct = self.sb(st, [128, c.DC * c.NM], F32, "sct")
            bt = self.sb(st, [128, c.DEPTH * 6 * c.DC], F32, "bt")
            self.load(ct[:], self.cin[:, :], w=[ct])
            self.load(bt[:], self.b_ada[:, :], w=[bt])
            self.act(sct[:], ct[:], AF.Silu, r=[ct], w=[sct])
            KC = c.DC
            ws = [self.sb(st, [128, KC * 512], F32, "wada%d" % i) for i in range(2)]
            wsb = [self.sb(st, [128, KC * 512], BF16, "wadab%d" % i) for i in range(2)]
            sctb = self.sb(st, [128, c.DC * c.NM], BF16, "sctb")
            self.cp("dve", sctb[:], sct[:], r=[sct], w=[sctb])
            wi = 0
            ncg = (6 * c.D) // 512
            half = (KC * 512) // 2
            for l in range(c.DEPTH):
                for cg in range(ncg):
                    wt = ws[wi % 2]
                    wtb = wsb[wi % 2]
                    wi += 1
                    src = self.w_ada[l * c.D:(l + 1) * c.D, cg * 512:(cg + 1) * 512].rearrange("(kc p) n -> p kc n", p=128)
                    self.load(wt[:].rearrange("p (kc n) -> p kc n", n=512), src, w=[wt])
                    self.cp("dve", wtb[:, 0:half], wt[:, 0:half], r=[wt], w=[wtb])
                    self.cp("act", wtb[:, half:], wt[:, half:], r=[wt], w=[wtb])
                    for cb in range(4):
                        bk = self.banks[self.bank_rr % 8]
                        self.bank_rr += 1
                        for kc in range(KC):
                            self.mm(bk[:, 0:c.NM], wtb[:, kc * 512 + cb * 128: kc * 512 + (cb + 1) * 128],
                                    sctb[:, kc * c.NM:(kc + 1) * c.NM], kc == 0, kc == KC - 1,
                                    r=[wtb, sctb], w=[bk])
                        t = cg * 4 + cb
                        o = (l * 6 * c.DC + t) * c.NM
                        self.act(self.mod[:, o:o + c.NM], bk[:, 0:c.NM], AF.Identity,
                                 bias=bt[:, l * 6 * c.DC + t: l * 6 * c.DC + t + 1], r=[bk, bt], w=[self.mod])
            for l in range(c.DEPTH):
                for (dst, j, gi) in ((self.s1, 1, l), (self.s2, 4, c.DEPTH + l)):
                    o = (l * 6 + j) * c.DC * c.NM
                    od = l * c.DC * c.NM
                    n = c.DC * c.NM
                    self.ts("dve", dst[:, od:od + n], self.mod[:, o:o + n], 1.0, None, ALU.add, None,
                            r=[self.mod], w=[dst])
                    g = self.gnt[:, gi * c.DC:(gi + 1) * c.DC]
                    gb = bc_ap(g, [[1, c.DC], [0, c.NM]])
                    d3 = dst[:, od:od + n].rearrange("p (a b) -> p a b", b=c.NM)
                    self.tt("dve", d3, d3, gb, ALU.mult, r=[dst, self.gnt], w=[dst])
            P.barrier()

    def cast_list(self, l):
        c = self.c
        wb = self.wbs[l % 2]
        out = []
        for (src, dst, R, C) in ((self.w_in, wb["in"], c.D, c.IN), (self.w_oa, wb["oa"], c.AW, c.D),
                                 (self.w_os, wb["os"], c.DI, c.D), (self.w_out, wb["out"], c.D, c.D),
                                 (self.w_ff1, wb["ff1"], c.D, c.DFF), (self.w_ff2, wb["ff2"], c.DFF, c.D)):
            step = max(128, ((1 << 21) // C) // 128 * 128)
            for r0 in range(0, R, step):
                r1 = min(R, r0 + step)
                out.append((dst[r0:r1, :], src[l * R + r0: l * R + r1, :]))
        return out

    def issue_casts(self, lst):
        for (dst, src) in lst:
            self.P.dma("pool", dst, src)

    def layer_consts(self, l):
        c = self.c
        if l == 0:
            gs = self.gs
            self.sinkexp = self.sb(gs, [128, c.NH], F32, "sinkexp")
            self.dtb = self.sb(gs, [128, 2 * c.H], F32, "dtb")
            self.aneg = self.sb(gs, [128, 2 * c.H], F32, "aneg")
            self.dsk = self.sb(gs, [128, 2 * c.H], F32, "dsk")
            self.dsum = self.sb(gs, [128, c.H], F32, "dsum")
            self.cvp = self.sb(gs, [128, c.CB * 6], F32, "cvp")
        rows = self.rows
        H2 = 2 * c.H

        def brow(o, n):
            return bass.AP(rows, l * (c.NH + 6 * c.H + c.DI) + o, [[0, 128], [1, n]])
        self.load(self.sinkexp[:], brow(0, c.NH), w=[self.sinkexp])
        self.load(self.dtb[:], brow(c.NH, H2), w=[self.dtb])
        self.load(self.aneg[:], brow(c.NH + H2, H2), w=[self.aneg])
        self.load(self.dsk[:], brow(c.NH + 2 * H2, H2), w=[self.dsk])
        self.load(self.cvp[:], self.convp[:, l * c.CB * 6:(l + 1) * c.CB * 6], w=[self.cvp])
        self.act(self.sinkexp[:], self.sinkexp[:], AF.Exp, r=[self.sinkexp], w=[self.sinkexp])
        self.act(self.aneg[:], self.aneg[:], AF.Exp, r=[self.aneg], w=[self.aneg])
        self.ts("dve", self.aneg[:], self.aneg[:], -1.0, None, ALU.mult, None, r=[self.aneg], w=[self.aneg])
        self.tt("dve", self.dsum[:], self.dsk[:, 0:c.H], self.dsk[:, c.H:H2], ALU.add, r=[self.dsk], w=[self.dsum])
        self.gssd_off = l * (c.NH + 6 * c.H + c.DI) + c.NH + 3 * H2

    def tiles512(self):
        c = self.c
        out = []
        for t0 in range(0, c.BPC * c.L, 512):
            out.append((t0, "ctx", None, None))
        for b in range(c.BPC):
            for s0 in range(0, c.S, 512):
                out.append((c.lat_off(b) + s0, "lat", b, s0))
        return out

    def norm_tile(self, st, xt, l_idx, s_t, sh_fn, m, out_t, out_is_f32=False):
        c = self.c
        DC = c.DC
        N = 512
        bk = self.banks[6]
        if DEBUG_NORM == 0:
            return
        for dc in range(DC):
            sq = self.sqs[dc % 2]
            self.act(sq[:], xt[:, dc * N:(dc + 1) * N], AF.Square, r=[xt], w=[sq])
            if DEBUG_NORM == -1:
                continue
            self.mm(bk[:, 0:N], self.ones_f, sq[:], dc == 0, dc == DC - 1, r=[sq, self.cf], w=[bk], inc=True)
        rstd = self.rstd
        if DEBUG_NORM == 1 or DEBUG_NORM == -1:
            return
        if DEBUG_NORM == 2:
            self.act(rstd[:], bk[:, 0:N], AF.Sqrt, bias=self.eps_c[:, 0:1], scale=1.0 / c.D, r=[bk, self.eps_c], w=[rstd])
            return
        self.rsqrt(rstd[:], bk[:, 0:N], 1.0 / c.D, [bk], rstd)
        if DEBUG_NORM == 3:
            return
        for dc in range(DC):
            tmp = self.ntmp[dc % 2]
            o = (l_idx * DC + dc) * c.NM + m
            self.stt("dve", tmp[:], xt[:, dc * N:(dc + 1) * N], s_t[:, o:o + 1], rstd[:], ALU.mult, ALU.mult,
                     r=[xt, s_t, rstd], w=[tmp])
            if DEBUG_NORM == 4:
                continue
            sh = sh_fn(dc)
            self.act(out_t[:, dc * N:(dc + 1) * N], tmp[:], AF.Identity, bias=sh, r=[tmp, self.mod, self.zero_c], w=[out_t])

    def alloc_norm_tmps(self, st):
        self.sqs = [self.sb(st, [128, 512], F32, "sq%d" % i) for i in range(2)]
        self.ntmp = [self.sb(st, [128, 512], F32, "ntmp%d" % i) for i in range(2)]
        self.rstd = self.sb(st, [128, 512], F32, "rstd")

    def dense(self, wb, R, c0, ncols, act_t, act_kc_ap, N, banks, epilogue, wslots):
        KT = R // 128
        KC = min(16, KT)
        nkg = KT // KC
        bi = 0
        for g0 in range(0, ncols, 512):
            gw = min(512, ncols - g0)
            ncb = gw // 128
            bks = []
            for i in range(ncb):
                bks.append(banks[self._dense_rr % len(banks)])
                self._dense_rr += 1
            for kg in range(nkg):
                wt = wslots[self._ws_rr % len(wslots)]
                self._ws_rr += 1
                src = wb[kg * KC * 128:(kg + 1) * KC * 128, c0 + g0: c0 + g0 + gw].rearrange("(kc p) n -> p kc n", p=128)
                dst = wt[:, 0:KC * gw].rearrange("p (kc n) -> p kc n", n=gw)
                self.load(dst, src, w=[wt])
                for cb in range(ncb):
                    for kc in range(KC):
                        first = kg == 0 and kc == 0
                        lastk = kg == nkg - 1 and kc == KC - 1
                        self.mm(bks[cb][:, 0:N], wt[:, kc * gw + cb * 128: kc * gw + (cb + 1) * 128],
                                act_kc_ap(kg * KC + kc), first, lastk, r=[wt, act_t], w=[bks[cb]], inc=(kc == KC - 1))
            for cb in range(ncb):
                epilogue(g0 // 128 + cb, bks[cb])

    def dense_tm(self, wb, R, c0, ncols, act_t, N, banks, epilogue, wslots):
        KT = R // 128
        assert KT <= 16 and ncols <= 512
        wt = wslots[self._ws_rr % len(wslots)]
        self._ws_rr += 1
        src = wb[0:R, c0:c0 + ncols].rearrange("(kc p) n -> p kc n", p=128)
        dst = wt[:, 0:KT * ncols].rearrange("p (kc n) -> p kc n", n=ncols)
        self.load(dst, src, w=[wt])
        for s in range(N // 128):
            bk = banks[self._dense_rr % len(banks)]
            self._dense_rr += 1
            for kc in range(KT):
                self.mm(bk[:, 0:ncols], act_t[:, kc * N + s * 128: kc * N + (s + 1) * 128],
                        wt[:, kc * ncols:(kc + 1) * ncols], kc == 0, kc == KT - 1, r=[wt, act_t], w=[bk])
            epilogue(s, bk)

    def phase1(self, l, last):
        from contextlib import ExitStack
        c = self.c
        N = 512
        self._dense_rr = 0
        self._ws_rr = 0
        with ExitStack() as st:
            self.alloc_norm_tmps(st)
            xt = self.sb(st, [128, c.DC * N], F32, "xt")
            h = self.sb(st, [128, c.DC * N], BF16, "h")
            wslots = [self.sb(st, [128, min(16, c.DC) * 512], BF16, "ws%d" % i) for i in range(3)]
            fm = [self.sb(st, [128, 4 * N], BF16, "fm%d" % i) for i in range(3)]
            tm = [self.sb(st, [128, 512], BF16, "tm%d" % i) for i in range(3)]
            cos_t = self.sb(st, [128, N], F32, "cos")
            sin_t = self.sb(st, [128, N], F32, "sin")
            t1 = [self.sb(st, [128, N], F32, "t1_%d" % i) for i in range(2)]
            t2 = [self.sb(st, [128, N], F32, "t2_%d" % i) for i in range(2)]
            qb = [self.sb(st, [128, N], BF16, "qb%d" % i) for i in range(2)]
            dtx = [self.sb(st, [128, 2 * c.H], F32, "dtx%d" % i) for i in range(2)]
            dta = [self.sb(st, [128, 2 * c.H], F32, "dta%d" % i) for i in range(2)]
            dtl = [self.sb(st, [128, 2 * c.H], F32, "dtl%d" % i) for i in range(2)]
            dto = [self.sb(st, [128, 2 * c.H], F32, "dto%d" % i) for i in range(2)]
            dbanks = self.banks[0:6]
            rr = {"fm": 0, "tm": 0, "rope": 0, "dt": 0}
            wl = self.wbs[l % 2]["in"]
            for ti_, (t0, kind, b, s0) in enumerate(self.tiles512()):
                if DEBUG_TILES is not None and ti_ >= DEBUG_TILES:
                    break
                is_ctx = kind == "ctx"
                m = c.BPC if is_ctx else b
                skip_q = last and is_ctx
                self.load(xt[:].rearrange("p (dc n) -> p dc n", n=N),
                          self.xT.ap().rearrange("(dc p) t -> p dc t", p=128)[:, :, t0:t0 + N], w=[xt])
                self.norm_tile(st, xt, l, self.s1, lambda dc: self.mod_ap(l, 0, dc, m), m, h)
                if not is_ctx:
                    self.load(cos_t[:], self.rope[:, s0:s0 + N], w=[cos_t])
                    self.load(sin_t[:], self.rope[:, c.S + s0: c.S + s0 + N], w=[sin_t])
                hk = lambda kc: h[:, kc * N:(kc + 1) * N]

                def fm_family(c0, ncols, dst, kindf):
                    state = {}

                    def epi(cbi, bk):
                        j = cbi % 4
                        if j == 0:
                            state["o"] = fm[rr["fm"] % 3]
                            rr["fm"] += 1
                        o = state["o"]
                        oap = o[:, j * N:(j + 1) * N]
                        if kindf == "rope" and not is_ctx and DEBUG_ROPE != 0:
                            i = rr["rope"] % 2
                            rr["rope"] += 1
                            self.cp("act", qb[i][:], bk[:, 0:N], r=[bk], w=[qb[i]])
                            rb = self.banks[7]
                            self.mm(rb[:, 0:N], self.R_b, qb[i][:], True, True, r=[qb[i], self.cb16], w=[rb])
                            if DEBUG_ROPE == 1:
                                self.cp("act", oap, rb[:, 0:N], r=[rb], w=[o])
                                return
                            self.tt("dve", t1[i][:], bk[:, 0:N], cos_t[:], ALU.mult, r=[bk, cos_t, qb[i]], w=[t1[i]])
                            if DEBUG_ROPE == 2:
                                self.cp("act", oap, t1[i][:], r=[t1[i]], w=[o])
                                return
                            self.tt("dve", t2[i][:], rb[:, 0:N], sin_t[:], ALU.mult, r=[rb, sin_t], w=[t2[i]])
                            if DEBUG_ROPE == 3:
                                self.cp("act", oap, t2[i][:], r=[t2[i], t1[i]], w=[o])
                                return
                            self.tt(POOL_ENG, oap, t1[i][:], t2[i][:], ALU.add, r=[t1[i], t2[i]], w=[o])
                        elif kindf == "sigmoid":
                            self.act(oap, bk[:, 0:N], AF.Sigmoid, r=[bk], w=[o])
                        else:
                            self.cp("act", oap, bk[:, 0:N], r=[bk], w=[o])
                        nb = min(4, ncols // 128 - (cbi // 4) * 4)
                        if j == nb - 1:
                            r0 = (cbi // 4) * 512
                            d = dst[r0:r0 + nb * 128, t0:t0 + N].rearrange("(j p) t -> p j t", p=128)
                            self.store(d, o[:, 0:nb * N].rearrange("p (j t) -> p j t", t=N), r=[o])
                    self.dense(wl, c.D, c0, ncols, h, hk, N, dbanks, epi, wslots)

                def tm_family(c0, ncols, dst, func):
                    for g0 in range(0, ncols, 512):
                        gw = min(512, ncols - g0)

                        def epi(s, bk, g0=g0, gw=gw):
                            o = tm[rr["tm"] % 3]
                            rr["tm"] += 1
                            if func is None:
                                self.cp("act", o[:, 0:gw], bk[:, 0:gw], r=[bk], w=[o])
                            else:
                                self.act(o[:, 0:gw], bk[:, 0:gw], func, r=[bk], w=[o])
                            self.store(dst[t0 + s * 128: t0 + (s + 1) * 128, g0:g0 + gw], o[:, 0:gw], r=[o])
                        self.dense_tm(wl, c.D, c0 + g0, gw, h, N, dbanks, epi, wslots)

                if DEBUG_SUB == 0:
                    return
                fm_family(c.COL_K, c.KW, self.kT, "rope")
                if DEBUG_SUB == 1:
                    return
                tm_family(c.COL_V, c.KW, self.vtm, None)
                if DEBUG_SUB == 2:
                    return
                fm_family(c.COL_XBC, c.CC, self.xbcT, "copy")
                if DEBUG_SUB == 3:
                    return

                H2 = 2 * c.H

                def epi_dt(s, bk):
                    i = rr["dt"] % 2
                    rr["dt"] += 1
                    self.tt("dve", dtx[i][:], bk[:, 0:H2], self.dtb[:], ALU.add, r=[bk, self.dtb], w=[dtx[i]])
                    self.act(dta[i][:], dtx[i][:], AF.Abs, r=[dtx[i]], w=[dta[i]])
                    self.act(dtl[i][:], dta[i][:], AF.Exp, scale=-1.0, r=[dta[i]], w=[dtl[i]])
                    self.act(dtl[i][:], dtl[i][:], AF.Ln, bias=self.one_c[:, 0:1], r=[dtl[i], self.one_c], w=[dtl[i]])
                    self.stt("dve", dto[i][:], dtx[i][:], 0.0, dtl[i][:], ALU.max, ALU.add, r=[dtx[i], dtl[i]], w=[dto[i]])
                    self.store(self.dtv[t0 + s * 128: t0 + (s + 1) * 128, :], dto[i][:], r=[dto[i]])
                self.dense_tm(wl, c.D, c.COL_DT, H2, h, N, dbanks, epi_dt, wslots)
                if DEBUG_SUB == 4:
                    return
                if not skip_q:
                    fm_family(c.COL_Q, c.AW, self.qT, "rope")
                    if DEBUG_SUB == 5:
                        continue
                    tm_family(c.COL_Z, c.DI, self.sz, AF.Silu)
                    if DEBUG_SUB == 6:
                        continue
                    fm_family(c.COL_GATE, 2 * c.D, self.gT, "sigmoid")

    def phase_attn(self, l, last):
        from contextlib import ExitStack
        c = self.c
        S, L = c.S, c.L
        scale = 1.0 / math.sqrt(128.0)
        NBL = S // 128
        NBC = L // 128
        with ExitStack() as st:
            kc_t = [self.sb(st, [128, L], BF16, "kc%d" % i) for i in range(2)]
            kl_t = [self.sb(st, [128, S], BF16, "kl%d" % i) for i in range(2)]
            vc_t = [self.sb(st, [128, NBC * 128], BF16, "vc%d" % i) for i in range(2)]
            vl_t = [self.sb(st, [128, NBL * 128], BF16, "vl%d" % i) for i in range(2)]
            q_t = [self.sb(st, [128, S], BF16, "q%d" % i) for i in range(2)]
            qc_t = [self.sb(st, [128, L], BF16, "qc%d" % i) for i in range(2)]
            o_t = [self.sb(st, [128, S], BF16, "o%d" % i) for i in range(2)]
            oc_t = [self.sb(st, [128, L], BF16, "oc%d" % i) for i in range(2)]
            pT = [self.sb(st, [128, 512], BF16, "pT%d" % i) for i in range(4)]
            rec = [self.sb(st, [128, 512], F32, "rec%d" % i) for i in range(2)]
            sbanks = self.banks[0:4]
            obanks = [(self.banks[4], self.banks[5]), (self.banks[6], self.banks[7])]
            cnt = {"s": 0, "o": 0, "p": 0, "g": 0, "h": 0}

            def score_block(kt, kap, qt, qap, nq, mask_specs):
                sb_ = sbanks[cnt["s"] % 4]
                cnt["s"] += 1
                self.mm(sb_[:, 0:nq], kap, qap, True, True, r=[kt, qt], w=[sb_])
                p = pT[cnt["p"] % 4]
                cnt["p"] += 1
                self.act(p[:, 0:nq], sb_[:, 0:nq], AF.Exp, scale=scale, r=[sb_], w=[p])
                for (off, which) in mask_specs:
                    self.tt("dve", p[:, off:off + 128], p[:, off:off + 128], self.U_b[which], ALU.mult,
                            r=[p, self.cb16], w=[p])
                return p

            def finish(ob, db, h, ot, ocol, nq):
                r_ = rec[cnt["o"] % 2]
                self.ts("dve", r_[:, 0:nq], db[:, 0:nq], self.sinkexp[:, h:h + 1], None, ALU.add, None,
                        r=[db, self.sinkexp], w=[r_])
                self.P.op("dve", lambda e: e.reciprocal(out=r_[:, 0:nq], in_=r_[:, 0:nq]), r=[r_], w=[r_])
                self.tt("dve", ot[:, ocol:ocol + nq], ob[:, 0:nq], r_[:, 0:nq], ALU.mult, r=[ob, r_], w=[ot])

            for b in range(c.BPC):
                co = c.ctx_off(b)
                lo = c.lat_off(b)
                for g in range(c.NKV):
                    gi = cnt["g"] % 2
                    cnt["g"] += 1
                    kc, kl, vc, vl = kc_t[gi], kl_t[gi], vc_t[gi], vl_t[gi]
                    self.load(kc[:], self.kT[g * 128:(g + 1) * 128, co:co + L], w=[kc])
                    self.load(kl[:], self.kT[g * 128:(g + 1) * 128, lo:lo + S], w=[kl])
                    self.load(vc[:].rearrange("p (n d) -> p n d", d=128),
                              self.vtm[co:co + L, g * 128:(g + 1) * 128].rearrange("(n p) d -> p n d", p=128), w=[vc])
                    self.load(vl[:].rearrange("p (n d) -> p n d", d=128),
                              self.vtm[lo:lo + S, g * 128:(g + 1) * 128].rearrange("(n p) d -> p n d", p=128), w=[vl])
                    for hh in range(c.REP):
                        h = g * c.REP + hh
                        hi = cnt["h"] % 2
                        cnt["h"] += 1
                        q, qc, ot, oc = q_t[hi], qc_t[hi], o_t[hi], oc_t[hi]
                        self.load(q[:], self.qT[h * 128:(h + 1) * 128, lo:lo + S], w=[q])
                        for qg in range(S // 512):
                            ob, db = obanks[cnt["o"] % 2]
                            blocks = []
                            for j in range(NBC):
                                blocks.append(("c", j, 0, 512, []))
                            for j in range(qg * 4 - 1, qg * 4 + 5):
                                if j < 0 or j >= NBL:
                                    continue
                                qlo = max(j - 1, qg * 4)
                                qhi = min(j + 1, qg * 4 + 3)
                                nq = (qhi - qlo + 1) * 128
                                masks = []
                                for qb_ in range(qlo, qhi + 1):
                                    if qb_ == j - 1:
                                        masks.append(((qb_ - qlo) * 128, 0))
                                    elif qb_ == j + 1:
                                        masks.append(((qb_ - qlo) * 128, 1))
                                blocks.append(("l", j, qlo, nq, masks))

                            def do_score(bl):
                                kind_, j, qlo, nq, masks = bl
                                if kind_ == "c":
                                    return score_block(kc, kc[:, j * 128:(j + 1) * 128], q, q[:, qg * 512:(qg + 1) * 512], 512, [])
                                return score_block(kl, kl[:, j * 128:(j + 1) * 128], q, q[:, qlo * 128: qlo * 128 + nq], nq, masks)

                            pcur = do_score(blocks[0])
                            for bi_, bl in enumerate(blocks):
                                pnext = do_score(blocks[bi_ + 1]) if bi_ + 1 < len(blocks) else None
                                kind_, j, qlo, nq, masks = bl
                                if kind_ == "c":
                                    vt, vap, c0 = vc, vc[:, j * 128:(j + 1) * 128], 0
                                else:
                                    vt, vap, c0 = vl, vl[:, j * 128:(j + 1) * 128], (qlo - qg * 4) * 128
                                first = bi_ == 0
                                self.mm(ob[:, c0:c0 + nq], vap, pcur[:, 0:nq], first, False, r=[vt, pcur], w=[ob], inc=True)
                                self.mm(db[:, c0:c0 + nq], self.ones_b, pcur[:, 0:nq], first, False, r=[self.cb16, pcur], w=[db], inc=True)
                                pcur = pnext
                            finish(ob, db, h, ot, qg * 512, 512)
                            cnt["o"] += 1
                        self.store(self.attT[h * 128:(h + 1) * 128, lo:lo + S], ot[:], r=[ot])
                        if not last:
                            self.load(qc[:], self.qT[h * 128:(h + 1) * 128, co:co + L], w=[qc])
                            ob, db = obanks[cnt["o"] % 2]
                            for j in range(NBC):
                                p = score_block(kc, kc[:, j * 128:(j + 1) * 128], qc, qc[:, 0:L], L, [])
                                self.mm(ob[:, 0:L], vc[:, j * 128:(j + 1) * 128], p[:, 0:L], j == 0, False, r=[vc, p], w=[ob], inc=True)
                                self.mm(db[:, 0:L], self.ones_b, p[:, 0:L], j == 0, False, r=[self.cb16, p], w=[db], inc=True)
                            finish(ob, db, h, oc, 0, L)
                            cnt["o"] += 1
                            self.store(self.attT[h * 128:(h + 1) * 128, co:co + L], oc[:], r=[oc])

    def phase_conv(self, l):
        from contextlib import ExitStack
        c = self.c
        XB = c.DI // 128
        GB = c.GN // 128
        with ExitStack() as st:
            dg = self.sb(st, [128, c.CB * 5 * 128], BF16, "dg")
            xin = [self.sb(st, [128, 4 * 516], BF16, "cxin%d" % i) for i in range(2)]
            ysb = [self.sb(st, [128, 4 * 512], BF16, "cy%d" % i) for i in range(2)]
            xs_tm = self.sb(st, [128, 4 * c.DI], BF16, "xs_tm")
            b_tm = self.sb(st, [128, 4 * c.GN], BF16, "b_tm")
            ident_f = self.cf[:, 0:128]
            q = 0
            for cb in range(c.CB):
                for k in range(5):
                    wv = self.cvp[:, cb * 6 + k: cb * 6 + k + 1]
                    o = (cb * 5 + k) * 128
                    self.act(dg[:, o:o + 128], ident_f, AF.Identity, scale=wv, r=[self.cf, self.cvp], w=[dg])
            cps = self.banks[0:4]
            tps = self.banks[4:8]
            cnt = {"x": 0, "a": 0, "y": 0, "t": 0}
            chunks = []
            for b in range(c.BPC):
                for s0 in range(0, c.L, 512):
                    Lc = min(512, c.L - s0)
                    chunks.append((c.ctx_off(b) + s0, Lc, s0 == 0, s0 + Lc == c.L))
            for b in range(c.BPC):
                for s0 in range(0, c.S, 512):
                    chunks.append((c.lat_off(b) + s0, 512, s0 == 0, s0 + 512 == c.S))
            for (t0, Lc, zl, zr) in chunks:
                nt = Lc // 128
                for cg in range(0, c.CB, 4):
                    ncb = min(4, c.CB - cg)
                    xi = xin[cnt["x"] % 2]
                    cnt["x"] += 1
                    x3 = xi[:].rearrange("p (j t) -> p j t", t=516)
                    a = t0 - 2 if not zl else t0
                    bnd = t0 + Lc + 2 if not zr else t0 + Lc
                    oa = 0 if not zl else 2
                    if zl:
                        self.memset("dve", x3[:, 0:ncb, 0:2], 0.0, w=[xi])
                    if zr:
                        self.memset("dve", x3[:, 0:ncb, Lc + 2:Lc + 4], 0.0, w=[xi])
                    src = self.xbcT[cg * 128:(cg + ncb) * 128, a:bnd].rearrange("(j p) t -> p j t", p=128)
                    self.P.dma("sp", x3[:, 0:ncb, oa:oa + (bnd - a)], src, r=(), w=[xi])
                    yt = ysb[cnt["y"] % 2]
                    cnt["y"] += 1
                    for j in range(ncb):
                        cb = cg + j
                        pb = cps[cnt["a"] % 4]
                        cnt["a"] += 1
                        for k in range(5):
                            o = (cb * 5 + k) * 128
                            self.mm(pb[:, 0:Lc], dg[:, o:o + 128], x3[:, j, k:k + Lc], k == 0, k == 4, r=[dg, xi], w=[pb])
                        self.act(yt[:, j * 512: j * 512 + Lc], pb[:, 0:Lc], AF.Silu, bias=self.cvp[:, cb * 6 + 5: cb * 6 + 6],
                                 r=[pb, self.cvp], w=[yt])
                        if cb < XB + GB:
                            bk = tps[cnt["t"] % 4]
                            cnt["t"] += 1
                            bkb = bk[:].bitcast(BF16)
                            for tb in range(nt):
                                self.P.op("pe", lambda e, tb=tb, j=j, bkb=bkb, yt=yt: e.transpose(
                                    bkb[:, tb * 128:(tb + 1) * 128], yt[:, j * 512 + tb * 128: j * 512 + (tb + 1) * 128], self.ident_b),
                                    r=[yt, self.cb16], w=[bk], inc=(tb == nt - 1))
                            if cb < XB:
                                dst = xs_tm[:].rearrange("p (tb ch) -> p tb ch", ch=c.DI)[:, 0:nt, cb * 128:(cb + 1) * 128]
                                dtile = xs_tm
                            else:
                                dst = b_tm[:].rearrange("p (tb ch) -> p tb ch", ch=c.GN)[:, 0:nt, (cb - XB) * 128:(cb - XB + 1) * 128]
                                dtile = b_tm
                            self.cp("dve", dst, bkb[:, 0:nt * 128].rearrange("p (tb ch) -> p tb ch", ch=128), r=[bk], w=[dtile])
                    for j in range(ncb):
                        cb = cg + j
                        if cb >= XB:
                            dstT = self.BT if cb < XB + GB else self.CT
                            rb = (cb - XB) if cb < XB + GB else (cb - XB - GB)
                            self.store(dstT[rb * 128:(rb + 1) * 128, t0:t0 + Lc], yt[:, j * 512: j * 512 + Lc], r=[yt])
                self.store(self.xs[t0:t0 + Lc, :].rearrange("(tb p) ch -> p tb ch", p=128),
                           xs_tm[:].rearrange("p (tb ch) -> p tb ch", ch=c.DI)[:, 0:nt, :], r=[xs_tm])
                self.store(self.Btm[t0:t0 + Lc, :].rearrange("(tb p) ch -> p tb ch", p=128),
                           b_tm[:].rearrange("p (tb ch) -> p tb ch", ch=c.GN)[:, 0:nt, :], r=[b_tm])

    def phase_ssd(self, l, last):
        from contextlib import ExitStack
        c = self.c
        H, E, G, DI, GN = c.H, c.E, c.G, c.DI, c.GN
        EW = E * 64
        NB3 = 3
        with ExitStack() as st:
            dt_c = [self.sb(st, [128, H], F32, "dt_c%d" % i) for i in range(2)]
            dta = [self.sb(st, [128, H], F32, "dta_c%d" % i) for i in range(2)]
            ea = [self.sb(st, [128, H], F32, "ea%d" % i) for i in range(2)]
            eal = [self.sb(st, [128, H], F32, "eal%d" % i) for i in range(2)]
            xs_c = [self.sb(st, [128, DI], BF16, "xs_c%d" % i) for i in range(2)]
            xdt = [self.sb(st, [128, DI], BF16, "xdt%d" % i) for i in range(2)]
            b_c = [self.sb(st, [128, GN], BF16, "b_c%d" % i) for i in range(2)]
            bT_c = [self.sb(st, [128, GN], BF16, "bT_c%d" % i) for i in range(2)]
            cT_c = [self.sb(st, [128, GN], BF16, "cT_c%d" % i) for i in range(2)]
            y_c = [self.sb(st, [128, DI], F32, "y_c%d" % i) for i in range(2)]
            Lm = [self.sb(st, [128, E * 128], F32, "Lm%d" % i) for i in range(NB3)]
            expD = [self.sb(st, [128, E * 128], F32, "expD%d" % i) for i in range(NB3)]
            Gm = [self.sb(st, [128, 128], F32, "Gm%d" % i) for i in range(NB3)]
            MT = [self.sb(st, [128, E * 128], BF16, "MT%d" % i) for i in range(2)]
            ytmp = [self.sb(st, [128, EW], F32, "ytmp%d" % i) for i in range(2)]
            wend = [self.sb(st, [128, EW], BF16, "wend%d" % i) for i in range(2)]
            S_f = [self.sb(st, [128, EW], F32, "S_f%d" % g) for g in range(G)]
            S_b = [self.sb(st, [128, EW], BF16, "S_b%d" % g) for g in range(G)]
            bk_Gm = self.banks[0]
            Dsets = [(self.banks[1], self.banks[2]), (self.banks[3], self.banks[4])]
            bk_Y, bk_Yo, bk_cs = self.banks[5], self.banks[6], self.banks[7]

            def prologue(ch):
                k, d, t0 = ch["k"], ch["d"], ch["t0"]
                self.load(dt_c[k][:], self.dtv[t0:t0 + 128, d * H:(d + 1) * H], w=[dt_c[k]])
                self.load(xs_c[k][:], self.xs[t0:t0 + 128, :], w=[xs_c[k]])
                self.load(b_c[k][:], self.Btm[t0:t0 + 128, :], w=[b_c[k]])
                self.load(bT_c[k][:].rearrange("p (g t) -> p g t", t=128),
                          self.BT[:, t0:t0 + 128].rearrange("(g p) t -> p g t", p=128), w=[bT_c[k]])
                self.load(cT_c[k][:].rearrange("p (g t) -> p g t", t=128),
                          self.CT[:, t0:t0 + 128].rearrange("(g p) t -> p g t", p=128), w=[cT_c[k]])
                self.tt("dve", dta[k][:], dt_c[k][:], self.aneg[:, d * H:(d + 1) * H], ALU.mult,
                        r=[dt_c[k], self.aneg], w=[dta[k]])
                self.mm(bk_Gm[:, 128:128 + H], self.U_f[d], dta[k][:], True, True, r=[self.cf, dta[k]], w=[bk_Gm])
                self.mm(bk_Gm[:, 128 + H:128 + 2 * H], self.ones_f, dta[k][:], True, True, r=[self.cf, dta[k]], w=[bk_Gm])
                self.act(ea[k][:], bk_Gm[:, 128:128 + H], AF.Exp, r=[bk_Gm], w=[ea[k]])
                self.act(eal[k][:], bk_Gm[:, 128 + H:128 + 2 * H], AF.Exp, r=[bk_Gm], w=[eal[k]])
                self.tt("dve", xdt[k][:].rearrange("p (h q) -> p h q", q=64),
                        xs_c[k][:].rearrange("p (h q) -> p h q", q=64),
                        bc_ap(dt_c[k][:], [[1, H], [0, 64]]), ALU.mult, r=[xs_c[k], dt_c[k]], w=[xdt[k]])

            def S1(u):
                k, d, g, n3 = u["k"], u["d"], u["g"], u["n"] % NB3
                for e_ in range(E):
                    hh = g * E + e_
                    self.act(Lm[n3][:, e_ * 128:(e_ + 1) * 128], self.U_f[d], AF.Identity,
                             scale=dta[k][:, hh:hh + 1], r=[self.cf, dta[k]], w=[Lm[n3]])

            def S2(u):
                k, d, g, n3 = u["k"], u["d"], u["g"], u["n"] % NB3
                D0, D1 = Dsets[u["n"] % 2]
                self.mm(D0[:, 0:512], self.A_f[d], Lm[n3][:, 0:512], True, True, r=[Lm[n3], self.cf], w=[D0])
                self.mm(D1[:, 0:512], self.A_f[d], Lm[n3][:, 512:1024], True, True, r=[Lm[n3], self.cf], w=[D1])
                if u["want_y"]:
                    bTg = bT_c[k][:, g * 128:(g + 1) * 128]
                    cTg = cT_c[k][:, g * 128:(g + 1) * 128]
                    self.mm(bk_Gm[:, 0:128], bTg, cTg, True, True, r=[bT_c[k], cT_c[k]], w=[bk_Gm])
                self.act(expD[n3][:, 0:512], D0[:, 0:512], AF.Exp, r=[D0], w=[expD[n3]])
                self.act(expD[n3][:, 512:1024], D1[:, 0:512], AF.Exp, r=[D1], w=[expD[n3]])
                if u["want_y"]:
                    self.tt("dve", Gm[n3][:], bk_Gm[:, 0:128], self.U_f[d], ALU.mult, r=[bk_Gm, self.cf], w=[Gm[n3]])

            def S3(u):
                k, d, g, n3, n2 = u["k"], u["d"], u["g"], u["n"] % NB3, u["n"] % 2
                icol = 127 if d == 0 else 0
                want_y = u["want_y"]
                cTg = cT_c[k][:, g * 128:(g + 1) * 128]
                self.tt("dve", wend[n2][:].rearrange("p (e q) -> p e q", q=64),
                        xdt[k][:, g * EW:(g + 1) * EW].rearrange("p (e q) -> p e q", q=64),
                        bc_ap(expD[n3][:, icol:icol + 1], [[128, E], [0, 64]]), ALU.mult,
                        r=[xdt[k], expD[n3]], w=[wend[n2]])
                if want_y:
                    self.tt("dve", MT[n2][:].rearrange("p (e i) -> p e i", i=128),
                            expD[n3][:].rearrange("p (e i) -> p e i", i=128),
                            bc_ap(Gm[n3][:], [[0, E], [1, 128]]), ALU.mult, r=[expD[n3], Gm[n3]], w=[MT[n2]])
                self.mm(bk_cs[:, 0:EW], b_c[k][:, g * 128:(g + 1) * 128], wend[n2][:], True, True,
                        r=[b_c[k], wend[n2]], w=[bk_cs])
                if want_y:
                    self.mm(bk_Yo[:, 0:EW], cTg, S_b[g][:], True, True, r=[cT_c[k], S_b[g]], w=[bk_Yo])
                    for e_ in range(E):
                        hh = g * E + e_
                        self.mm(bk_Y[:, e_ * 64:(e_ + 1) * 64], MT[n2][:, e_ * 128:(e_ + 1) * 128],
                                xdt[k][:, hh * 64:(hh + 1) * 64], True, True, r=[MT[n2], xdt[k]], w=[bk_Y],
                                inc=(e_ == E - 1))
                Sg = S_f[g][:]
                self.tt("dve", Sg.rearrange("p (e q) -> p e q", q=64), Sg.rearrange("p (e q) -> p e q", q=64),
                        bc_ap(eal[k][:, g * E:(g + 1) * E], [[1, E], [0, 64]]), ALU.mult,
                        r=[S_f[g], eal[k]], w=[S_f[g]])
                self.tt("dve", Sg, Sg, bk_cs[:, 0:EW], ALU.add, r=[S_f[g], bk_cs], w=[S_f[g]])
                if want_y:
                    self.tt("dve", ytmp[n2][:].rearrange("p (e q) -> p e q", q=64),
                            bk_Yo[:, 0:EW].rearrange("p (e q) -> p e q", q=64),
                            bc_ap(ea[k][:, g * E:(g + 1) * E], [[1, E], [0, 64]]), ALU.mult,
                            r=[bk_Yo, ea[k]], w=[ytmp[n2]])
                    self.tt("dve", y_c[k][:, g * EW:(g + 1) * EW], ytmp[n2][:], bk_Y[:, 0:EW], ALU.add,
                            r=[ytmp[n2], bk_Y], w=[y_c[k]])
                    if g == G - 1:
                        self.store(self.yd[d][u["t0"]:u["t0"] + 128, :], y_c[k][:], r=[y_c[k]])

            def S4(u):
                g = u["g"]
                self.cp("act", S_b[g][:], S_f[g][:], r=[S_f[g]], w=[S_b[g]])

            ci = 0
            for b in range(c.BPC):
                for d in range(2):
                    for g in range(G):
                        self.memset("dve", S_f[g][:], 0.0, w=[S_f[g]])
                        self.memset("dve", S_b[g][:], 0.0, w=[S_b[g]])
                    seq = [(c.ctx_off(b) + i * 128, True) for i in range(c.L // 128)]
                    lat = [(c.lat_off(b) + i * 128, False) for i in range(c.S // 128)]
                    if d == 1:
                        seq = seq[::-1]
                        lat = lat[::-1]
                    units = []
                    chunks = []
                    for (t0, is_ctx) in seq + lat:
                        ch = {"k": ci % 2, "d": d, "t0": t0}
                        ci += 1
                        chunks.append(ch)
                        for g in range(G):
                            units.append({"k": ch["k"], "d": d, "g": g, "t0": t0, "n": len(units),
                                          "want_y": not (last and is_ctx), "ch": ch if g == 0 else None})
                    NU = len(units)
                    for t in range(-2, NU + 1):
                        if 0 <= t + 2 < NU:
                            u = units[t + 2]
                            if u["ch"] is not None:
                                prologue(u["ch"])
                            S1(u)
                        if 0 <= t + 1 < NU:
                            S2(units[t + 1])
                        if 0 <= t < NU:
                            S3(units[t])
                        if 0 <= t - 1 < NU:
                            S4(units[t - 1])

    def phase4(self, l, last):
        from contextlib import ExitStack
        c = self.c
        DI, H = c.DI, c.H
        NBK = DI // 128
        with ExitStack() as st:
            dsum_bc = self.sb(st, [128, DI], F32, "dsum_bc")
            gssd_bc = self.sb(st, [128, DI], F32, "gssd_bc")
            self.cp("dve", dsum_bc[:].rearrange("p (h q) -> p h q", q=64), bc_ap(self.dsum[:], [[1, H], [0, 64]]),
                    r=[self.dsum], w=[dsum_bc])
            self.load(gssd_bc[:], bass.AP(self.rows, self.gssd_off, [[0, 128], [1, DI]]), w=[gssd_bc])
            yf = [self.sb(st, [128, DI], F32, "p4yf%d" % i) for i in range(2)]
            yb = [self.sb(st, [128, DI], F32, "p4yb%d" % i) for i in range(2)]
            xs_ = [self.sb(st, [128, DI], BF16, "p4xs%d" % i) for i in range(2)]
            sz_ = [self.sb(st, [128, DI], BF16, "p4sz%d" % i) for i in range(2)]
            ob = [self.sb(st, [128, DI], BF16, "p4o%d" % i) for i in range(1)] * 2
            ssq = [self.sb(st, [128, 1], F32, "p4ssq%d" % i) for i in range(2)]
            acc = [self.sb(st, [128, NBK * 512], BF16, "p4acc%d" % i) for i in range(1)] * 2
            tps = self.banks[0:8]
            tcount = 0
            ai = 0
            tbs = []
            for (t0, kind, b, s0) in self.tiles512():
                if last and kind == "ctx":
                    continue
                tbs.append(t0)
            for ti, t0 in enumerate(tbs):
                ac = acc[ai % 2]
                ai += 1
                for s in range(4):
                    k = (ti * 4 + s) % 2
                    r0 = t0 + s * 128
                    self.load(yf[k][:], self.yd[0][r0:r0 + 128, :], w=[yf[k]])
                    self.load(yb[k][:], self.yd[1][r0:r0 + 128, :], w=[yb[k]])
                    self.load(xs_[k][:], self.xs[r0:r0 + 128, :], w=[xs_[k]])
                    self.load(sz_[k][:], self.sz[r0:r0 + 128, :], w=[sz_[k]])
                    self.tt("dve", yf[k][:], yf[k][:], yb[k][:], ALU.add, r=[yf[k], yb[k]], w=[yf[k]])
                    self.tt("pool", yb[k][:], xs_[k][:], dsum_bc[:], ALU.mult, r=[xs_[k], dsum_bc, yf[k]], w=[yb[k]])
                    self.tt("dve", yf[k][:], yf[k][:], yb[k][:], ALU.add, r=[yf[k], yb[k]], w=[yf[k]])
                    self.tt("pool", yf[k][:], yf[k][:], sz_[k][:], ALU.mult, r=[yf[k], sz_[k]], w=[yf[k]])
                    self.memset("dve", ssq[k][:], 0.0, w=[ssq[k]])
                    self.act(yb[k][:], yf[k][:], AF.Square, accum=ssq[k][:], r=[yf[k], ssq[k]], w=[yb[k], ssq[k]])
                    self.rsqrt(ssq[k][:], ssq[k][:], 1.0 / DI, [ssq[k]], ssq[k])
                    self.stt("dve", ob[k][:], yf[k][:], ssq[k][:, 0:1], gssd_bc[:], ALU.mult, ALU.mult,
                             r=[yf[k], ssq[k], gssd_bc], w=[ob[k]])
                    for q0 in range(0, NBK, 8):
                        bk = tps[tcount % 8]
                        tcount += 1
                        bkb = bk[:].bitcast(BF16)
                        nq = min(8, NBK - q0)
                        for q in range(nq):
                            self.P.op("pe", lambda e, q=q, q0=q0, bkb=bkb, k=k: e.transpose(
                                bkb[:, q * 128:(q + 1) * 128], ob[k][:, (q0 + q) * 128:(q0 + q + 1) * 128], self.ident_b),
                                r=[ob[k], self.cb16], w=[bk], inc=(q == nq - 1))
                        dst = ac[:].rearrange("p (blk t) -> p blk t", t=512)[:, q0:q0 + nq, s * 128:(s + 1) * 128]
                        self.cp("act", dst, bkb[:, 0:nq * 128].rearrange("p (blk t) -> p blk t", t=128), r=[bk], w=[ac])
                self.store(self.ssdT[:, t0:t0 + 512].rearrange("(blk p) t -> p blk t", p=128),
                           ac[:].rearrange("p (blk t) -> p blk t", t=512), r=[ac])

    def phase5a(self, l, last):
        from contextlib import ExitStack
        c = self.c
        N = 512
        AC = c.AW // 128
        SC = c.DI // 128
        DC = c.DC
        self._dense_rr = 0
        self._ws_rr = 0
        with ExitStack() as st:
            at = self.sb(st, [128, AC * N], BF16, "p5at")
            sd = self.sb(st, [128, SC * N], BF16, "p5sd")
            mT = self.sb(st, [128, DC * N], BF16, "p5m")
            wslots = [self.sb(st, [128, 16 * 512], BF16, "p5ws%d" % i) for i in range(3)]
            gA = [self.sb(st, [128, 4 * N], BF16, "p5gA%d" % i) for i in range(2)]
            gB = [self.sb(st, [128, 4 * N], BF16, "p5gB%d" % i) for i in range(2)]
            tA = [self.sb(st, [128, 4 * N], F32, "p5tA%d" % i) for i in range(2)]
            tB = [self.sb(st, [128, N], F32, "p5tB%d" % i) for i in range(2)]
            xb = [self.sb(st, [128, 4 * N], F32, "p5xb%d" % i) for i in range(2)]
            banksA = self.banks[0:4]
            banksB = self.banks[4:8]
            cnt = {"g": 0, "t": 0, "x": 0}
            for (t0, kind, b, s0) in self.tiles512():
                is_ctx = kind == "ctx"
                if last and is_ctx:
                    continue
                m = c.BPC if is_ctx else b
                self.load(at[:].rearrange("p (k n) -> p k n", n=N),
                          self.attT.ap().rearrange("(k p) t -> p k t", p=128)[:, :, t0:t0 + N], w=[at])
                self.load(sd[:].rearrange("p (k n) -> p k n", n=N),
                          self.ssdT.ap().rearrange("(k p) t -> p k t", p=128)[:, :, t0:t0 + N], w=[sd])
                for cg in range(0, DC, 4):
                    ncb = min(4, DC - cg)
                    gi = cnt["g"] % 2
                    cnt["g"] += 1
                    self.load(gA[gi][:, 0:ncb * N].rearrange("p (j n) -> p j n", n=N),
                              self.gT[cg * 128:(cg + ncb) * 128, t0:t0 + N].rearrange("(j p) t -> p j t", p=128), w=[gA[gi]])
                    self.load(gB[gi][:, 0:ncb * N].rearrange("p (j n) -> p j n", n=N),
                              self.gT[c.D + cg * 128: c.D + (cg + ncb) * 128, t0:t0 + N].rearrange("(j p) t -> p j t", p=128), w=[gB[gi]])

                    def epiA(cbi, bk, gi=gi):
                        self.tt("dve", tA[gi][:, cbi * N:(cbi + 1) * N], bk[:, 0:N], gA[gi][:, cbi * N:(cbi + 1) * N], ALU.mult,
                                r=[bk, gA[gi]], w=[tA[gi]])

                    def epiB(cbi, bk, gi=gi, cg=cg):
                        i = cnt["t"] % 2
                        cnt["t"] += 1
                        self.tt("dve", tB[i][:], bk[:, 0:N], gB[gi][:, cbi * N:(cbi + 1) * N], ALU.mult, r=[bk, gB[gi]], w=[tB[i]])
                        self.tt("dve", mT[:, (cg + cbi) * N:(cg + cbi + 1) * N], tB[i][:], tA[gi][:, cbi * N:(cbi + 1) * N], ALU.add,
                                r=[tB[i], tA[gi]], w=[mT])
                    self._dense_rr = 0
                    self.dense(self.wbs[l % 2]["oa"], c.AW, cg * 128, ncb * 128, at, lambda kc: at[:, kc * N:(kc + 1) * N], N, banksA, epiA, wslots)
                    self._dense_rr = 0
                    self.dense(self.wbs[l % 2]["os"], c.DI, cg * 128, ncb * 128, sd, lambda kc: sd[:, kc * N:(kc + 1) * N], N, banksB, epiB, wslots)
                for cg in range(0, DC, 4):
                    ncb = min(4, DC - cg)
                    xi = cnt["x"] % 2
                    cnt["x"] += 1
                    xv = self.xT[cg * 128:(cg + ncb) * 128, t0:t0 + N].rearrange("(j p) t -> p j t", p=128)
                    self.load(xb[xi][:, 0:ncb * N].rearrange("p (j n) -> p j n", n=N), xv, w=[xb[xi]])

                    def epiO(cbi, bk, xi=xi, cg=cg):
                        self.stt("dve", xb[xi][:, cbi * N:(cbi + 1) * N], bk[:, 0:N], self.mod_ap(l, 2, cg + cbi, m),
                                 xb[xi][:, cbi * N:(cbi + 1) * N], ALU.mult, ALU.add, r=[bk, self.mod, xb[xi]], w=[xb[xi]])
                    self.dense(self.wbs[l % 2]["out"], c.D, cg * 128, ncb * 128, mT, lambda kc: mT[:, kc * N:(kc + 1) * N], N, self.banks[0:8], epiO, wslots)
                    self.store(xv, xb[xi][:, 0:ncb * N].rearrange("p (j n) -> p j n", n=N), r=[xb[xi]])

    def phase5b(self, l, last):
        from contextlib import ExitStack
        c = self.c
        N = 512
        DC = c.DC
        FC = c.DFF // 128
        self._dense_rr = 0
        self._ws_rr = 0
        with ExitStack() as st:
            self.alloc_norm_tmps(st)
            xt = self.sb(st, [128, DC * N], F32, "p6x")
            h2 = self.sb(st, [128, DC * N], BF16, "p6h")
            f1 = self.sb(st, [128, FC * N], BF16, "p6f")
            wslots = [self.sb(st, [128, 16 * 512], BF16, "p6ws%d" % i) for i in range(3)]
            rl = [self.sb(st, [128, N], F32, "p6r%d" % i) for i in range(2)]
            cnt = {"r": 0}
            dbanks = self.banks[0:6]
            for (t0, kind, b, s0) in self.tiles512():
                is_ctx = kind == "ctx"
                if last and is_ctx:
                    continue
                m = c.BPC if is_ctx else b
                xsrc = self.xT.ap().rearrange("(dc p) t -> p dc t", p=128)[:, :, t0:t0 + N]
                self.load(xt[:].rearrange("p (dc n) -> p dc n", n=N), xsrc, w=[xt])
                self.norm_tile(st, xt, l, self.s2, lambda dc: self.mod_ap(l, 3, dc, m), m, h2)

                def epi1(cbi, bk):
                    i = cnt["r"] % 2
                    cnt["r"] += 1
                    self.act(rl[i][:], bk[:, 0:N], AF.Relu, r=[bk], w=[rl[i]])
                    self.act(f1[:, cbi * N:(cbi + 1) * N], rl[i][:], AF.Square, r=[rl[i]], w=[f1])
                self.dense(self.wbs[l % 2]["ff1"], c.D, 0, c.DFF, h2, lambda kc: h2[:, kc * N:(kc + 1) * N], N, dbanks, epi1, wslots)

                def epi2(cbi, bk):
                    self.stt("dve", xt[:, cbi * N:(cbi + 1) * N], bk[:, 0:N], self.mod_ap(l, 5, cbi, m),
                             xt[:, cbi * N:(cbi + 1) * N], ALU.mult, ALU.add, r=[bk, self.mod, xt], w=[xt])
                self.dense(self.wbs[l % 2]["ff2"], c.DFF, 0, c.D, f1, lambda kc: f1[:, kc * N:(kc + 1) * N], N, dbanks, epi2, wslots)
                if not last:
                    self.store(xsrc, xt[:].rearrange("p (dc n) -> p dc n", n=N), r=[xt])
                else:
                    gf = self.sb(st, [128, DC * c.NM], F32, "gfin%d" % t0)
                    g = self.gnt[:, 2 * c.DEPTH * DC:(2 * c.DEPTH + 1) * DC]
                    self.cp("dve", gf[:].rearrange("p (a b) -> p a b", b=c.NM), bc_ap(g, [[1, DC], [0, c.NM]]), r=[self.gnt], w=[gf])
                    of = self.sb(st, [128, DC * N], F32, "ofin%d" % t0) if False else f1
                    ofv = f1[:, 0:2 * DC * N].bitcast(F32)
                    DCn = DC
                    bk = self.banks[6]
                    for dc in range(DCn):
                        sq = self.sqs[dc % 2]
                        self.act(sq[:], xt[:, dc * N:(dc + 1) * N], AF.Square, r=[xt], w=[sq])
                        self.mm(bk[:, 0:N], self.ones_f, sq[:], dc == 0, dc == DCn - 1, r=[sq, self.cf], w=[bk], inc=True)
                    rstd = self.rstd
                    self.rsqrt(rstd[:], bk[:, 0:N], 1.0 / c.D, [bk], rstd)
                    for dc in range(DCn):
                        self.stt("dve", ofv[:, dc * N:(dc + 1) * N], xt[:, dc * N:(dc + 1) * N], g[:, dc:dc + 1], rstd[:],
                                 ALU.mult, ALU.mult, r=[xt, self.gnt, rstd, f1], w=[f1])
                    lo = b * c.S + s0
                    self.store(self.outT.ap().rearrange("(dc p) t -> p dc t", p=128)[:, :, lo:lo + N],
                               ofv[:, 0:DC * N].rearrange("p (dc n) -> p dc n", n=N), r=[f1])


def host_consts():
    r = np.arange(128)[:, None]
    cc = np.arange(128)[None, :]
    ident = (r == cc).astype(np.float32)
    U0 = (r <= cc).astype(np.float32)
    U1 = (r >= cc).astype(np.float32)
    SL = (r > cc).astype(np.float32)
    SU = (r < cc).astype(np.float32)
    ones = np.ones((128, 128), np.float32)
    R = np.zeros((128, 128), np.float32)
    for m in range(128):
        if m % 64 < 32:
            R[m + 32, m] = -1.0
        else:
            R[m - 32, m] = 1.0
    return np.concatenate([ident, U0, U1, SL, SU, ones, R], axis=1)


def host_rope(S, grid_w=64, theta=10000.0):
    t = np.arange(S)
    row = (t // grid_w).astype(np.float32)
    col = (t % grid_w).astype(np.float32)
    axis_dim = 64
    inv = (np.float32(theta) ** (-np.arange(0, axis_dim, 2, dtype=np.float32) / np.float32(axis_dim))).astype(np.float32)
    ang_r = (row[:, None] * inv[None]).astype(np.float32)
    ang_c = (col[:, None] * inv[None]).astype(np.float32)
    cos = np.zeros((128, S), np.float32)
    sin = np.zeros((128, S), np.float32)
    for d in range(128):
        a = ang_r if d < 64 else ang_c
        cos[d] = np.cos(a[:, d % 32])
        sin[d] = np.sin(a[:, d % 32])
    return np.concatenate([cos, sin], axis=1)


def fm(v, nchunk):
    v = np.asarray(v, np.float32)
    lead = v.shape[:-1]
    v = v.reshape(lead + (nchunk, 128))
    return np.moveaxis(v, -1, 0)


def prep_inputs(cfg, inp):
    c = cfg
    f32 = lambda a: np.ascontiguousarray(np.asarray(a, np.float32))
    x = f32(inp["x"])
    ctx = f32(inp["ctx"])
    cc = f32(inp["c"])
    c_ctx = f32(inp["c_ctx"])
    shared = {
        "w_ada": f32(inp["w_ada"]).reshape(c.DEPTH * c.D, 6 * c.D),
        "w_in": f32(inp["w_in"]).reshape(c.DEPTH * c.D, c.IN),
        "w_oa": f32(inp["w_o_attn"]).reshape(c.DEPTH * c.AW, c.D),
        "w_os": f32(inp["w_o_ssd"]).reshape(c.DEPTH * c.DI, c.D),
        "w_out": f32(inp["w_out"]).reshape(c.DEPTH * c.D, c.D),
        "w_ff1": f32(inp["w_ff1"]).reshape(c.DEPTH * c.D, c.DFF),
        "w_ff2": f32(inp["w_ff2"]).reshape(c.DEPTH * c.DFF, c.D),
    }
    shared["b_ada"] = f32(fm(inp["b_ada"], 6 * c.DC).reshape(128, -1))
    gn = np.concatenate([f32(inp["g_norm1"]), f32(inp["g_norm2"]), f32(inp["g_final"])[None]], axis=0)
    shared["gn"] = f32(fm(gn, c.DC).reshape(128, -1))
    cw = f32(inp["conv_w"])
    cb = f32(inp["conv_b"])
    cv = np.concatenate([cw, cb[:, None, :]], axis=1)
    cv = cv.reshape(c.DEPTH, 6, c.CB, 128)
    shared["convp"] = f32(np.transpose(cv, (3, 0, 2, 1)).reshape(128, -1))
    rows = np.concatenate([f32(inp["attn_sink"]), f32(inp["dt_bias"]).reshape(c.DEPTH, -1),
                           f32(inp["a_log"]).reshape(c.DEPTH, -1), f32(inp["d_skip"]).reshape(c.DEPTH, -1),
                           f32(inp["g_ssd"])], axis=1)
    shared["rows"] = f32(rows)
    shared["consts"] = host_consts()
    shared["rope"] = host_rope(c.S, c.GRID_W)
    in_maps = []
    for core in range(c.NCORES):
        bs = [core * c.BPC + i for i in range(c.BPC)]
        xin = np.concatenate([ctx[b].T for b in bs] + [x[b].T for b in bs], axis=1)
        cvec = np.stack([cc[b] for b in bs] + [c_ctx], axis=0)
        cin = np.transpose(cvec.reshape(c.NM, c.DC, 128), (2, 1, 0)).reshape(128, -1)
        m = dict(shared)
        m["xin"] = f32(xin)
        m["cin"] = f32(cin)
        in_maps.append(m)
    return in_maps


def build_nc(cfg):
    from contextlib import ExitStack
    nc = bass.Bass("TRN2", target_bir_lowering=False)
    with ExitStack() as stack:
        b = Builder(cfg, nc, stack)
        b.build()
    return nc


LAST_EXEC_NS = [None]


def run_cfg(cfg, inp, trace=False):
    in_maps = prep_inputs(cfg, inp)
    nc = build_nc(cfg)
    if trace:
        res = run_bass_kernel_spmd(nc, in_maps, core_ids=list(range(cfg.NCORES)), trace=True)
        LAST_EXEC_NS[0] = res.exec_time_ns
    else:
        res = run_bass_kernel_spmd(nc, in_maps, core_ids=list(range(cfg.NCORES)))
    outs = []
    for core in range(cfg.NCORES):
        oT = res.results[core]["outT"]
        o = oT.T.reshape(cfg.BPC, cfg.S, cfg.D)
        outs.append(o)
    return np.ascontiguousarray(np.concatenate(outs, axis=0).astype(np.float32))


def kernel(**inputs):
    cfg = Cfg()
    return run_cfg(cfg, inputs)
```
